# Optimizing a Trainium2 kernel written in Bass

```python
import math
import jax, jax.numpy as jnp
from jax import lax
import numpy as np

D_MODEL = 4096
BATCH = 1
SEQ = 16384
DEPTH = 2

GRID_W = 64
CTX_LEN = 256
HEAD_DIM = 128
NA_HEADS = 16
NA_WIDTH = NA_HEADS * HEAD_DIM
NA_KR = 8
NA_KW = 16
DIFF_HEADS = 8
DIFF_QK_WIDTH = DIFF_HEADS * 2 * HEAD_DIM
DIFF_WIDTH = DIFF_HEADS * 2 * HEAD_DIM
DIFF_LN_EPS = 1e-5
CONV_WIDTH = 2048
CONV_K = 31
ROPE_BASE = 10000.0
Q_BLOCK = 128
LN_EPS = 1e-6

SEGMENTS = (
    ('na_q', NA_WIDTH), ('na_k', NA_WIDTH), ('na_v', NA_WIDTH), ('na_gate', NA_WIDTH),
    ('df_q', DIFF_QK_WIDTH), ('df_k', DIFF_QK_WIDTH), ('df_v', DIFF_WIDTH), ('df_gate', DIFF_WIDTH),
    ('cv_val', CONV_WIDTH), ('cv_glu', CONV_WIDTH), ('cv_gate', CONV_WIDTH),
    ('merge_na', D_MODEL), ('merge_df', D_MODEL), ('merge_cv', D_MODEL),
)
N_IN = 4 * NA_WIDTH + 2 * DIFF_QK_WIDTH + 2 * DIFF_WIDTH + 3 * CONV_WIDTH + 3 * D_MODEL
CTX_KV_SEGMENTS = ('na_k', 'na_v', 'df_k', 'df_v')

kernel_name = 'hybrid_natten_diffattn_conformer_prefix_block'


def _segment_table():
    table, off = {}, 0
    for name, size in SEGMENTS:
        table[name] = (off, size)
        off += size
    return table


def _layernorm(x, g=None, b=None, eps=LN_EPS):
    xf = x.astype(jnp.float32)
    mu = jnp.mean(xf, -1, keepdims=True)
    var = jnp.mean(jnp.square(xf - mu), -1, keepdims=True)
    y = (xf - mu) * lax.rsqrt(var + eps)
    if g is not None:
        y = y * g.astype(jnp.float32) + b.astype(jnp.float32)
    return y.astype(x.dtype)


def _rmsnorm(x, g, eps):
    xf = x.astype(jnp.float32)
    y = xf * lax.rsqrt(jnp.mean(xf * xf, -1, keepdims=True) + eps)
    return (y * g.astype(jnp.float32)).astype(x.dtype)


def _project_all(h, w, b):
    z = h @ w + b
    return {n: z[..., o:o + s] for n, (o, s) in _segment_table().items()}


def _project_some(h, w, b, names):
    t = _segment_table()
    return {n: h @ w[:, t[n][0]:t[n][0] + t[n][1]] + b[t[n][0]:t[n][0] + t[n][1]] for n in names}


def _heads(t, n_heads):
    return t.reshape(t.shape[0], t.shape[1], n_heads, -1)


def _axial_rope_tables(n_tokens):
    t = jnp.arange(n_tokens, dtype=jnp.int32)
    row = (t // GRID_W).astype(jnp.float32)
    col = (t % GRID_W).astype(jnp.float32)
    n_pairs_axis = HEAD_DIM // 4
    inv_freq = ROPE_BASE ** (-jnp.arange(n_pairs_axis, dtype=jnp.float32) / n_pairs_axis)
    ang = jnp.concatenate([row[:, None] * inv_freq, col[:, None] * inv_freq], -1)
    return jnp.cos(ang), jnp.sin(ang)


def _apply_rope(x, cos, sin):
    xf = x.astype(jnp.float32).reshape(x.shape[:-1] + (HEAD_DIM // 2, 2))
    x1, x2 = xf[..., 0], xf[..., 1]
    cc, ss = cos[None, :, None, :], sin[None, :, None, :]
    out = jnp.stack([x1 * cc - x2 * ss, x1 * ss + x2 * cc], -1).reshape(x.shape)
    return out.astype(x.dtype)


def _dense_attention(q, k, v):
    b, lq, h, dh = q.shape
    s = jnp.einsum('bqhd,bkhd->bhqk', q, k).astype(jnp.float32) * (dh ** -0.5)
    p = jax.nn.softmax(s, -1).astype(v.dtype)
    return jnp.einsum('bhqk,bkhd->bqhd', p, v).reshape(b, lq, -1)


def _neighbourhood_attention(q, k, v, k_ctx, v_ctx, rpb):
    b, l, h, dh = q.shape
    rows = l // GRID_W
    kr = min(NA_KR, rows)
    scale = dh ** -0.5
    qg = q.reshape(b, rows, GRID_W, h, dh)
    kg = k.reshape(b, rows, GRID_W, h, dh)
    vg = v.reshape(b, rows, GRID_W, h, dh)
    cols = jnp.arange(GRID_W)
    col_start = jnp.clip(cols - NA_KW // 2, 0, GRID_W - NA_KW)
    col_idx = col_start[:, None] + jnp.arange(NA_KW)[None, :]
    col_off = col_idx - cols[:, None] + (NA_KW - 1)
    rpb_cols = rpb.astype(jnp.float32)[:, :, col_off]

    def one_row(r):
        r0 = jnp.clip(r - kr // 2, 0, rows - kr)
        q_r = lax.dynamic_index_in_dim(qg, r, axis=1, keepdims=False)
        k_win = lax.dynamic_slice_in_dim(kg, r0, kr, axis=1)[:, :, col_idx]
        v_win = lax.dynamic_slice_in_dim(vg, r0, kr, axis=1)[:, :, col_idx]
        row_off = r0 + jnp.arange(kr) - r + (NA_KR - 1)
        bias = jnp.transpose(rpb_cols[:, row_off], (0, 2, 1, 3))
        s_loc = jnp.einsum('bjhd,brjwhd->bhjrw', q_r, k_win).astype(jnp.float32) * scale + bias[None]
        s_ctx = jnp.einsum('bjhd,bchd->bhjc', q_r, k_ctx).astype(jnp.float32) * scale
        s = jnp.concatenate([s_loc.reshape(b, h, GRID_W, kr * NA_KW), s_ctx], -1)
        p = jax.nn.softmax(s, -1).astype(v.dtype)
        p_loc = p[..., :kr * NA_KW].reshape(b, h, GRID_W, kr, NA_KW)
        p_ctx = p[..., kr * NA_KW:]
        return (jnp.einsum('bhjrw,brjwhd->bjhd', p_loc, v_win)
                + jnp.einsum('bhjc,bchd->bjhd', p_ctx, v_ctx))

    out = lax.map(one_row, jnp.arange(rows))
    return jnp.transpose(out, (1, 0, 2, 3, 4)).reshape(b, l, h * dh)


def _diff_attention(q, k, v, lam, lam_init, subln_g):
    b, lq, h, _, dh = q.shape
    nb = lq // Q_BLOCK
    scale = dh ** -0.5
    qb = jnp.transpose(q.reshape(b, nb, Q_BLOCK, h, 2, dh), (1, 0, 2, 3, 4, 5))

    def one_block(q_blk):
        s = jnp.einsum('bqhmd,bkhmd->bhmqk', q_blk, k).astype(jnp.float32) * scale
        p = jax.nn.softmax(s, -1)
        w = (p[:, :, 0] - lam * p[:, :, 1]).astype(v.dtype)
        o = jnp.einsum('bhqk,bkhe->bqhe', w, v)
        return _rmsnorm(o, subln_g, DIFF_LN_EPS) * (1.0 - lam_init)

    out = lax.map(one_block, qb)
    return jnp.transpose(out, (1, 0, 2, 3, 4)).reshape(b, lq, -1)


def _conformer_conv(val, glu, w, bias, g, beta):
    u = val * jax.nn.sigmoid(glu)
    ch = u.shape[-1]
    y = lax.conv_general_dilated(u, w[:, None, :].astype(u.dtype), window_strides=(1,),
                                 padding=[(CONV_K // 2, CONV_K // 2)],
                                 dimension_numbers=('NWC', 'WIO', 'NWC'),
                                 feature_group_count=ch) + bias
    return jax.nn.silu(_layernorm(y, g, beta, eps=1e-5))


def _merge(p, o_na, o_df, o_cv, w_na, w_df, w_cv, w_o):
    y_na = (o_na * jax.nn.silu(p['na_gate'])) @ w_na
    y_df = (o_df * jax.nn.silu(p['df_gate'])) @ w_df
    y_cv = (o_cv * jax.nn.silu(p['cv_gate'])) @ w_cv
    y = (jax.nn.sigmoid(p['merge_na']) * y_na + jax.nn.sigmoid(p['merge_df']) * y_df
         + jax.nn.sigmoid(p['merge_cv']) * y_cv)
    return y @ w_o


def setup_inputs(seed: int = 0) -> dict:
    key = jax.random.key(seed)
    ks = jax.random.split(key, 26)
    d = D_MODEL
    beta = (8.0 * DEPTH) ** -0.25

    def nrm(k, shape, s):
        return jax.random.normal(k, shape, jnp.float32) * s

    t = _segment_table()
    col_scale = np.ones((N_IN,), np.float32)
    for n in ('na_v', 'df_v'):
        o, s = t[n]
        col_scale[o:o + s] = beta
    return {
        'x': nrm(ks[0], (BATCH, SEQ, d), 1.0),
        'c': nrm(ks[1], (BATCH, d), 1.0),
        'ctx': nrm(ks[2], (BATCH, CTX_LEN, d), 1.0),
        'c_ctx': nrm(ks[3], (d,), 1.0),
        'w_ada': nrm(ks[4], (DEPTH, d, 3 * d), 0.5 * d ** -0.5),
        'b_ada': nrm(ks[5], (DEPTH, 3 * d), 0.02),
        'w_in': nrm(ks[6], (DEPTH, d, N_IN), d ** -0.5) * jnp.asarray(col_scale),
        'b_in': nrm(ks[7], (DEPTH, N_IN), 0.02),
        'na_rpb': nrm(ks[8], (DEPTH, NA_HEADS, 2 * NA_KR - 1, 2 * NA_KW - 1), 0.1),
        'diff_lq1': nrm(ks[9], (DEPTH, HEAD_DIM), 0.1),
        'diff_lk1': nrm(ks[10], (DEPTH, HEAD_DIM), 0.1),
        'diff_lq2': nrm(ks[11], (DEPTH, HEAD_DIM), 0.1),
        'diff_lk2': nrm(ks[12], (DEPTH, HEAD_DIM), 0.1),
        'diff_subln_g': 1.0 + nrm(ks[13], (DEPTH, 2 * HEAD_DIM), 0.02),
        'conv_w': nrm(ks[14], (DEPTH, CONV_K, CONV_WIDTH), CONV_K ** -0.5),
        'conv_b': nrm(ks[15], (DEPTH, CONV_WIDTH), 0.02),
        'conv_ln_g': 1.0 + nrm(ks[16], (DEPTH, CONV_WIDTH), 0.02),
        'conv_ln_b': nrm(ks[17], (DEPTH, CONV_WIDTH), 0.02),
        'w_proj_na': nrm(ks[18], (DEPTH, NA_WIDTH, d), beta * NA_WIDTH ** -0.5),
        'w_proj_diff': nrm(ks[19], (DEPTH, DIFF_WIDTH, d), beta * DIFF_WIDTH ** -0.5),
        'w_proj_conv': nrm(ks[20], (DEPTH, CONV_WIDTH, d), beta * CONV_WIDTH ** -0.5),
        'w_out': nrm(ks[21], (DEPTH, d, d), beta * d ** -0.5),
        'post_ln_g': 1.0 + nrm(ks[22], (DEPTH, d), 0.02),
        'post_ln_b': nrm(ks[23], (DEPTH, d), 0.02),
    }


def reference(x, c, ctx, c_ctx, w_ada, b_ada, w_in, b_in, na_rpb, diff_lq1, diff_lk1, diff_lq2,
              diff_lk2, diff_subln_g, conv_w, conv_b, conv_ln_g, conv_ln_b, w_proj_na, w_proj_diff,
              w_proj_conv, w_out, post_ln_g, post_ln_b):
    b, l, _ = x.shape
    n_ctx = ctx.shape[1]
    alpha = (2.0 * DEPTH) ** 0.25
    cos, sin = _axial_rope_tables(l)
    for i in range(DEPTH):
        last = i == DEPTH - 1
        shift, scale, gate = jnp.split(jax.nn.silu(c) @ w_ada[i] + b_ada[i], 3, axis=-1)
        shift_c, scale_c, gate_c = jnp.split(jax.nn.silu(c_ctx) @ w_ada[i] + b_ada[i], 3, axis=-1)
        h = _layernorm(x) * (1.0 + scale[:, None, :]) + shift[:, None, :]
        hc = _layernorm(ctx) * (1.0 + scale_c) + shift_c
        p = _project_all(h, w_in[i], b_in[i])
        pc = (_project_some(hc, w_in[i], b_in[i], CTX_KV_SEGMENTS) if last
              else _project_all(hc, w_in[i], b_in[i]))

        na_kc, na_vc = _heads(pc['na_k'], NA_HEADS), _heads(pc['na_v'], NA_HEADS)
        o_na = _neighbourhood_attention(_heads(p['na_q'], NA_HEADS), _heads(p['na_k'], NA_HEADS),
                                        _heads(p['na_v'], NA_HEADS), na_kc, na_vc, na_rpb[i])

        df_q = _apply_rope(_heads(p['df_q'], 2 * DIFF_HEADS), cos, sin).reshape(b, l, DIFF_HEADS, 2, HEAD_DIM)
        df_k = _apply_rope(_heads(p['df_k'], 2 * DIFF_HEADS), cos, sin).reshape(b, l, DIFF_HEADS, 2, HEAD_DIM)
        df_kc = pc['df_k'].reshape(b, n_ctx, DIFF_HEADS, 2, HEAD_DIM)
        df_vc = _heads(pc['df_v'], DIFF_HEADS)
        lam_init = 0.8 - 0.6 * math.exp(-0.3 * i)
        lam = (jnp.exp(jnp.sum(diff_lq1[i].astype(jnp.float32) * diff_lk1[i].astype(jnp.float32)))
               - jnp.exp(jnp.sum(diff_lq2[i].astype(jnp.float32) * diff_lk2[i].astype(jnp.float32)))
               + lam_init)
        o_df = _diff_attention(df_q, jnp.concatenate([df_k, df_kc], 1),
                               jnp.concatenate([_heads(p['df_v'], DIFF_HEADS), df_vc], 1),
                               lam, lam_init, diff_subln_g[i])

        o_cv = _conformer_conv(p['cv_val'], p['cv_glu'], conv_w[i], conv_b[i], conv_ln_g[i], conv_ln_b[i])

        y = _merge(p, o_na, o_df, o_cv, w_proj_na[i], w_proj_diff[i], w_proj_conv[i], w_out[i])
        x_new = _layernorm(alpha * x + gate[:, None, :] * y, post_ln_g[i], post_ln_b[i])

        if not last:
            oc_na = _dense_attention(_heads(pc['na_q'], NA_HEADS), na_kc, na_vc)
            oc_df = _diff_attention(pc['df_q'].reshape(b, n_ctx, DIFF_HEADS, 2, HEAD_DIM), df_kc, df_vc,
                                    lam, lam_init, diff_subln_g[i])
            oc_cv = _conformer_conv(pc['cv_val'], pc['cv_glu'], conv_w[i], conv_b[i], conv_ln_g[i], conv_ln_b[i])
            yc = _merge(pc, oc_na, oc_df, oc_cv, w_proj_na[i], w_proj_diff[i], w_proj_conv[i], w_out[i])
            ctx = _layernorm(alpha * ctx + gate_c * yc, post_ln_g[i], post_ln_b[i])
        x = x_new
    return x
```

```python
import numpy as np
import ml_dtypes
from concourse.bass_utils import run_bass_kernel_spmd
import numpy as np
from contextlib import ExitStack
import concourse.bass as bass
import concourse.mybir as mybir

F32 = mybir.dt.float32
BF16 = mybir.dt.bfloat16
ALU = mybir.AluOpType
AF = mybir.ActivationFunctionType
AX = mybir.AxisListType

PE, DVE, ACT, POOL, SP = "tensor", "vector", "scalar", "gpsimd", "sync"
ENGS = [PE, DVE, ACT, POOL, SP]


class SemE:
    __slots__ = ("h", "cnt")

    def __init__(self):
        self.h = None
        self.cnt = 0


class Buf:
    __slots__ = ("name", "w", "rs", "sem")

    def __init__(self, name):
        self.name = name
        self.w = None
        self.rs = []
        self.sem = None


class Ev:
    __slots__ = ("kind", "eng", "op", "sem", "val")

    def __init__(self, kind, eng=None, op=None, sem=None, val=0):
        self.kind, self.eng, self.op, self.sem, self.val = kind, eng, op, sem, val


class Op:
    __slots__ = ("eng", "emit", "deps", "inc", "ms", "ev", "kind", "sem")

    def __init__(self, eng, emit, kind="c"):
        self.eng, self.emit, self.kind = eng, emit, kind
        self.deps = []
        self.inc = False
        self.ms = 0
        self.ev = None
        self.sem = None


class Sched:
    def __init__(self, nc, same_sync=True):
        self.nc = nc
        self.es = ExitStack()
        self.ops = {e: [] for e in ENGS}
        self.same_sync = same_sync
        self.evs = []
        self.pend = {e: [] for e in ENGS}
        self.esem = {e: SemE() for e in ENGS}
        self.ccsem = {}
        self.free_sems = []
        self.phase_bufs = []
        self.seen = {e: {} for e in ENGS}
        self.nsem = 0

    def buf(self, name, persist=False):
        b = Buf(name)
        if not persist:
            self.phase_bufs.append(b)
        return b

    def _getsem(self, b):
        if b.sem is None:
            b.sem = self.free_sems.pop() if self.free_sems else SemE()
        return b.sem

    def _deps(self, op, reads, writes):
        deps = []
        for b in reads:
            if b.w is not None:
                deps.append(b.w)
        for b in writes:
            if b.w is not None:
                deps.append(b.w)
            deps.extend(b.rs)
        deps.extend(self.pend[op.eng])
        self.pend[op.eng] = []
        for d in deps:
            if d.kind == "c":
                if d.eng == op.eng and (d.eng == PE or not self.same_sync):
                    continue
                d.op.inc = True
            op.deps.append(d)

    def _fin(self, o, ev, reads, writes):
        o.ev = ev
        for b in reads:
            b.rs.append(ev)
        for b in writes:
            b.w = ev
            b.rs = []
        self.ops[o.eng].append(o)

    def op(self, eng, emit, reads=(), writes=()):
        o = Op(eng, emit)
        self._deps(o, reads, writes)
        self._fin(o, Ev("c", eng=eng, op=o), reads, writes)
        return o

    def dma(self, queue, emit, sembuf, reads=(), writes=()):
        o = Op(queue, emit, kind="d")
        self._deps(o, reads, writes)
        s = self._getsem(sembuf)
        s.cnt += 16
        o.sem = s
        ev = Ev("d", sem=s, val=s.cnt)
        self._fin(o, ev, reads, writes)
        self.evs.append(ev)
        return o

    def dma_multi(self, queue, emits, sembuf, reads=(), writes=()):
        s = self._getsem(sembuf)
        first = True
        for em in emits:
            o = Op(queue, em, kind="d")
            if first:
                self._deps(o, reads, writes)
                first = False
            s.cnt += 16
            o.sem = s
            self.ops[queue].append(o)
        ev = Ev("d", sem=s, val=s.cnt)
        for b in reads:
            b.rs.append(ev)
        for b in writes:
            b.w = ev
            b.rs = []
        self.evs.append(ev)

    def cc(self, emit, semname, reads=(), writes=()):
        o = Op(POOL, emit, kind="k")
        self._deps(o, reads, writes)
        s = self.ccsem.setdefault(semname, SemE())
        s.cnt += 1
        o.sem = s
        ev = Ev("k", sem=s, val=s.cnt)
        self._fin(o, ev, reads, writes)
        self.evs.append(ev)
        return o

    def barrier(self):
        evs = list(self.evs)
        self.evs = []
        for e in ENGS:
            for o in reversed(self.ops[e]):
                if o.kind == "c":
                    evs.append(o.ev)
                    break
        for e in ENGS:
            for ev in evs:
                if ev.kind == "c":
                    if ev.eng == e:
                        continue
                    ev.op.inc = True
                self.pend[e].append(ev)

    def _alloc(self, s):
        if s.h is None:
            s.h = self.es.enter_context(self.nc.semaphore("s%d" % self.nsem))
            self.nsem += 1
        return s.h

    def emit_phase(self):
        self.barrier()
        nc = self.nc
        for e in ENGS:
            c = self.esem[e].cnt
            for o in self.ops[e]:
                if o.kind == "c" and o.inc:
                    c += 1
                    o.ms = c
            self.esem[e].cnt = c
            self._alloc(self.esem[e])
        for e in ENGS:
            for o in self.ops[e]:
                if o.sem is not None:
                    self._alloc(o.sem)
                for d in o.deps:
                    if d.sem is not None:
                        self._alloc(d.sem)

        def waitfor(h, e, d):
            if d.kind == "c":
                sem, val = self.esem[d.eng], d.op.ms
            else:
                sem, val = d.sem, d.val
            if self.seen[e].get(id(sem), 0) >= val:
                return
            self.seen[e][id(sem)] = val
            h.wait_ge(sem.h, val)

        def run_engine(e, h):
            for o in self.ops[e]:
                for d in o.deps:
                    waitfor(h, e, d)
                ins = o.emit(h)
                if o.kind == "c":
                    if o.inc:
                        ins.then_inc(self.esem[e].h, 1)
                elif o.kind == "d":
                    ins.then_inc(o.sem.h, 16)
                else:
                    ins.then_inc(o.sem.h)
            for d in self.pend[e]:
                waitfor(h, e, d)
            self.pend[e] = []

        with nc.Block() as block:
            @block.tensor
            def _(h):
                run_engine(PE, h)

            @block.vector
            def _(h):
                run_engine(DVE, h)

            @block.scalar
            def _(h):
                run_engine(ACT, h)

            @block.gpsimd
            def _(h):
                run_engine(POOL, h)

            @block.sync
            def _(h):
                run_engine(SP, h)
        self.ops = {e: [] for e in ENGS}
        for b in self.phase_bufs:
            if b.sem is not None:
                self.free_sems.append(b.sem)
                b.sem = None
        self.phase_bufs = []


NCORE = 8


class Cfg:
    def __init__(self, D=4096, SEQ=16384, CTX=256, NAW=2048, DFW=2048, CVW=2048, L=2):
        self.D, self.SEQ, self.CTX, self.NAW, self.DFW, self.CVW, self.L = D, SEQ, CTX, NAW, DFW, CVW, L
        self.KC = D // 128
        self.TOK = SEQ // NCORE
        self.T = self.TOK + CTX
        segs = [("na_q", NAW), ("na_k", NAW), ("na_v", NAW), ("na_gate", NAW), ("df_q", DFW), ("df_k", DFW),
                ("df_v", DFW), ("df_gate", DFW), ("cv_val", CVW), ("cv_glu", CVW), ("cv_gate", CVW),
                ("merge_na", D), ("merge_df", D), ("merge_cv", D)]
        self.seg = {}
        o = 0
        for n, s in segs:
            self.seg[n] = (o, s)
            o += s
        self.NIN = o


class GW:
    def __init__(self, nc, S, name, K, N, src, probe=False):
        self.probe = probe
        self.K, self.N = K, N
        self.kcl = K // 128 // NCORE
        self.nst = N // 256
        piece = self.kcl * 128 * 256 * 2
        self.spc = max(1, (512 * 1024) // piece)
        while self.nst % self.spc:
            self.spc -= 1
        self.nch = self.nst // self.spc
        rows = self.spc * self.kcl * 128
        self.rows = rows
        self.snd = [nc.dram_tensor("%s_snd%d" % (name, i), [rows, 256], BF16, kind="Internal").ap() for i in range(self.nch)]
        self.g4 = [nc.dram_tensor("%s_g4_%d" % (name, i), [4 * rows, 256], BF16, kind="Internal").ap() for i in range(self.nch)]
        self.g8 = [nc.dram_tensor("%s_g8_%d" % (name, i), [8 * rows, 256], BF16, kind="Internal").ap() for i in range(self.nch)]
        self.buf = S.buf(name + "_g", persist=True)
        self.src = src
        self.name = name

    def gather(self, nc, S, castbuf):
        bs = S.buf(self.name + "_s")
        b4 = S.buf(self.name + "_4")
        for ci in range(self.nch):
            c0 = 0 if self.probe else ci * self.spc * 256
            src = self.src[:, c0:c0 + self.spc * 256].rearrange("r (s n) -> s r n", n=256)
            dst = self.snd[ci].rearrange("(s r) n -> s r n", s=self.spc)
            cb = castbuf[ci % len(castbuf)]
            S.dma(POOL, lambda h, d=dst, s=src: h.dma_start(out=d, in_=s), cb, writes=[bs, cb])
        for ci in range(self.nch):
            S.cc(lambda h, ci=ci: h.collective_compute("AllGather", ALU.bypass, replica_groups=[[0, 1, 2, 3], [4, 5, 6, 7]],
                                                       ins=[self.snd[ci]], outs=[self.g4[ci]]), "a", reads=[bs], writes=[b4])
        for ci in range(self.nch):
            S.cc(lambda h, ci=ci: h.collective_compute("AllGather", ALU.bypass, replica_groups=[[0, 4], [1, 5], [2, 6], [3, 7]],
                                                       ins=[self.g4[ci]], outs=[self.g8[ci]]), "b", reads=[b4], writes=[self.buf])

    def st_ap(self, st):
        ci, s = divmod(st, self.spc)
        v = self.g8[ci].rearrange("(r s k p) n -> s p r k n", r=NCORE, s=self.spc, k=self.kcl, p=128)
        return v[s]


class GA:
    def __init__(self, nc, S, name, R, C):
        rpc = max(1, min(R, (512 * 1024) // (C * 2)))
        while R % rpc:
            rpc -= 1
        self.rpc, self.nch, self.R, self.C, self.name = rpc, R // rpc, R, C, name
        self.snd = [nc.dram_tensor("%s_snd%d" % (name, i), [rpc, C], BF16, kind="Internal").ap() for i in range(self.nch)]
        self.g4 = [nc.dram_tensor("%s_g4_%d" % (name, i), [4 * rpc, C], BF16, kind="Internal").ap() for i in range(self.nch)]
        self.g8 = [nc.dram_tensor("%s_g8_%d" % (name, i), [8 * rpc, C], BF16, kind="Internal").ap() for i in range(self.nch)]
        self.bs = [S.buf("%s_s%d" % (name, i), persist=True) for i in range(self.nch)]
        self.b4 = S.buf(name + "_4", persist=True)
        self.buf = S.buf(name + "_g", persist=True)

    def snd_rows(self, r0, n):
        ci, o = divmod(r0, self.rpc)
        assert o + n <= self.rpc
        return self.snd[ci][o:o + n]

    def bsc(self, r0):
        return self.bs[r0 // self.rpc]

    def rows(self, rank, r0, n):
        ci, o = divmod(r0, self.rpc)
        assert o + n <= self.rpc
        return self.g8[ci][rank * self.rpc + o: rank * self.rpc + o + n]

    def gather(self, S):
        for ci in range(self.nch):
            S.cc(lambda h, ci=ci: h.collective_compute("AllGather", ALU.bypass, replica_groups=[[0, 1, 2, 3], [4, 5, 6, 7]],
                                                       ins=[self.snd[ci]], outs=[self.g4[ci]]), "a", reads=[self.bs[ci]], writes=[self.b4])
        for ci in range(self.nch):
            S.cc(lambda h, ci=ci: h.collective_compute("AllGather", ALU.bypass, replica_groups=[[0, 4], [1, 5], [2, 6], [3, 7]],
                                                       ins=[self.g4[ci]], outs=[self.g8[ci]]), "b", reads=[self.b4], writes=[self.buf])


def build(cfg, debug=None, stop=99, probe=False):
    nc = bass.Bass("TRN2", target_bir_lowering=False)
    D, KC, T, TOK, CTX, L, NIN = cfg.D, cfg.KC, cfg.T, cfg.TOK, cfg.CTX, cfg.L, cfg.NIN
    inp = {}

    def ein(name, shape, dt=F32):
        inp[name] = nc.dram_tensor(name, list(shape), dt, kind="ExternalInput").ap()
        return inp[name]

    x_in = ein("x", [TOK if not probe else 128, D])
    ctx_in = ein("ctx", [CTX, D])
    c2 = ein("c2", [2, D])
    w_ada = ein("w_ada", [L, D, 3 * D // NCORE if not probe else 512])
    b_ada = ein("b_ada", [L, 3 * D // NCORE])
    w_in = ein("w_in", [L, D // NCORE, NIN if not probe else 512])
    b_in = ein("b_in", [L, NIN])
    ident_in = ein("ident", [128, 128], BF16)
    NAW, DFW, CVW = cfg.NAW, cfg.DFW, cfg.CVW
    cos_in = ein("ropecos", [128, TOK])
    sin_in = ein("ropesin", [128, TOK])
    perm_in = ein("perm", [128, 128], BF16)
    sel_in = ein("sel", [128, 16])
    rowmask_in = ein("rowmask", [5, 128, 8, 128])
    rpb_in = ein("rpbT", [L, NAW // 128, 128, 8, 128])
    lam_in = ein("lamv", [L * 4 * 128])
    subg_in = ein("subg", [L, 256])
    convw_in = ein("conv_w", [L, 31, CVW])
    convb_in = ein("conv_b", [L, CVW])
    convg_in = ein("conv_g", [L, CVW])
    convbt_in = ein("conv_bt", [L, CVW])
    plg_in = ein("post_g", [L * D])
    plb_in = ein("post_b", [L * D])
    wpna_in = ein("w_pna", [L, NAW // NCORE, D])
    wpdf_in = ein("w_pdf", [L, DFW // NCORE, D])
    wpcv_in = ein("w_pcv", [L, CVW // NCORE, D])
    wout_in = ein("w_out", [L, D // NCORE, D])
    out = nc.dram_tensor("out", [TOK, D], F32, kind="ExternalOutput").ap()
    dbg = None
    if debug:
        dbg = nc.dram_tensor("dbg", list(debug), F32, kind="ExternalOutput").ap()

    S = Sched(nc)
    with S.es:
        es0 = S.es
        banks = [es0.enter_context(nc.psum_tensor("bank%d" % i, [128, 512], F32)) for i in range(8)]
        bbank = [S.buf("bank%d" % i, persist=True) for i in range(8)]

        def MM(out_, lhsT, rhs, st, sp, R, W):
            S.op(PE, lambda h: h.matmul(out_, lhsT, rhs, start=st, stop=sp), reads=R, writes=W)

        def AC(out_, in_, func, R, W, **kw):
            S.op(ACT, lambda h: h.activation(out=out_, in_=in_, func=func, **kw), reads=R, writes=W)

        def TT(out_, a, b, op, R, W, eng=DVE):
            S.op(eng, lambda h: h.tensor_tensor(out_, a, b, op), reads=R, writes=W)

        def TS(out_, a, s1, s2, op0, op1, R, W, eng=DVE):
            if op1 is None:
                S.op(eng, lambda h: h.tensor_scalar(out_, a, s1, None, op0), reads=R, writes=W)
            else:
                S.op(eng, lambda h: h.tensor_scalar(out_, a, s1, s2, op0, op1), reads=R, writes=W)

        def STT(out_, a, sc, b, op0, op1, R, W):
            S.op(DVE, lambda h: h.scalar_tensor_tensor(out_, a, sc, b, op0, op1), reads=R, writes=W)

        def LD(out_, in_, sb, R, W, q=SP, slow=False):
            S.dma(q, lambda h: h.dma_start(out=out_, in_=in_, allow_slow_non_contiguous=slow), sb, reads=R, writes=W)

        def RCP(out_, in_, R, W):
            S.op(DVE, lambda h: h.reciprocal(out_, in_), reads=R, writes=W)

        def MS(ap, val, W, eng=POOL):
            S.op(eng, lambda h: h.memset(ap, val), writes=W)

        ident = es0.enter_context(nc.sbuf_tensor("ident_sb", [128, 128], BF16))
        ones32 = es0.enter_context(nc.sbuf_tensor("ones32", [128, 128], F32))
        eps6 = es0.enter_context(nc.sbuf_tensor("eps6", [128, 2], F32))
        bconst = S.buf("const", persist=True)
        castbuf = S.buf("castsem", persist=True)
        castbufs = [S.buf("castsem%d" % i, persist=True) for i in range(4)]
        modsnd = nc.dram_tensor("modsnd", [2 * L, 3 * D // NCORE], F32, kind="Internal").ap()
        mod4 = nc.dram_tensor("mod4", [4 * 2 * L, 3 * D // NCORE], F32, kind="Internal").ap()
        mod8 = nc.dram_tensor("mod8", [8 * 2 * L, 3 * D // NCORE], F32, kind="Internal").ap()
        bmod = S.buf("mod8", persist=True)
        zT = nc.dram_tensor("zT", [NIN, T], BF16, kind="Internal").ap()
        bzT = S.buf("zT", persist=True)
        xcur = nc.dram_tensor("xcur", [T, D], F32, kind="Internal").ap()
        bxcur = S.buf("xcur", persist=True)
        vtok = nc.dram_tensor("vtok", [T, NAW + DFW], BF16, kind="Internal").ap()
        bvtok = S.buf("vtok", persist=True)
        gT = nc.dram_tensor("gT", [NAW + DFW + CVW, T], BF16, kind="Internal").ap()
        bgT = S.buf("gT", persist=True)
        uT = nc.dram_tensor("uT", [CVW, T], BF16, kind="Internal").ap()
        buT = S.buf("uT", persist=True)
        kdf = GA(nc, S, "kdf", DFW, TOK)
        vdf = GA(nc, S, "vdf", TOK, DFW)
        knh = GA(nc, S, "knh", NAW, 512)
        vnh = GA(nc, S, "vnh", 512, NAW)
        cvh = GA(nc, S, "cvh", CVW, 32)

        gw_in = [GW(nc, S, "win%d" % l, D, NIN, w_in[l], probe) for l in range(L)]
        gw_pna = [GW(nc, S, "wpna%d" % l, NAW, D, wpna_in[l]) for l in range(L)]
        gw_pdf = [GW(nc, S, "wpdf%d" % l, DFW, D, wpdf_in[l]) for l in range(L)]
        gw_pcv = [GW(nc, S, "wpcv%d" % l, CVW, D, wpcv_in[l]) for l in range(L)]
        gw_out = [GW(nc, S, "wout%d" % l, D, D, wout_in[l]) for l in range(L)]
        with ExitStack() as es:
            S.op(POOL, lambda h: h.memset(ones32[:], 1.0), writes=[bconst])
            S.op(POOL, lambda h: h.memset(eps6[:, 0:1], 1e-6), writes=[bconst])
            S.op(POOL, lambda h: h.memset(eps6[:, 1:2], 1e-5), writes=[bconst])
            S.dma(SP, lambda h: h.dma_start(out=ident[:], in_=ident_in), bconst, writes=[bconst])
            import os
            for l in range(int(os.environ.get("NGL", L))):
                gw_in[l].gather(nc, S, castbufs)
            for l in range(L):
                for g_ in (gw_pna[l], gw_pdf[l], gw_pcv[l], gw_out[l]):
                    g_.gather(nc, S, castbufs)
            for r0 in range(0, TOK, 128):
                S.dma(SP, lambda h, r0=r0: h.dma_start(out=xcur[r0:r0 + 128, :], in_=x_in[(0 if probe else r0):(0 if probe else r0) + 128, :]), castbuf, writes=[bxcur])
            S.dma(SP, lambda h: h.dma_start(out=xcur[TOK:T, :], in_=ctx_in), castbuf, writes=[bxcur])
            S.emit_phase()

        NA3 = 3 * D // NCORE
        if stop < 1:
            return nc
        with ExitStack() as es:
            cT = es.enter_context(nc.sbuf_tensor("cT", [128, KC, 2], F32))
            bcT = S.buf("cT")
            wa = [es.enter_context(nc.sbuf_tensor("wa%d" % i, [128, 8, 512], F32)) for i in range(2)]
            bwa = [S.buf("wa%d" % i) for i in range(2)]
            mo = es.enter_context(nc.sbuf_tensor("mo", [2, L, NA3], F32))
            bb = es.enter_context(nc.sbuf_tensor("bb", [2, L, NA3], F32))
            bmo = S.buf("mo")
            bbb = S.buf("bb")
            for v in range(2):
                S.dma(SP, lambda h, v=v: h.dma_start(out=cT[:, :, v], in_=c2[v].rearrange("(k p) -> p k", p=128), allow_slow_non_contiguous=True),
                      bcT, writes=[bcT])
            for v in range(2):
                S.dma(SP, lambda h, v=v: h.dma_start(out=bb[v:v + 1], in_=b_ada.rearrange("(o l) n -> o l n", o=1)), bbb, writes=[bbb])
            S.op(ACT, lambda h: h.activation(out=cT[:], in_=cT[:], func=AF.Silu), reads=[bcT], writes=[bcT])
            it = 0
            for l in range(L):
                for n0 in range(0, NA3, 512):
                    nn = min(512, NA3 - n0)
                    for k0 in range(0, KC, 8):
                        kk = min(8, KC - k0)
                        sl = it % 2
                        it += 1
                        S.dma(SP, lambda h, sl=sl, l=l, n0=n0, nn=nn, k0=k0, kk=kk: h.dma_start(
                            out=wa[sl][:, 0:kk, 0:nn],
                            in_=w_ada[l, k0 * 128:(k0 + kk) * 128, (0 if probe else n0):(0 if probe else n0) + nn].rearrange("(k p) n -> p k n", p=128)),
                            bwa[sl], writes=[bwa[sl]])
                        for k in range(kk):
                            S.op(PE, lambda h, sl=sl, k=k, k0=k0, nn=nn: h.matmul(banks[0][0:2, 0:nn], cT[:, k0 + k, :], wa[sl][:, k, 0:nn],
                                                                                   start=(k0 + k == 0), stop=(k0 + k == KC - 1)),
                                 reads=[bcT, bwa[sl]], writes=[bbank[0]])
                    S.op(DVE, lambda h, l=l, n0=n0, nn=nn: h.tensor_tensor(mo[:, l, n0:n0 + nn], banks[0][0:2, 0:nn], bb[:, l, n0:n0 + nn], ALU.add),
                         reads=[bbank[0], bbb], writes=[bmo])
            bms = S.buf("modsnd")
            bm4 = S.buf("mod4")
            S.dma(SP, lambda h: h.dma_start(out=modsnd.rearrange("(v l) n -> v l n", v=2), in_=mo[:]), bmo, reads=[bmo], writes=[bms])
            S.cc(lambda h: h.collective_compute("AllGather", ALU.bypass, replica_groups=[[0, 1, 2, 3], [4, 5, 6, 7]], ins=[modsnd], outs=[mod4]),
                 "a", reads=[bms], writes=[bm4])
            S.cc(lambda h: h.collective_compute("AllGather", ALU.bypass, replica_groups=[[0, 4], [1, 5], [2, 6], [3, 7]], ins=[mod4], outs=[mod8]),
                 "b", reads=[bm4], writes=[bmod])
            S.emit_phase()

        if stop < 2:
            return nc
        def mod_cols(v, l, j0, n):
            res = []
            j = j0
            while j < j0 + n:
                r, o = divmod(j, NA3)
                m = min(NA3 - o, j0 + n - j)
                row = (r * 2 + v) * L + l
                res.append((mod8[row, o:o + m], j - j0, m))
                j += m
            return res

        for l in range(L):
            with ExitStack() as es:
                hT = es.enter_context(nc.sbuf_tensor("hT_%d" % l, [128, KC, T], BF16))
                bhT = S.buf("hT")
                sc1 = es.enter_context(nc.sbuf_tensor("sc1_%d" % l, [128, 2, KC], F32))
                sh1 = es.enter_context(nc.sbuf_tensor("sh1_%d" % l, [128, 2, KC], F32))
                bsc = S.buf("sc")
                for v in range(2):
                    for (ap, o, m) in mod_cols(v, l, D, D):
                        S.dma(SP, lambda h, ap=ap, o=o, m=m, v=v: h.dma_start(out=sc1[:, v, o // 128:(o + m) // 128],
                                                                               in_=ap.rearrange("(k p) -> p k", p=128), allow_slow_non_contiguous=True),
                              bsc, reads=[bmod], writes=[bsc])
                    for (ap, o, m) in mod_cols(v, l, 0, D):
                        S.dma(SP, lambda h, ap=ap, o=o, m=m, v=v: h.dma_start(out=sh1[:, v, o // 128:(o + m) // 128],
                                                                               in_=ap.rearrange("(k p) -> p k", p=128), allow_slow_non_contiguous=True),
                              bsc, reads=[bmod], writes=[bsc])
                S.op(DVE, lambda h: h.tensor_scalar(sc1[:], sc1[:], 1.0, None, ALU.add), reads=[bsc], writes=[bsc])
                es2 = ExitStack()
                xt = [es2.enter_context(nc.sbuf_tensor("xt%d_%d" % (i, l), [128, D], F32)) for i in range(2)]
                bxt = [S.buf("xt%d" % i) for i in range(2)]
                xn1 = es2.enter_context(nc.sbuf_tensor("xn_%d" % l, [128, D], BF16))
                xn = [xn1, xn1]
                bxn1 = S.buf("xn")
                bxn = [bxn1, bxn1]
                st = es2.enter_context(nc.sbuf_tensor("st_%d" % l, [128, 8, 6], F32))
                mv = es2.enter_context(nc.sbuf_tensor("mv_%d" % l, [128, 2], F32))
                bst = S.buf("st")
                NT = T // 128
                nch = (D + 511) // 512
                for t in range(NT):
                    sl = t % 2
                    v = 0 if t * 128 < TOK else 1
                    S.dma(SP, lambda h, t=t, sl=sl: h.dma_start(out=xt[sl][:], in_=xcur[t * 128:(t + 1) * 128, :]), bxt[sl],
                          reads=[bxcur], writes=[bxt[sl]])
                    for c in range(nch):
                        S.op(DVE, lambda h, c=c, sl=sl: h.bn_stats(st[:, c, :], xt[sl][:, c * 512:(c + 1) * 512]), reads=[bxt[sl]], writes=[bst])
                    S.op(DVE, lambda h: h.bn_aggr(mv[:], st[:, 0:nch, :].rearrange("p c s -> p (c s)")), reads=[bst], writes=[bst])
                    S.op(ACT, lambda h: h.activation(out=mv[:, 1:2], in_=mv[:, 1:2], func=AF.Sqrt, bias=eps6[:, 0:1]), reads=[bst, bconst], writes=[bst])
                    S.op(DVE, lambda h: h.reciprocal(mv[:, 1:2], mv[:, 1:2]), reads=[bst], writes=[bst])
                    S.op(DVE, lambda h, sl=sl: h.tensor_scalar(xn[sl][:], xt[sl][:], mv[:, 0:1], mv[:, 1:2], ALU.subtract, ALU.mult),
                         reads=[bxt[sl], bst], writes=[bxn[sl]])
                    for k in range(KC):
                        bk = 1 + k % 6
                        S.op(PE, lambda h, k=k, sl=sl, bk=bk: h.matmul(banks[bk][:, 0:128], xn[sl][:, k * 128:(k + 1) * 128], ident[:], start=True, stop=True),
                             reads=[bxn[sl], bconst], writes=[bbank[bk]])
                        S.op(ACT, lambda h, k=k, t=t, v=v, bk=bk: h.activation(out=hT[:, k, t * 128:(t + 1) * 128], in_=banks[bk][:, 0:128],
                                                                                func=AF.Identity, scale=sc1[:, v, k:k + 1], bias=sh1[:, v, k:k + 1]),
                             reads=[bbank[bk], bsc], writes=[bhT])
                S.emit_phase()
                es2.close()
                if stop < 3:
                    return nc
                wt = [es.enter_context(nc.sbuf_tensor("wt%d_%d" % (i, l), [128, KC, 256], BF16)) for i in range(2)]
                bwt = [S.buf("wt%d" % i) for i in range(2)]
                zt = [es.enter_context(nc.sbuf_tensor("zt%d_%d" % (i, l), [128, T], BF16)) for i in range(2)]
                bzt = [S.buf("zt%d" % i) for i in range(2)]
                bia = es.enter_context(nc.sbuf_tensor("bia_%d" % l, [128, NIN // 128], F32))
                bbia = S.buf("bia")
                S.dma(SP, lambda h, l=l: h.dma_start(out=bia[:], in_=b_in[l].rearrange("(n p) -> p n", p=128), allow_slow_non_contiguous=True),
                      bbia, writes=[bbia])
                g = gw_in[l]
                tch = [(t0, min(512, T - t0)) for t0 in range(0, T, 512)]
                cnt = 0
                for s_ in range(g.nst):
                    sl = s_ % 2
                    S.dma_multi(SP, [lambda h, s_=s_, sl=sl, r=r: h.dma_start(out=wt[sl][:, r * g.kcl:(r + 1) * g.kcl, :], in_=g.st_ap(s_)[:, r])
                                     for r in range(NCORE)], bwt[sl], reads=[g.buf], writes=[bwt[sl]])
                    for half in range(2):
                        nt = s_ * 2 + half
                        zs = nt % 2
                        for ci, (t0, tn) in enumerate(tch):
                            bk = 1 + (cnt % 6)
                            cnt += 1
                            for k in range(KC):
                                S.op(PE, lambda h, k=k, sl=sl, half=half, t0=t0, tn=tn, bk=bk: h.matmul(
                                    banks[bk][:, 0:tn], wt[sl][:, k, half * 128:(half + 1) * 128], hT[:, k, t0:t0 + tn], start=(k == 0), stop=(k == KC - 1)),
                                    reads=[bwt[sl], bhT], writes=[bbank[bk]])
                            S.op(ACT, lambda h, zs=zs, t0=t0, tn=tn, bk=bk, nt=nt: h.activation(out=zt[zs][:, t0:t0 + tn], in_=banks[bk][:, 0:tn],
                                                                                             func=AF.Identity, bias=bia[:, nt:nt + 1]),
                                 reads=[bbank[bk], bbia], writes=[bzt[zs]])
                        S.dma(POOL, lambda h, zs=zs, nt=nt: h.dma_start(out=zT[nt * 128:(nt + 1) * 128, :], in_=zt[zs][:]), bzt[zs],
                              reads=[bzt[zs]], writes=[bzT])
                bbcs = [es.enter_context(nc.sbuf_tensor("bbc%d_%d" % (i, l), [128, 256], F32)) for i in range(2)]
                bbbc = [S.buf("bbc%d" % i) for i in range(2)]
                vts = [es.enter_context(nc.sbuf_tensor("vts%d_%d" % (i, l), [128, 256], BF16)) for i in range(2)]
                bvts = [S.buf("vts%d" % i) for i in range(2)]
                vit = 0
                sidx = g.nst
                for (so, vo, sz) in ((cfg.seg["na_v"][0], 0, NAW), (cfg.seg["df_v"][0], NAW, DFW)):
                    for s2 in range(sz // 256):
                        st_ = (so + s2 * 256) // 256
                        sl = sidx % 2
                        sidx += 1
                        S.dma_multi(SP, [lambda h, st_=st_, sl=sl, r=r: h.dma_start(out=wt[sl][:, r * g.kcl:(r + 1) * g.kcl, :], in_=g.st_ap(st_)[:, r])
                                         for r in range(NCORE)], bwt[sl], reads=[g.buf], writes=[bwt[sl]])
                        LD(bbcs[sl][:], bass.AP(b_in.tensor, l * NIN + so + s2 * 256, [[0, 128], [1, 256]]), bbbc[sl], [], [bbbc[sl]])
                        for t in range(NT):
                            bk = 1 + (cnt % 6)
                            cnt += 1
                            v_ = vit % 2
                            vit += 1
                            for k in range(KC):
                                MM(banks[bk][:, 0:256], hT[:, k, t * 128:(t + 1) * 128], wt[sl][:, k, :], k == 0, k == KC - 1, [bhT, bwt[sl]], [bbank[bk]])
                            TT(vts[v_][:], banks[bk][:, 0:256], bbcs[sl][:], ALU.add, [bbank[bk], bbbc[sl]], [bvts[v_]])
                            LD(vtok[t * 128:(t + 1) * 128, vo + s2 * 256:vo + (s2 + 1) * 256], vts[v_][:], bvts[v_], [bvts[v_]], [bvtok], q=POOL)
                S.emit_phase()
            last = (l == L - 1)
            TE = TOK if last else T
            ROWS = TOK // 64
            NP = ROWS // 2
            NH = NAW // 128
            HD = DFW // 256
            CT = CVW // 128
            sc_att = 128.0 ** -0.5
            o_naq, o_nak, o_nag = cfg.seg["na_q"][0], cfg.seg["na_k"][0], cfg.seg["na_gate"][0]
            o_dfq, o_dfk, o_dfg = cfg.seg["df_q"][0], cfg.seg["df_k"][0], cfg.seg["df_gate"][0]
            o_cvv, o_cvg, o_cvgate = cfg.seg["cv_val"][0], cfg.seg["cv_glu"][0], cfg.seg["cv_gate"][0]
            o_mg = [cfg.seg["merge_na"][0], cfg.seg["merge_df"][0], cfg.seg["merge_cv"][0]]
            tch = [(t0, min(512, TOK - t0)) for t0 in range(0, TOK, 512)]

            with ExitStack() as es:
                cosT = es.enter_context(nc.sbuf_tensor("cosT_%d" % l, [128, TOK], F32))
                sinT = es.enter_context(nc.sbuf_tensor("sinT_%d" % l, [128, TOK], F32))
                perm = es.enter_context(nc.sbuf_tensor("perm_%d" % l, [128, 128], BF16))
                btab = S.buf("ropetab")
                LD(cosT[:], cos_in, btab, [], [btab])
                LD(sinT[:], sin_in, btab, [], [btab])
                LD(perm[:], perm_in, btab, [], [btab])
                xr = [es.enter_context(nc.sbuf_tensor("xr%d_%d" % (i, l), [128, TOK], BF16)) for i in range(2)]
                bxr = [S.buf("xr%d" % i) for i in range(2)]
                xo = [es.enter_context(nc.sbuf_tensor("xo%d_%d" % (i, l), [128, TOK], BF16)) for i in range(2)]
                bxo = [S.buf("xo%d" % i) for i in range(2)]
                t1 = es.enter_context(nc.sbuf_tensor("rt1_%d" % l, [128, 512], F32))
                t2 = es.enter_context(nc.sbuf_tensor("rt2_%d" % l, [128, 512], F32))
                bt1, bt2 = S.buf("rt1"), S.buf("rt2")
                it = 0
                for (so, isk) in ((o_dfq, False), (o_dfk, True)):
                    for hh in range(DFW // 128):
                        sl = it % 2
                        it += 1
                        LD(xr[sl][:], zT[so + hh * 128: so + (hh + 1) * 128, 0:TOK], bxr[sl], [bzT], [bxr[sl]])
                        for ci, (t0, tn) in enumerate(tch):
                            bk = ci % 2
                            MM(banks[bk][:, 0:tn], perm[:], xr[sl][:, t0:t0 + tn], True, True, [btab, bxr[sl]], [bbank[bk]])
                            TT(t1[:, 0:tn], xr[sl][:, t0:t0 + tn], cosT[:, t0:t0 + tn], ALU.mult, [bxr[sl], btab], [bt1])
                            TT(t2[:, 0:tn], banks[bk][:, 0:tn], sinT[:, t0:t0 + tn], ALU.mult, [bbank[bk], btab], [bt2])
                            TT(xo[sl][:, t0:t0 + tn], t1[:, 0:tn], t2[:, 0:tn], ALU.add, [bt1, bt2], [bxo[sl]])
                        if isk:
                            LD(kdf.snd_rows(hh * 128, 128), xo[sl][:], bxo[sl], [bxo[sl]], [kdf.bsc(hh * 128)], q=POOL)
                        else:
                            LD(zT[so + hh * 128: so + (hh + 1) * 128, 0:TOK], xo[sl][:], bxo[sl], [bxo[sl]], [bzT], q=POOL)
                for r0 in range(0, TOK, vdf.rpc):
                    LD(vdf.snd_rows(r0, vdf.rpc), vtok[r0:r0 + vdf.rpc, NAW:NAW + DFW], castbuf, [bvtok], [vdf.bsc(r0)])
                for r0 in range(0, NAW, knh.rpc):
                    LD(knh.snd_rows(r0, knh.rpc)[:, 0:256], zT[o_nak + r0:o_nak + r0 + knh.rpc, 0:256], castbuf, [bzT], [knh.bsc(r0)])
                    LD(knh.snd_rows(r0, knh.rpc)[:, 256:512], zT[o_nak + r0:o_nak + r0 + knh.rpc, TOK - 256:TOK], castbuf, [bzT], [knh.bsc(r0)])
                for r0 in range(0, 512, vnh.rpc):
                    tk0 = r0 if r0 < 256 else TOK - 512 + r0
                    LD(vnh.snd_rows(r0, vnh.rpc), vtok[tk0:tk0 + vnh.rpc, 0:NAW], castbuf, [bvtok], [vnh.bsc(r0)])
                uv = [es.enter_context(nc.sbuf_tensor("uv%d_%d" % (i, l), [128, T], BF16)) for i in range(2)]
                ug = [es.enter_context(nc.sbuf_tensor("ug%d_%d" % (i, l), [128, T], BF16)) for i in range(2)]
                usg = es.enter_context(nc.sbuf_tensor("usg_%d" % l, [128, T], F32))
                uo = [es.enter_context(nc.sbuf_tensor("uo%d_%d" % (i, l), [128, T], BF16)) for i in range(2)]
                buv = [S.buf("uv%d" % i) for i in range(2)]
                bug = [S.buf("ug%d" % i) for i in range(2)]
                buo = [S.buf("uo%d" % i) for i in range(2)]
                busg = S.buf("usg")
                for ct in range(CT):
                    sl = ct % 2
                    LD(uv[sl][:], zT[o_cvv + ct * 128:o_cvv + (ct + 1) * 128, :], buv[sl], [bzT], [buv[sl]])
                    LD(ug[sl][:], zT[o_cvg + ct * 128:o_cvg + (ct + 1) * 128, :], bug[sl], [bzT], [bug[sl]])
                    AC(usg[:], ug[sl][:], AF.Sigmoid, [bug[sl]], [busg])
                    TT(uo[sl][:], uv[sl][:], usg[:], ALU.mult, [buv[sl], busg], [buo[sl]])
                    LD(uT[ct * 128:(ct + 1) * 128, :], uo[sl][:], buo[sl], [buo[sl]], [buT], q=POOL)
                    LD(cvh.snd_rows(ct * 128, 128)[:, 0:15], uo[sl][:, 0:15], buo[sl], [buo[sl]], [cvh.bsc(ct * 128)], q=POOL, slow=True)
                    LD(cvh.snd_rows(ct * 128, 128)[:, 16:31], uo[sl][:, TOK - 15:TOK], buo[sl], [buo[sl]], [cvh.bsc(ct * 128)], q=POOL, slow=True)
                for ga in (kdf, vdf, knh, vnh, cvh):
                    ga.gather(S)
                S.emit_phase()
            if stop < 4:
                return nc

            with ExitStack() as es:
                NB = NP + 8
                sel = es.enter_context(nc.sbuf_tensor("sel_%d" % l, [128, 16], F32))
                rm = es.enter_context(nc.sbuf_tensor("rm_%d" % l, [128, 5, 1024], F32))
                onesb = es.enter_context(nc.sbuf_tensor("onesb_%d" % l, [128, 128], BF16))
                bsel = S.buf("sel")
                LD(sel[:], sel_in, bsel, [], [bsel])
                LD(rm[:], rowmask_in.rearrange("s p c q -> p s (c q)"), bsel, [], [bsel])
                MS(onesb[:], 1.0, [bsel])
                rp = [es.enter_context(nc.sbuf_tensor("rp%d_%d" % (i, l), [128, 1024], F32)) for i in range(2)]
                brp = [S.buf("rp%d" % i) for i in range(2)]
                B5 = es.enter_context(nc.sbuf_tensor("B5_%d" % l, [128, 5, 1024], F32))
                bB5 = S.buf("B5")
                kbuf = es.enter_context(nc.sbuf_tensor("kbuf_%d" % l, [128, NB * 128], BF16))
                vbuf = es.enter_context(nc.sbuf_tensor("vbuf_%d" % l, [128, NB, 128], BF16))
                bkb, bvb = S.buf("kbuf"), S.buf("vbuf")
                MS(kbuf[:], 0.0, [bkb])
                MS(vbuf[:], 0.0, [bvb])
                hk = es.enter_context(nc.sbuf_tensor("hk_%d" % l, [128, 8, 512], BF16))
                hv = es.enter_context(nc.sbuf_tensor("hv_%d" % l, [128, 8, 4, 128], BF16))
                bhk, bhv = S.buf("hk"), S.buf("hv")
                kcs = es.enter_context(nc.sbuf_tensor("kcs_%d" % l, [128, CTX], BF16))
                vcs = es.enter_context(nc.sbuf_tensor("vcs_%d" % l, [128, CTX // 128, 128], BF16))
                bkc = S.buf("kcs")
                qT = es.enter_context(nc.sbuf_tensor("qTn_%d" % l, [128, T], BF16))
                gg = es.enter_context(nc.sbuf_tensor("ggn_%d" % l, [128, T], BF16))
                gsil = es.enter_context(nc.sbuf_tensor("gsn_%d" % l, [128, T], F32))
                bq, bgg, bgs = S.buf("qTn"), S.buf("ggn"), S.buf("gsn")
                oh = [es.enter_context(nc.sbuf_tensor("oh%d_%d" % (i, l), [128, T], BF16)) for i in range(2)]
                boh = [S.buf("oh%d" % i) for i in range(2)]
                ein_ = es.enter_context(nc.sbuf_tensor("ein_%d" % l, [128, 1024], F32))
                bein = S.buf("ein")
                E = [es.enter_context(nc.sbuf_tensor("E%d_%d" % (i, l), [128, 1024 + CTX], BF16)) for i in range(2)]
                bE = [S.buf("E%d" % i) for i in range(2)]
                rinv = es.enter_context(nc.sbuf_tensor("rinv_%d" % l, [128, 128], F32))
                otmp = es.enter_context(nc.sbuf_tensor("otmp_%d" % l, [128, 128], F32))
                bri = S.buf("rinv")
                nctx = CTX // 128
                unit = 0
                for h in range(NH):
                    hs = h % 2
                    LD(rp[hs][:], rpb_in[l, h].rearrange("p c q -> p (c q)"), brp[hs], [], [brp[hs]])
                    for s_ in range(5):
                        TT(B5[:, s_, :], rp[hs][:], rm[:, s_, :], ALU.add, [brp[hs], bsel], [bB5], eng=POOL)
                    LD(kbuf[:, 512:512 + TOK], zT[o_nak + h * 128:o_nak + (h + 1) * 128, 0:TOK], bkb, [bzT], [bkb])
                    LD(vbuf[:, 4:4 + NP, :], vtok[0:TOK, h * 128:(h + 1) * 128].rearrange("(c p) d -> p c d", p=128), bvb, [bvtok], [bvb])
                    LD(kcs[:], zT[o_nak + h * 128:o_nak + (h + 1) * 128, TOK:T], bkc, [bzT], [bkc])
                    LD(vcs[:], vtok[TOK:T, h * 128:(h + 1) * 128].rearrange("(c p) d -> p c d", p=128), bkc, [bvtok], [bkc])
                    S.dma_multi(SP, [(lambda hh, r=r, h=h: hh.dma_start(out=hk[:, r, :], in_=knh.rows(r, h * 128, 128))) for r in range(NCORE)],
                                bhk, reads=[knh.buf], writes=[bhk])
                    S.dma_multi(SP, [(lambda hh, r=r, h=h, c0=c0: hh.dma_start(out=hv[:, r, c0 // 128:(c0 + vnh.rpc) // 128, :],
                                                                                 in_=vnh.rows(r, c0, vnh.rpc)[:, h * 128:(h + 1) * 128]
                                                                                 .rearrange("(c p) d -> p c d", p=128)))
                                     for r in range(NCORE) for c0 in range(0, 512, vnh.rpc)],
                                bhv, reads=[vnh.buf], writes=[bhv])
                    for r in range(NCORE):
                        ka, kb_ = kbuf[:, 256:512], kbuf[:, 512 + TOK:512 + TOK + 256]
                        va, vb_ = vbuf[:, 2:4, :], vbuf[:, 4 + NP:6 + NP, :]
                        if r == 0:
                            TS(ka, hk[:, r, 256:512], sel[:, r:r + 1], None, ALU.mult, None, [bhk, bsel], [bkb])
                            TS(kb_, hk[:, r, 0:256], sel[:, 8 + r:9 + r], None, ALU.mult, None, [bhk, bsel], [bkb])
                            TS(va, hv[:, r, 2:4, :], sel[:, r:r + 1], None, ALU.mult, None, [bhv, bsel], [bvb])
                            TS(vb_, hv[:, r, 0:2, :], sel[:, 8 + r:9 + r], None, ALU.mult, None, [bhv, bsel], [bvb])
                        else:
                            STT(ka, hk[:, r, 256:512], sel[:, r:r + 1], ka, ALU.mult, ALU.add, [bhk, bsel], [bkb])
                            STT(kb_, hk[:, r, 0:256], sel[:, 8 + r:9 + r], kb_, ALU.mult, ALU.add, [bhk, bsel], [bkb])
                            STT(va, hv[:, r, 2:4, :], sel[:, r:r + 1], va, ALU.mult, ALU.add, [bhv, bsel], [bvb])
                            STT(vb_, hv[:, r, 0:2, :], sel[:, 8 + r:9 + r], vb_, ALU.mult, ALU.add, [bhv, bsel], [bvb])
                    LD(qT[:], zT[o_naq + h * 128:o_naq + (h + 1) * 128, :], bq, [bzT], [bq])
                    LD(gg[:], zT[o_nag + h * 128:o_nag + (h + 1) * 128, :], bgg, [bzT], [bgg])
                    AC(gsil[:], gg[:], AF.Silu, [bgg], [bgs])
                    qtiles = [(p, True) for p in range(NP)] + ([] if last else [(j, False) for j in range(nctx)])
                    for (p, local) in qtiles:
                        sb = 3 * (unit % 2)
                        es_ = unit % 2
                        unit += 1
                        bS0, bS1, bC = banks[sb], banks[sb + 1], banks[sb + 2]
                        q0 = p * 128 if local else TOK + p * 128
                        qap = qT[:, q0:q0 + 128]
                        if local:
                            slot = 0 if p == 0 else 1 if p == 1 else 3 if p == NP - 2 else 4 if p == NP - 1 else 2
                            for c in range(8):
                                bS = bS0 if c < 4 else bS1
                                MM(bS[:, (c % 4) * 128:(c % 4 + 1) * 128], kbuf[:, (p + c) * 128:(p + c + 1) * 128], qap, True, True,
                                   [bkb, bq], [bbank[sb + c // 4]])
                        for j in range(nctx):
                            MM(bC[:, j * 128:(j + 1) * 128], kcs[:, j * 128:(j + 1) * 128], qap, True, True, [bkc, bq], [bbank[sb + 2]])
                        if local:
                            for half in range(2):
                                STT(ein_[:, half * 512:(half + 1) * 512], banks[sb + half][:], sc_att, B5[:, slot, half * 512:(half + 1) * 512],
                                    ALU.mult, ALU.add, [bbank[sb + half], bB5], [bein])
                            AC(E[es_][:, 0:1024], ein_[:], AF.Exp, [bein], [bE[es_]])
                        AC(E[es_][:, 1024:1024 + CTX], bC[:, 0:CTX], AF.Exp, [bbank[sb + 2]], [bE[es_]], scale=sc_att)
                        chunks = ([(E[es_][:, c * 128:(c + 1) * 128], vbuf[:, p + c, :]) for c in range(8)] if local else []) + \
                                 [(E[es_][:, 1024 + j * 128:1024 + (j + 1) * 128], vcs[:, j, :]) for j in range(nctx)]
                        for i_, (eap, vap) in enumerate(chunks):
                            MM(bC[:, 256:384], onesb[:], eap, i_ == 0, i_ == len(chunks) - 1, [bsel, bE[es_]], [bbank[sb + 2]])
                        for i_, (eap, vap) in enumerate(chunks):
                            MM(bC[:, 384:512], vap, eap, i_ == 0, i_ == len(chunks) - 1, [bvb, bkc, bE[es_]], [bbank[sb + 2]])
                        RCP(rinv[:], bC[:, 256:384], [bbank[sb + 2]], [bri])
                        TT(otmp[:], bC[:, 384:512], rinv[:], ALU.mult, [bbank[sb + 2], bri], [bri])
                        TT(oh[hs][:, q0:q0 + 128], otmp[:], gsil[:, q0:q0 + 128], ALU.mult, [bri, bgs], [boh[hs]])
                    LD(gT[h * 128:(h + 1) * 128, 0:TE], oh[hs][:, 0:TE], boh[hs], [boh[hs]], [bgT], q=POOL)
                S.emit_phase()
            if stop < 5:
                return nc

            lam_init = 0.8 - 0.6 * float(np.exp(-0.3 * l))
            with ExitStack() as es:
                onesb = es.enter_context(nc.sbuf_tensor("onesd_%d" % l, [128, 128], BF16))
                bcn = S.buf("dconst")
                MS(onesb[:], 1.0, [bcn])
                lv = es.enter_context(nc.sbuf_tensor("lv_%d" % l, [128, 4, 128], F32))
                lsc = es.enter_context(nc.sbuf_tensor("lsc_%d" % l, [128, 4], F32))
                gcol = es.enter_context(nc.sbuf_tensor("gcol_%d" % l, [128, 2], F32))
                for i_ in range(4):
                    LD(lv[:, i_, :], bass.AP(lam_in.tensor, (l * 4 + i_) * 128, [[0, 128], [1, 128]]), bcn, [], [bcn])
                LD(gcol[:], subg_in[l].rearrange("(c p) -> p c", p=128), bcn, [], [bcn], slow=True)
                TT(lv[:, 0, :], lv[:, 0, :], lv[:, 1, :], ALU.mult, [bcn], [bcn])
                TT(lv[:, 2, :], lv[:, 2, :], lv[:, 3, :], ALU.mult, [bcn], [bcn])
                S.op(DVE, lambda hh: hh.tensor_reduce(lsc[:, 0:1], lv[:, 0, :], AX.X, ALU.add), reads=[bcn], writes=[bcn])
                S.op(DVE, lambda hh: hh.tensor_reduce(lsc[:, 1:2], lv[:, 2, :], AX.X, ALU.add), reads=[bcn], writes=[bcn])
                AC(lsc[:, 0:2], lsc[:, 0:2], AF.Exp, [bcn], [bcn])
                TT(lsc[:, 2:3], lsc[:, 0:1], lsc[:, 1:2], ALU.subtract, [bcn], [bcn])
                TS(lsc[:, 2:3], lsc[:, 2:3], lam_init, -1.0, ALU.add, ALU.mult, [bcn], [bcn])
                TS(gcol[:], gcol[:], 1.0 - lam_init, None, ALU.mult, None, [bcn], [bcn])
                qT = es.enter_context(nc.sbuf_tensor("qTd_%d" % l, [128, 2, T], BF16))
                bq = S.buf("qTd")
                gg = es.enter_context(nc.sbuf_tensor("ggd_%d" % l, [128, 2, T], BF16))
                gsil = es.enter_context(nc.sbuf_tensor("gsd_%d" % l, [128, 2, T], F32))
                bgg, bgs = S.buf("ggd"), S.buf("gsd")
                kp = [es.enter_context(nc.sbuf_tensor("kp%d_%d" % (i, l), [128, 2, TOK], BF16)) for i in range(2)]
                vp = [es.enter_context(nc.sbuf_tensor("vp%d_%d" % (i, l), [128, TOK // 128, 256], BF16)) for i in range(2)]
                bkp = [S.buf("kp%d" % i) for i in range(2)]
                bvp = [S.buf("vp%d" % i) for i in range(2)]
                kc_ = es.enter_context(nc.sbuf_tensor("kcd_%d" % l, [128, 2, CTX], BF16))
                vc_ = es.enter_context(nc.sbuf_tensor("vcd_%d" % l, [128, CTX // 128, 256], BF16))
                bkc = S.buf("kcd")
                Et = [es.enter_context(nc.sbuf_tensor("Et%d_%d" % (i, l), [128, 512], BF16)) for i in range(3)]
                bEt = [S.buf("Et%d" % i) for i in range(3)]
                fr = es.enter_context(nc.sbuf_tensor("fr_%d" % l, [128, 2, 512], F32))
                fo = es.enter_context(nc.sbuf_tensor("fo_%d" % l, [128, 2, 512], F32))
                fu = es.enter_context(nc.sbuf_tensor("fu_%d" % l, [128, 512], F32))
                ft_ = es.enter_context(nc.sbuf_tensor("ft_%d" % l, [128, 512], F32))
                fsq = es.enter_context(nc.sbuf_tensor("fsq_%d" % l, [128, 512], F32))
                frn = es.enter_context(nc.sbuf_tensor("frn_%d" % l, [128, 512], F32))
                fg = [es.enter_context(nc.sbuf_tensor("fg%d_%d" % (i, l), [128, 512], BF16)) for i in range(2)]
                bfr, bfo, bfu, bft, bfsq, bfrn = S.buf("fr"), S.buf("fo"), S.buf("fu"), S.buf("ft"), S.buf("fsq"), S.buf("frn")
                bfg = [S.buf("fg%d" % i) for i in range(2)]
                pit = 0
                eit = 0
                git = 0
                sit = 0
                for hd in range(HD):
                    for half in range(2):
                        LD(qT[:, half, :], zT[o_dfq + (2 * hd + half) * 128:o_dfq + (2 * hd + half + 1) * 128, :], bq, [bzT], [bq])
                        LD(gg[:, half, :], zT[o_dfg + (2 * hd + half) * 128:o_dfg + (2 * hd + half + 1) * 128, :], bgg, [bzT], [bgg])
                        LD(kc_[:, half, :], zT[o_dfk + (2 * hd + half) * 128:o_dfk + (2 * hd + half + 1) * 128, TOK:T], bkc, [bzT], [bkc])
                    LD(vc_[:], vtok[TOK:T, NAW + hd * 256:NAW + (hd + 1) * 256].rearrange("(c p) d -> p c d", p=128), bkc, [bvtok], [bkc])
                    AC(gsil[:].rearrange("p a t -> p (a t)"), gg[:].rearrange("p a t -> p (a t)"), AF.Silu, [bgg], [bgs])
                    qcs = [(t0, tn, True) for (t0, tn) in tch] + ([] if last else [(TOK, CTX, False)])
                    for (t0, tn, local) in qcs:
                        pieces = (list(range(NCORE)) if local else []) + [-1]
                        first = True
                        for pi_, r in enumerate(pieces):
                            if r >= 0:
                                sl = pit % 2
                                pit += 1
                                S.dma_multi(SP, [(lambda hh, half=half, sl=sl, r=r, hd=hd: hh.dma_start(out=kp[sl][:, half, :],
                                                                                                       in_=kdf.rows(r, (2 * hd + half) * 128, 128)))
                                                 for half in range(2)], bkp[sl], reads=[kdf.buf], writes=[bkp[sl]])
                                S.dma_multi(SP, [(lambda hh, sl=sl, r=r, hd=hd, c0=c0: hh.dma_start(
                                    out=vp[sl][:, c0 // 128:(c0 + vdf.rpc) // 128, :],
                                    in_=vdf.rows(r, c0, vdf.rpc)[:, hd * 256:(hd + 1) * 256].rearrange("(c p) d -> p c d", p=128)))
                                    for c0 in range(0, TOK, vdf.rpc)], bvp[sl], reads=[vdf.buf], writes=[bvp[sl]])
                                nk = TOK // 128
                                kget = lambda half, kc, sl=sl: kp[sl][:, half, kc * 128:(kc + 1) * 128]
                                vget = lambda kc, dv, sl=sl: vp[sl][:, kc, dv * 128:(dv + 1) * 128]
                                rb = [bkp[sl], bvp[sl]]
                            else:
                                nk = CTX // 128
                                kget = lambda half, kc: kc_[:, half, kc * 128:(kc + 1) * 128]
                                vget = lambda kc, dv: vc_[:, kc, dv * 128:(dv + 1) * 128]
                                rb = [bkc, bkc]
                            for kc in range(nk):
                                lastu = (pi_ == len(pieces) - 1 and kc == nk - 1)
                                for half in range(2):
                                    sb = sit % 2
                                    sit += 1
                                    e_ = eit % 3
                                    eit += 1
                                    MM(banks[sb][:, 0:tn], kget(half, kc), qT[:, half, t0:t0 + tn], True, True, [rb[0], bq], [bbank[sb]])
                                    AC(Et[e_][:, 0:tn], banks[sb][:, 0:tn], AF.Exp, [bbank[sb]], [bEt[e_]], scale=sc_att)
                                    MM(banks[2 + half][:, 0:tn], onesb[:], Et[e_][:, 0:tn], first, lastu, [bcn, bEt[e_]], [bbank[2 + half]])
                                    for dv in range(2):
                                        MM(banks[4 + half * 2 + dv][:, 0:tn], vget(kc, dv), Et[e_][:, 0:tn], first, lastu, [rb[1], bEt[e_]],
                                           [bbank[4 + half * 2 + dv]])
                                first = False
                        for half in range(2):
                            RCP(fr[:, half, 0:tn], banks[2 + half][:, 0:tn], [bbank[2 + half]], [bfr])
                        for dv in range(2):
                            TT(fu[:, 0:tn], banks[4 + dv][:, 0:tn], fr[:, 0, 0:tn], ALU.mult, [bbank[4 + dv], bfr], [bfu])
                            TT(ft_[:, 0:tn], banks[6 + dv][:, 0:tn], fr[:, 1, 0:tn], ALU.mult, [bbank[6 + dv], bfr], [bft])
                            STT(fo[:, dv, 0:tn], ft_[:, 0:tn], lsc[:, 2:3], fu[:, 0:tn], ALU.mult, ALU.add, [bft, bfu, bcn], [bfo])
                            AC(fsq[:, 0:tn], fo[:, dv, 0:tn], AF.Square, [bfo], [bfsq])
                            MM(banks[0][:, 0:tn], ones32[:], fsq[:, 0:tn], dv == 0, dv == 1, [bconst, bfsq], [bbank[0]])
                        AC(frn[:, 0:tn], banks[0][:, 0:tn], AF.Sqrt, [bbank[0], bconst], [bfrn], scale=1.0 / 256.0, bias=eps6[:, 1:2])
                        RCP(frn[:, 0:tn], frn[:, 0:tn], [bfrn], [bfrn])
                        for dv in range(2):
                            g_ = git % 2
                            git += 1
                            TT(fo[:, dv, 0:tn], fo[:, dv, 0:tn], frn[:, 0:tn], ALU.mult, [bfo, bfrn], [bfo])
                            STT(fg[g_][:, 0:tn], fo[:, dv, 0:tn], gcol[:, dv:dv + 1], gsil[:, dv, t0:t0 + tn], ALU.mult, ALU.mult,
                                [bfo, bcn, bgs], [bfg[g_]])
                            LD(gT[NAW + hd * 256 + dv * 128:NAW + hd * 256 + (dv + 1) * 128, t0:t0 + tn], fg[g_][:, 0:tn], bfg[g_], [bfg[g_]], [bgT],
                               q=POOL)
                S.emit_phase()
            if stop < 6:
                return nc

            with ExitStack() as es:
                sel = es.enter_context(nc.sbuf_tensor("selc_%d" % l, [128, 16], F32))
                bsel = S.buf("selc")
                LD(sel[:], sel_in, bsel, [], [bsel])
                cw = es.enter_context(nc.sbuf_tensor("cw_%d" % l, [128, CT, 31], F32))
                cpar = es.enter_context(nc.sbuf_tensor("cpar_%d" % l, [128, 3, CT], F32))
                for ct in range(CT):
                    LD(cw[:, ct, :], convw_in[l][:, ct * 128:(ct + 1) * 128].rearrange("k p -> p k"), bsel, [], [bsel], slow=True)
                for i_, src in enumerate((convb_in, convg_in, convbt_in)):
                    LD(cpar[:, i_, :], src[l].rearrange("(c p) -> p c", p=128), bsel, [], [bsel], slow=True)
                hc = es.enter_context(nc.sbuf_tensor("hc_%d" % l, [128, 8, 32], BF16))
                bhc = S.buf("hc")
                seqs = [(0, TOK, True)] + ([] if last else [(TOK, CTX, False)])
                ub = [es.enter_context(nc.sbuf_tensor("ub%d_%d" % (i, l), [128, 30 + TOK], BF16)) for i in range(2)]
                bub = [S.buf("ub%d" % i) for i in range(2)]
                yT = es.enter_context(nc.sbuf_tensor("yT_%d" % l, [128, CT, TOK], F32))
                byT = S.buf("yT")
                sq = es.enter_context(nc.sbuf_tensor("csq_%d" % l, [128, 512], F32))
                mean = es.enter_context(nc.sbuf_tensor("cmean_%d" % l, [128, 512], F32))
                msq = es.enter_context(nc.sbuf_tensor("cmsq_%d" % l, [128, 512], F32))
                rstd = es.enter_context(nc.sbuf_tensor("crstd_%d" % l, [128, 512], F32))
                tn_ = es.enter_context(nc.sbuf_tensor("ctn_%d" % l, [128, 512], F32))
                gt = [es.enter_context(nc.sbuf_tensor("cgt%d_%d" % (i, l), [128, 512], BF16)) for i in range(2)]
                gs_ = es.enter_context(nc.sbuf_tensor("cgs_%d" % l, [128, 512], F32))
                og = [es.enter_context(nc.sbuf_tensor("cog%d_%d" % (i, l), [128, 512], BF16)) for i in range(2)]
                bsq, bmean, bmsq, brstd, btn, bgs_ = S.buf("csq"), S.buf("cmean"), S.buf("cmsq"), S.buf("crstd"), S.buf("ctn"), S.buf("cgs")
                bgt = [S.buf("cgt%d" % i) for i in range(2)]
                bog = [S.buf("cog%d" % i) for i in range(2)]
                uit = 0
                oit = 0
                for (c0, nt, halo) in seqs:
                    for ct in range(CT):
                        sl = uit % 2
                        uit += 1
                        LD(ub[sl][:, 15:15 + nt], uT[ct * 128:(ct + 1) * 128, c0:c0 + nt], bub[sl], [buT], [bub[sl]])
                        if halo:
                            S.dma_multi(SP, [(lambda hh, r=r, ct=ct: hh.dma_start(out=hc[:, r, :], in_=cvh.rows(r, ct * 128, 128))) for r in range(NCORE)],
                                        bhc, reads=[cvh.buf], writes=[bhc])
                            for r in range(NCORE):
                                ha, hb = ub[sl][:, 0:15], ub[sl][:, 15 + nt:30 + nt]
                                if r == 0:
                                    TS(ha, hc[:, r, 16:31], sel[:, r:r + 1], None, ALU.mult, None, [bhc, bsel], [bub[sl]])
                                    TS(hb, hc[:, r, 0:15], sel[:, 8 + r:9 + r], None, ALU.mult, None, [bhc, bsel], [bub[sl]])
                                else:
                                    STT(ha, hc[:, r, 16:31], sel[:, r:r + 1], ha, ALU.mult, ALU.add, [bhc, bsel], [bub[sl]])
                                    STT(hb, hc[:, r, 0:15], sel[:, 8 + r:9 + r], hb, ALU.mult, ALU.add, [bhc, bsel], [bub[sl]])
                        else:
                            MS(ub[sl][:, 0:15], 0.0, [bub[sl]])
                            MS(ub[sl][:, 15 + nt:30 + nt], 0.0, [bub[sl]])
                        TS(yT[:, ct, 0:nt], ub[sl][:, 0:nt], cw[:, ct, 0:1], cpar[:, 0, ct:ct + 1], ALU.mult, ALU.add, [bub[sl], bsel], [byT])
                        for k in range(1, 31):
                            STT(yT[:, ct, 0:nt], ub[sl][:, k:k + nt], cw[:, ct, k:k + 1], yT[:, ct, 0:nt], ALU.mult, ALU.add, [bub[sl], bsel], [byT])
                    for t0 in range(0, nt, 512):
                        tn = min(512, nt - t0)
                        for ct in range(CT):
                            MM(banks[0][:, 0:tn], ones32[:], yT[:, ct, t0:t0 + tn], ct == 0, ct == CT - 1, [bconst, byT], [bbank[0]])
                        for ct in range(CT):
                            AC(sq[:, 0:tn], yT[:, ct, t0:t0 + tn], AF.Square, [byT], [bsq])
                            MM(banks[1][:, 0:tn], ones32[:], sq[:, 0:tn], ct == 0, ct == CT - 1, [bconst, bsq], [bbank[1]])
                        AC(mean[:, 0:tn], banks[0][:, 0:tn], AF.Identity, [bbank[0]], [bmean], scale=1.0 / CVW)
                        TT(msq[:, 0:tn], mean[:, 0:tn], mean[:, 0:tn], ALU.mult, [bmean], [bmsq])
                        STT(rstd[:, 0:tn], banks[1][:, 0:tn], 1.0 / CVW, msq[:, 0:tn], ALU.mult, ALU.subtract, [bbank[1], bmsq], [brstd])
                        AC(rstd[:, 0:tn], rstd[:, 0:tn], AF.Sqrt, [brstd, bconst], [brstd], bias=eps6[:, 1:2])
                        RCP(rstd[:, 0:tn], rstd[:, 0:tn], [brstd], [brstd])
                        for ct in range(CT):
                            o_ = oit % 2
                            oit += 1
                            LD(gt[o_][:, 0:tn], zT[o_cvgate + ct * 128:o_cvgate + (ct + 1) * 128, c0 + t0:c0 + t0 + tn], bgt[o_], [bzT], [bgt[o_]])
                            AC(gs_[:, 0:tn], gt[o_][:, 0:tn], AF.Silu, [bgt[o_]], [bgs_])
                            TT(tn_[:, 0:tn], yT[:, ct, t0:t0 + tn], mean[:, 0:tn], ALU.subtract, [byT, bmean], [btn])
                            TT(tn_[:, 0:tn], tn_[:, 0:tn], rstd[:, 0:tn], ALU.mult, [btn, brstd], [btn])
                            AC(tn_[:, 0:tn], tn_[:, 0:tn], AF.Silu, [btn, bsel], [btn], scale=cpar[:, 1, ct:ct + 1], bias=cpar[:, 2, ct:ct + 1])
                            TT(og[o_][:, 0:tn], tn_[:, 0:tn], gs_[:, 0:tn], ALU.mult, [btn, bgs_], [bog[o_]])
                            LD(gT[NAW + DFW + ct * 128:NAW + DFW + (ct + 1) * 128, c0 + t0:c0 + t0 + tn], og[o_][:, 0:tn], bog[o_], [bog[o_]], [bgT], q=POOL)
                S.emit_phase()
            if stop < 7:
                return nc

            alpha = (2.0 * L) ** 0.25
            WK = NAW // 128
            TC = 256
            with ExitStack() as es:
                gch = es.enter_context(nc.sbuf_tensor("gch_%d" % l, [128, 3 * WK, TC], BF16))
                bgch = S.buf("gch")
                wps = es.enter_context(nc.sbuf_tensor("wps_%d" % l, [128, 3, WK, 256], BF16))
                bwps = S.buf("wps")
                mgt = [es.enter_context(nc.sbuf_tensor("mgt%d_%d" % (i, l), [128, 3, TC], BF16)) for i in range(2)]
                bmgt = [S.buf("mgt%d" % i) for i in range(2)]
                sg = es.enter_context(nc.sbuf_tensor("msg_%d" % l, [128, 3, TC], F32))
                bsg = S.buf("msg")
                ya = es.enter_context(nc.sbuf_tensor("mya_%d" % l, [128, TC], F32))
                yb = es.enter_context(nc.sbuf_tensor("myb_%d" % l, [128, TC], F32))
                bya, byb = S.buf("mya"), S.buf("myb")
                yT_ = es.enter_context(nc.sbuf_tensor("myT_%d" % l, [128, KC, TC], BF16))
                byT_ = S.buf("myT")
                wo = [es.enter_context(nc.sbuf_tensor("wo%d_%d" % (i, l), [128, KC, 256], BF16)) for i in range(2)]
                bwo = [S.buf("wo%d" % i) for i in range(2)]
                osb = es.enter_context(nc.sbuf_tensor("osb_%d" % l, [128, D], F32))
                bosb = S.buf("osb")
                xt_ = es.enter_context(nc.sbuf_tensor("pxt_%d" % l, [128, D], F32))
                bxt_ = S.buf("pxt")
                gbc = es.enter_context(nc.sbuf_tensor("gbc_%d" % l, [128, D], F32))
                pgb = es.enter_context(nc.sbuf_tensor("pgb_%d" % l, [128, 2, D], F32))
                bgbc, bpgb = S.buf("gbc"), S.buf("pgb")
                st2 = es.enter_context(nc.sbuf_tensor("st2_%d" % l, [128, 8, 6], F32))
                mv2 = es.enter_context(nc.sbuf_tensor("mv2_%d" % l, [128, 2], F32))
                bst2 = S.buf("st2")
                LD(pgb[:, 0, :], bass.AP(plg_in.tensor, l * D, [[0, 128], [1, D]]), bpgb, [], [bpgb])
                LD(pgb[:, 1, :], bass.AP(plb_in.tensor, l * D, [[0, 128], [1, D]]), bpgb, [], [bpgb])
                cur_v = -1
                woit = 0
                mit = 0
                gws = [gw_pna[l], gw_pdf[l], gw_pcv[l]]
                for t0 in range(0, TE, TC):
                    v = 0 if t0 < TOK else 1
                    if v != cur_v:
                        cur_v = v
                        for (ap, o, m) in mod_cols(v, l, 2 * D, D):
                            LD(gbc[:, o:o + m], bass.AP(ap.tensor, ap.offset, [[0, 128], [1, m]]), bgbc, [bmod], [bgbc])
                    for b_ in range(3):
                        LD(gch[:, b_ * WK:(b_ + 1) * WK, :], gT[b_ * NAW:(b_ + 1) * NAW, t0:t0 + TC].rearrange("(k p) t -> p k t", p=128),
                           bgch, [bgT], [bgch])
                    for fst in range(D // 256):
                        for b_ in range(3):
                            S.dma_multi(SP, [(lambda hh, b_=b_, r=r, fst=fst: hh.dma_start(out=wps[:, b_, r * gws[b_].kcl:(r + 1) * gws[b_].kcl, :],
                                                                                          in_=gws[b_].st_ap(fst)[:, r])) for r in range(NCORE)],
                                        bwps, reads=[gws[b_].buf], writes=[bwps])
                        for half in range(2):
                            ftile = fst * 2 + half
                            m_ = mit % 2
                            mit += 1
                            for b_ in range(3):
                                LD(mgt[m_][:, b_, :], zT[o_mg[b_] + ftile * 128:o_mg[b_] + (ftile + 1) * 128, t0:t0 + TC], bmgt[m_], [bzT], [bmgt[m_]])
                            AC(sg[:].rearrange("p a t -> p (a t)"), mgt[m_][:].rearrange("p a t -> p (a t)"), AF.Sigmoid, [bmgt[m_]], [bsg])
                            for b_ in range(3):
                                for k in range(WK):
                                    MM(banks[b_][:, 0:TC], wps[:, b_, k, half * 128:(half + 1) * 128], gch[:, b_ * WK + k, :], k == 0, k == WK - 1,
                                       [bwps, bgch], [bbank[b_]])
                            TT(ya[:], banks[0][:, 0:TC], sg[:, 0, :], ALU.mult, [bbank[0], bsg], [bya])
                            TT(yb[:], banks[1][:, 0:TC], sg[:, 1, :], ALU.mult, [bbank[1], bsg], [byb])
                            TT(ya[:], ya[:], yb[:], ALU.add, [bya, byb], [bya])
                            TT(yb[:], banks[2][:, 0:TC], sg[:, 2, :], ALU.mult, [bbank[2], bsg], [byb])
                            TT(yT_[:, ftile, :], ya[:], yb[:], ALU.add, [bya, byb], [byT_])
                    for tt in range(TC // 128):
                        tok0 = t0 + tt * 128
                        LD(xt_[:], xcur[tok0:tok0 + 128, :], bxt_, [bxcur], [bxt_])
                        for fst in range(D // 256):
                            w_ = woit % 2
                            woit += 1
                            bk = 3 + (woit % 4)
                            S.dma_multi(SP, [(lambda hh, w_=w_, r=r, fst=fst: hh.dma_start(out=wo[w_][:, r * gw_out[l].kcl:(r + 1) * gw_out[l].kcl, :],
                                                                                          in_=gw_out[l].st_ap(fst)[:, r])) for r in range(NCORE)],
                                        bwo[w_], reads=[gw_out[l].buf], writes=[bwo[w_]])
                            for k in range(KC):
                                MM(banks[bk][:, 0:256], yT_[:, k, tt * 128:(tt + 1) * 128], wo[w_][:, k, :], k == 0, k == KC - 1, [byT_, bwo[w_]], [bbank[bk]])
                            TT(osb[:, fst * 256:(fst + 1) * 256], banks[bk][:, 0:256], gbc[:, fst * 256:(fst + 1) * 256], ALU.mult, [bbank[bk], bgbc], [bosb])
                        STT(osb[:], xt_[:], alpha, osb[:], ALU.mult, ALU.add, [bxt_, bosb], [bosb])
                        nchs = (D + 511) // 512
                        for c in range(nchs):
                            S.op(DVE, lambda hh, c=c: hh.bn_stats(st2[:, c, :], osb[:, c * 512:(c + 1) * 512]), reads=[bosb], writes=[bst2])
                        S.op(DVE, lambda hh: hh.bn_aggr(mv2[:], st2[:, 0:nchs, :].rearrange("p c s -> p (c s)")), reads=[bst2], writes=[bst2])
                        AC(mv2[:, 1:2], mv2[:, 1:2], AF.Sqrt, [bst2, bconst], [bst2], bias=eps6[:, 0:1])
                        RCP(mv2[:, 1:2], mv2[:, 1:2], [bst2], [bst2])
                        TS(osb[:], osb[:], mv2[:, 0:1], mv2[:, 1:2], ALU.subtract, ALU.mult, [bosb, bst2], [bosb])
                        TT(osb[:], osb[:], pgb[:, 0, :], ALU.mult, [bosb, bpgb], [bosb])
                        TT(xt_[:], osb[:], pgb[:, 1, :], ALU.add, [bosb, bpgb], [bxt_])
                        LD(xcur[tok0:tok0 + 128, :], xt_[:], bxt_, [bxt_], [bxcur], q=POOL)
                S.emit_phase()

        if stop < 99:
            return nc
        for r0 in range(0, TOK, 128):
            S.dma(SP, lambda h, r0=r0: h.dma_start(out=out[r0:r0 + 128, :], in_=xcur[r0:r0 + 128, :]), castbuf, reads=[bxcur])
        S.emit_phase()
    return nc


NEG = -30000.0


def make_maps(cfg, x, c, ctx, c_ctx, w_ada, b_ada, w_in, b_in, na_rpb, diff_lq1, diff_lk1, diff_lq2, diff_lk2, diff_subln_g,
              conv_w, conv_b, conv_ln_g, conv_ln_b, w_proj_na, w_proj_diff, w_proj_conv, w_out, post_ln_g, post_ln_b):
    D, L, TOK, NAW, DFW, CVW = cfg.D, cfg.L, cfg.TOK, cfg.NAW, cfg.DFW, cfg.CVW
    f32 = np.float32
    NA3 = 3 * D // NCORE
    x2 = np.asarray(x, f32).reshape(cfg.SEQ, D)
    ctx2 = np.ascontiguousarray(np.asarray(ctx, f32).reshape(cfg.CTX, D))
    c2 = np.ascontiguousarray(np.stack([np.asarray(c, f32).reshape(D), np.asarray(c_ctx, f32).reshape(D)]))
    ident = np.eye(128, dtype=ml_dtypes.bfloat16)
    perm = np.zeros((128, 128), f32)
    perm[np.arange(128) ^ 1, np.arange(128)] = 1.0
    perm = perm.astype(ml_dtypes.bfloat16)
    ROWS = TOK // 64
    NP = ROWS // 2
    GROWS = cfg.SEQ // 64
    NH = NAW // 128
    b2 = np.arange(2)[:, None, None, None, None]
    kc = np.arange(64)[None, :, None, None, None]
    cc = np.arange(8)[None, None, :, None, None]
    aa = np.arange(2)[None, None, None, :, None]
    qc = np.arange(64)[None, None, None, None, :]
    dr = 2 * cc + b2 - aa - 1
    dc = kc - qc + 15
    cs = np.clip(qc - 8, 0, 48)
    ok = (dr >= 0) & (dr <= 14) & (kc >= cs) & (kc < cs + 16) & (dc >= 0) & (dc <= 30)
    ok = np.broadcast_to(ok, (2, 64, 8, 2, 64))
    dri = np.broadcast_to(np.clip(dr, 0, 14), ok.shape)
    dci = np.broadcast_to(np.clip(dc, 0, 30), ok.shape)
    rpb = np.asarray(na_rpb, f32)
    rpbT = np.where(ok[None, None], rpb[:, :, dri, dci], f32(NEG)).astype(f32).reshape(L, NH, 128, 8, 128)
    lamv = np.ascontiguousarray(np.stack([diff_lq1, diff_lk1, diff_lq2, diff_lk2], 1).astype(f32).reshape(-1))
    inv_freq = (10000.0 ** (-np.arange(32, dtype=f32) / f32(32))).astype(f32)
    maps = []
    for r in range(NCORE):
        t = np.arange(r * TOK, (r + 1) * TOK)
        row = (t // 64).astype(f32)
        col = (t % 64).astype(f32)
        ang = np.concatenate([row[:, None] * inv_freq, col[:, None] * inv_freq], -1).astype(f32)
        cosT = np.ascontiguousarray(np.repeat(np.cos(ang).astype(f32), 2, axis=1).T)
        sgn = np.where(np.arange(128) % 2 == 0, -1.0, 1.0).astype(f32)
        sinT = np.ascontiguousarray((np.repeat(np.sin(ang).astype(f32), 2, axis=1) * sgn).T)
        sel = np.zeros((128, 16), f32)
        if r > 0:
            sel[:, r - 1] = 1.0
        if r < NCORE - 1:
            sel[:, 8 + r + 1] = 1.0
        rowmask = np.zeros((5, 2, 64, 8, 2, 64), f32)
        for si, p in enumerate((0, 1, 2, NP - 2, NP - 1)):
            q_abs = r * ROWS + 2 * p + aa
            k_abs = r * ROWS + 2 * p + 2 * cc + b2 - 8
            r0 = np.clip(q_abs - 4, 0, GROWS - 8)
            valid = (k_abs >= r0) & (k_abs < r0 + 8)
            rowmask[si] = np.where(np.broadcast_to(valid, (2, 64, 8, 2, 64)), 0.0, NEG)
        rowmask = rowmask.reshape(5, 128, 8, 128)
        maps.append(dict(
            x=np.ascontiguousarray(x2[r * TOK:(r + 1) * TOK]), ctx=ctx2, c2=c2,
            w_ada=np.ascontiguousarray(w_ada[:, :, r * NA3:(r + 1) * NA3]),
            b_ada=np.ascontiguousarray(b_ada[:, r * NA3:(r + 1) * NA3]),
            w_in=np.ascontiguousarray(w_in[:, r * D // NCORE:(r + 1) * D // NCORE, :]),
            b_in=np.ascontiguousarray(b_in), ident=ident, perm=perm, ropecos=cosT, ropesin=sinT, sel=sel, rowmask=rowmask,
            rpbT=rpbT, lamv=lamv, subg=np.ascontiguousarray(diff_subln_g, f32), conv_w=np.ascontiguousarray(conv_w, f32),
            conv_b=np.ascontiguousarray(conv_b, f32), conv_g=np.ascontiguousarray(conv_ln_g, f32), conv_bt=np.ascontiguousarray(conv_ln_b, f32),
            post_g=np.ascontiguousarray(post_ln_g, f32).reshape(-1), post_b=np.ascontiguousarray(post_ln_b, f32).reshape(-1),
            w_pna=np.ascontiguousarray(w_proj_na[:, r * NAW // NCORE:(r + 1) * NAW // NCORE, :]),
            w_pdf=np.ascontiguousarray(w_proj_diff[:, r * DFW // NCORE:(r + 1) * DFW // NCORE, :]),
            w_pcv=np.ascontiguousarray(w_proj_conv[:, r * CVW // NCORE:(r + 1) * CVW // NCORE, :]),
            w_out=np.ascontiguousarray(w_out[:, r * D // NCORE:(r + 1) * D // NCORE, :])))
    return maps


def kernel(**inputs):
    cfg = Cfg()
    nc = build(cfg)
    maps = make_maps(cfg, **{k: np.asarray(v) for k, v in inputs.items()})
    res = run_bass_kernel_spmd(nc, maps, core_ids=list(range(NCORE)))
    out = np.concatenate([res.results[r]["out"] for r in range(NCORE)], axis=0)
    return out.reshape(1, cfg.SEQ, cfg.D).astype(np.float32)
```

```python
import numpy as np
import ml_dtypes
from concourse.bass_utils import run_bass_kernel_spmd
import numpy as np
from contextlib import ExitStack
import concourse.bass as bass
import concourse.mybir as mybir

F32 = mybir.dt.float32
BF16 = mybir.dt.bfloat16
ALU = mybir.AluOpType
AF = mybir.ActivationFunctionType
AX = mybir.AxisListType

PE, DVE, ACT, POOL, SP = "tensor", "vector", "scalar", "gpsimd", "sync"
ENGS = [PE, DVE, ACT, POOL, SP]


class SemE:
    __slots__ = ("h", "cnt")

    def __init__(self):
        self.h = None
        self.cnt = 0


class Buf:
    __slots__ = ("name", "w", "rs", "sem")

    def __init__(self, name):
        self.name = name
        self.w = None
        self.rs = []
        self.sem = None


class Ev:
    __slots__ = ("kind", "eng", "op", "sem", "val")

    def __init__(self, kind, eng=None, op=None, sem=None, val=0):
        self.kind, self.eng, self.op, self.sem, self.val = kind, eng, op, sem, val


class Op:
    __slots__ = ("eng", "emit", "deps", "inc", "ms", "ev", "kind", "sem")

    def __init__(self, eng, emit, kind="c"):
        self.eng, self.emit, self.kind = eng, emit, kind
        self.deps = []
        self.inc = False
        self.ms = 0
        self.ev = None
        self.sem = None


class Sched:
    def __init__(self, nc, same_sync=True):
        self.nc = nc
        self.es = ExitStack()
        self.ops = {e: [] for e in ENGS}
        self.same_sync = same_sync
        self.evs = []
        self.pend = {e: [] for e in ENGS}
        self.esem = {e: SemE() for e in ENGS}
        self.ccsem = {}
        self.free_sems = []
        self.phase_bufs = []
        self.seen = {e: {} for e in ENGS}
        self.nsem = 0

    def buf(self, name, persist=False):
        b = Buf(name)
        if not persist:
            self.phase_bufs.append(b)
        return b

    def _getsem(self, b):
        if b.sem is None:
            b.sem = self.free_sems.pop() if self.free_sems else SemE()
        return b.sem

    def _deps(self, op, reads, writes):
        deps = []
        for b in reads:
            if b.w is not None:
                deps.append(b.w)
        for b in writes:
            if b.w is not None:
                deps.append(b.w)
            deps.extend(b.rs)
        deps.extend(self.pend[op.eng])
        self.pend[op.eng] = []
        for d in deps:
            if d.kind == "c":
                if d.eng == op.eng and (d.eng == PE or not self.same_sync):
                    continue
                d.op.inc = True
            op.deps.append(d)

    def _fin(self, o, ev, reads, writes):
        o.ev = ev
        for b in reads:
            b.rs.append(ev)
        for b in writes:
            b.w = ev
            b.rs = []
        self.ops[o.eng].append(o)

    def op(self, eng, emit, reads=(), writes=()):
        o = Op(eng, emit)
        self._deps(o, reads, writes)
        self._fin(o, Ev("c", eng=eng, op=o), reads, writes)
        return o

    def dma(self, queue, emit, sembuf, reads=(), writes=()):
        o = Op(queue, emit, kind="d")
        self._deps(o, reads, writes)
        s = self._getsem(sembuf)
        s.cnt += 16
        o.sem = s
        ev = Ev("d", sem=s, val=s.cnt)
        self._fin(o, ev, reads, writes)
        self.evs.append(ev)
        return o

    def dma_multi(self, queue, emits, sembuf, reads=(), writes=()):
        s = self._getsem(sembuf)
        first = True
        for em in emits:
            o = Op(queue, em, kind="d")
            if first:
                self._deps(o, reads, writes)
                first = False
            s.cnt += 16
            o.sem = s
            self.ops[queue].append(o)
        ev = Ev("d", sem=s, val=s.cnt)
        for b in reads:
            b.rs.append(ev)
        for b in writes:
            b.w = ev
            b.rs = []
        self.evs.append(ev)

    def cc(self, emit, semname, reads=(), writes=()):
        o = Op(POOL, emit, kind="k")
        self._deps(o, reads, writes)
        s = self.ccsem.setdefault(semname, SemE())
        s.cnt += 1
        o.sem = s
        ev = Ev("k", sem=s, val=s.cnt)
        self._fin(o, ev, reads, writes)
        self.evs.append(ev)
        return o

    def barrier(self):
        evs = list(self.evs)
        self.evs = []
        for e in ENGS:
            for o in reversed(self.ops[e]):
                if o.kind == "c":
                    evs.append(o.ev)
                    break
        for e in ENGS:
            for ev in evs:
                if ev.kind == "c":
                    if ev.eng == e:
                        continue
                    ev.op.inc = True
                self.pend[e].append(ev)

    def _alloc(self, s):
        if s.h is None:
            s.h = self.es.enter_context(self.nc.semaphore("s%d" % self.nsem))
            self.nsem += 1
        return s.h

    def emit_phase(self):
        self.barrier()
        nc = self.nc
        for e in ENGS:
            c = self.esem[e].cnt
            for o in self.ops[e]:
                if o.kind == "c" and o.inc:
                    c += 1
                    o.ms = c
            self.esem[e].cnt = c
            self._alloc(self.esem[e])
        for e in ENGS:
            for o in self.ops[e]:
                if o.sem is not None:
                    self._alloc(o.sem)
                for d in o.deps:
                    if d.sem is not None:
                        self._alloc(d.sem)

        def waitfor(h, e, d):
            if d.kind == "c":
                sem, val = self.esem[d.eng], d.op.ms
            else:
                sem, val = d.sem, d.val
            if self.seen[e].get(id(sem), 0) >= val:
                return
            self.seen[e][id(sem)] = val
            h.wait_ge(sem.h, val)

        def run_engine(e, h):
            for o in self.ops[e]:
                for d in o.deps:
                    waitfor(h, e, d)
                ins = o.emit(h)
                if o.kind == "c":
                    if o.inc:
                        ins.then_inc(self.esem[e].h, 1)
                elif o.kind == "d":
                    ins.then_inc(o.sem.h, 16)
                else:
                    ins.then_inc(o.sem.h)
            for d in self.pend[e]:
                waitfor(h, e, d)
            self.pend[e] = []

        with nc.Block() as block:
            @block.tensor
            def _(h):
                run_engine(PE, h)

            @block.vector
            def _(h):
                run_engine(DVE, h)

            @block.scalar
            def _(h):
                run_engine(ACT, h)

            @block.gpsimd
            def _(h):
                run_engine(POOL, h)

            @block.sync
            def _(h):
                run_engine(SP, h)
        self.ops = {e: [] for e in ENGS}
        for b in self.phase_bufs:
            if b.sem is not None:
                self.free_sems.append(b.sem)
                b.sem = None
        self.phase_bufs = []


NCORE = 8


class Cfg:
    def __init__(self, D=4096, SEQ=16384, CTX=256, NAW=2048, DFW=2048, CVW=2048, L=2):
        self.D, self.SEQ, self.CTX, self.NAW, self.DFW, self.CVW, self.L = D, SEQ, CTX, NAW, DFW, CVW, L
        self.KC = D // 128
        self.TOK = SEQ // NCORE
        self.T = self.TOK + CTX
        segs = [("na_q", NAW), ("na_k", NAW), ("na_v", NAW), ("na_gate", NAW), ("df_q", DFW), ("df_k", DFW),
                ("df_v", DFW), ("df_gate", DFW), ("cv_val", CVW), ("cv_glu", CVW), ("cv_gate", CVW),
                ("merge_na", D), ("merge_df", D), ("merge_cv", D)]
        self.seg = {}
        o = 0
        for n, s in segs:
            self.seg[n] = (o, s)
            o += s
        self.NIN = o


class GW:
    def __init__(self, nc, S, name, K, N, src, probe=False):
        self.probe = probe
        self.K, self.N = K, N
        self.kcl = K // 128 // NCORE
        self.nst = N // 256
        piece = self.kcl * 128 * 256 * 2
        self.spc = max(1, (512 * 1024) // piece)
        while self.nst % self.spc:
            self.spc -= 1
        self.nch = self.nst // self.spc
        rows = self.spc * self.kcl * 128
        self.rows = rows
        self.snd = [nc.dram_tensor("%s_snd%d" % (name, i), [rows, 256], BF16, kind="Internal").ap() for i in range(self.nch)]
        self.g4 = [nc.dram_tensor("%s_g4_%d" % (name, i), [4 * rows, 256], BF16, kind="Internal").ap() for i in range(self.nch)]
        self.g8 = [nc.dram_tensor("%s_g8_%d" % (name, i), [8 * rows, 256], BF16, kind="Internal").ap() for i in range(self.nch)]
        self.buf = S.buf(name + "_g", persist=True)
        self.src = src
        self.name = name

    def gather(self, nc, S, castbuf):
        bs = S.buf(self.name + "_s")
        b4 = S.buf(self.name + "_4")
        for ci in range(self.nch):
            c0 = 0 if self.probe else ci * self.spc * 256
            src = self.src[:, c0:c0 + self.spc * 256].rearrange("r (s n) -> s r n", n=256)
            dst = self.snd[ci].rearrange("(s r) n -> s r n", s=self.spc)
            cb = castbuf[ci % len(castbuf)]
            S.dma(POOL, lambda h, d=dst, s=src: h.dma_start(out=d, in_=s), cb, writes=[bs, cb])
        for ci in range(self.nch):
            S.cc(lambda h, ci=ci: h.collective_compute("AllGather", ALU.bypass, replica_groups=[[0, 1, 2, 3], [4, 5, 6, 7]],
                                                       ins=[self.snd[ci]], outs=[self.g4[ci]]), "a", reads=[bs], writes=[b4])
        for ci in range(self.nch):
            S.cc(lambda h, ci=ci: h.collective_compute("AllGather", ALU.bypass, replica_groups=[[0, 4], [1, 5], [2, 6], [3, 7]],
                                                       ins=[self.g4[ci]], outs=[self.g8[ci]]), "b", reads=[b4], writes=[self.buf])

    def st_ap(self, st):
        ci, s = divmod(st, self.spc)
        v = self.g8[ci].rearrange("(r s k p) n -> s p r k n", r=NCORE, s=self.spc, k=self.kcl, p=128)
        return v[s]


class GA:
    def __init__(self, nc, S, name, R, C):
        rpc = max(1, min(R, (512 * 1024) // (C * 2)))
        while R % rpc:
            rpc -= 1
        self.rpc, self.nch, self.R, self.C, self.name = rpc, R // rpc, R, C, name
        self.snd = [nc.dram_tensor("%s_snd%d" % (name, i), [rpc, C], BF16, kind="Internal").ap() for i in range(self.nch)]
        self.g4 = [nc.dram_tensor("%s_g4_%d" % (name, i), [4 * rpc, C], BF16, kind="Internal").ap() for i in range(self.nch)]
        self.g8 = [nc.dram_tensor("%s_g8_%d" % (name, i), [8 * rpc, C], BF16, kind="Internal").ap() for i in range(self.nch)]
        self.bs = [S.buf("%s_s%d" % (name, i), persist=True) for i in range(self.nch)]
        self.b4 = S.buf(name + "_4", persist=True)
        self.buf = S.buf(name + "_g", persist=True)

    def snd_rows(self, r0, n):
        ci, o = divmod(r0, self.rpc)
        assert o + n <= self.rpc
        return self.snd[ci][o:o + n]

    def bsc(self, r0):
        return self.bs[r0 // self.rpc]

    def rows(self, rank, r0, n):
        ci, o = divmod(r0, self.rpc)
        assert o + n <= self.rpc
        return self.g8[ci][rank * self.rpc + o: rank * self.rpc + o + n]

    def gather(self, S):
        for ci in range(self.nch):
            S.cc(lambda h, ci=ci: h.collective_compute("AllGather", ALU.bypass, replica_groups=[[0, 1, 2, 3], [4, 5, 6, 7]],
                                                       ins=[self.snd[ci]], outs=[self.g4[ci]]), "a", reads=[self.bs[ci]], writes=[self.b4])
        for ci in range(self.nch):
            S.cc(lambda h, ci=ci: h.collective_compute("AllGather", ALU.bypass, replica_groups=[[0, 4], [1, 5], [2, 6], [3, 7]],
                                                       ins=[self.g4[ci]], outs=[self.g8[ci]]), "b", reads=[self.b4], writes=[self.buf])


def build(cfg, debug=None, stop=99, probe=False):
    nc = bass.Bass("TRN2", target_bir_lowering=False)
    D, KC, T, TOK, CTX, L, NIN = cfg.D, cfg.KC, cfg.T, cfg.TOK, cfg.CTX, cfg.L, cfg.NIN
    inp = {}

    def ein(name, shape, dt=F32):
        inp[name] = nc.dram_tensor(name, list(shape), dt, kind="ExternalInput").ap()
        return inp[name]

    x_in = ein("x", [TOK if not probe else 128, D])
    ctx_in = ein("ctx", [CTX, D])
    c2 = ein("c2", [2, D])
    w_ada = ein("w_ada", [L, D, 3 * D // NCORE if not probe else 512])
    b_ada = ein("b_ada", [L, 3 * D // NCORE])
    w_in = ein("w_in", [L, D // NCORE, NIN if not probe else 512])
    b_in = ein("b_in", [L, NIN])
    ident_in = ein("ident", [128, 128], BF16)
    NAW, DFW, CVW = cfg.NAW, cfg.DFW, cfg.CVW
    cos_in = ein("ropecos", [128, TOK])
    sin_in = ein("ropesin", [128, TOK])
    perm_in = ein("perm", [128, 128], BF16)
    sel_in = ein("sel", [128, 16])
    rowmask_in = ein("rowmask", [5, 128, 8, 128])
    rpb_in = ein("rpbT", [L, NAW // 128, 128, 8, 128])
    lam_in = ein("lamv", [L * 4 * 128])
    subg_in = ein("subg", [L, 256])
    convw_in = ein("conv_w", [L, 31, CVW])
    convb_in = ein("conv_b", [L, CVW])
    convg_in = ein("conv_g", [L, CVW])
    convbt_in = ein("conv_bt", [L, CVW])
    plg_in = ein("post_g", [L * D])
    plb_in = ein("post_b", [L * D])
    wpna_in = ein("w_pna", [L, NAW // NCORE, D])
    wpdf_in = ein("w_pdf", [L, DFW // NCORE, D])
    wpcv_in = ein("w_pcv", [L, CVW // NCORE, D])
    wout_in = ein("w_out", [L, D // NCORE, D])
    out = nc.dram_tensor("out", [TOK, D], F32, kind="ExternalOutput").ap()
    dbg = None
    if debug:
        dbg = nc.dram_tensor("dbg", list(debug), F32, kind="ExternalOutput").ap()

    S = Sched(nc)
    with S.es:
        es0 = S.es
        banks = [es0.enter_context(nc.psum_tensor("bank%d" % i, [128, 512], F32)) for i in range(8)]
        bbank = [S.buf("bank%d" % i, persist=True) for i in range(8)]

        def MM(out_, lhsT, rhs, st, sp, R, W):
            S.op(PE, lambda h: h.matmul(out_, lhsT, rhs, start=st, stop=sp), reads=R, writes=W)

        def AC(out_, in_, func, R, W, **kw):
            S.op(ACT, lambda h: h.activation(out=out_, in_=in_, func=func, **kw), reads=R, writes=W)

        def TT(out_, a, b, op, R, W, eng=DVE):
            S.op(eng, lambda h: h.tensor_tensor(out_, a, b, op), reads=R, writes=W)

        def TS(out_, a, s1, s2, op0, op1, R, W, eng=DVE):
            if op1 is None:
                S.op(eng, lambda h: h.tensor_scalar(out_, a, s1, None, op0), reads=R, writes=W)
            else:
                S.op(eng, lambda h: h.tensor_scalar(out_, a, s1, s2, op0, op1), reads=R, writes=W)

        def STT(out_, a, sc, b, op0, op1, R, W):
            S.op(DVE, lambda h: h.scalar_tensor_tensor(out_, a, sc, b, op0, op1), reads=R, writes=W)

        def LD(out_, in_, sb, R, W, q=SP, slow=False):
            S.dma(q, lambda h: h.dma_start(out=out_, in_=in_, allow_slow_non_contiguous=slow), sb, reads=R, writes=W)

        def RCP(out_, in_, R, W):
            S.op(DVE, lambda h: h.reciprocal(out_, in_), reads=R, writes=W)

        def MS(ap, val, W, eng=POOL):
            S.op(eng, lambda h: h.memset(ap, val), writes=W)

        ident = es0.enter_context(nc.sbuf_tensor("ident_sb", [128, 128], BF16))
        ones32 = es0.enter_context(nc.sbuf_tensor("ones32", [128, 128], F32))
        eps6 = es0.enter_context(nc.sbuf_tensor("eps6", [128, 2], F32))
        bconst = S.buf("const", persist=True)
        castbuf = S.buf("castsem", persist=True)
        castbufs = [S.buf("castsem%d" % i, persist=True) for i in range(4)]
        modsnd = nc.dram_tensor("modsnd", [2 * L, 3 * D // NCORE], F32, kind="Internal").ap()
        mod4 = nc.dram_tensor("mod4", [4 * 2 * L, 3 * D // NCORE], F32, kind="Internal").ap()
        mod8 = nc.dram_tensor("mod8", [8 * 2 * L, 3 * D // NCORE], F32, kind="Internal").ap()
        bmod = S.buf("mod8", persist=True)
        zT = nc.dram_tensor("zT", [NIN, T], BF16, kind="Internal").ap()
        bzT = S.buf("zT", persist=True)
        xcur = nc.dram_tensor("xcur", [T, D], F32, kind="Internal").ap()
        bxcur = S.buf("xcur", persist=True)
        vtok = nc.dram_tensor("vtok", [T, NAW + DFW], BF16, kind="Internal").ap()
        bvtok = S.buf("vtok", persist=True)
        gT = nc.dram_tensor("gT", [NAW + DFW + CVW, T], BF16, kind="Internal").ap()
        bgT = S.buf("gT", persist=True)
        uT = nc.dram_tensor("uT", [CVW, T], BF16, kind="Internal").ap()
        buT = S.buf("uT", persist=True)
        kdf = GA(nc, S, "kdf", DFW, TOK)
        vdf = GA(nc, S, "vdf", TOK, DFW)
        knh = GA(nc, S, "knh", NAW, 512)
        vnh = GA(nc, S, "vnh", 512, NAW)
        cvh = GA(nc, S, "cvh", CVW, 32)

        gw_in = [GW(nc, S, "win%d" % l, D, NIN, w_in[l], probe) for l in range(L)]
        gw_pna = [GW(nc, S, "wpna%d" % l, NAW, D, wpna_in[l]) for l in range(L)]
        gw_pdf = [GW(nc, S, "wpdf%d" % l, DFW, D, wpdf_in[l]) for l in range(L)]
        gw_pcv = [GW(nc, S, "wpcv%d" % l, CVW, D, wpcv_in[l]) for l in range(L)]
        gw_out = [GW(nc, S, "wout%d" % l, D, D, wout_in[l]) for l in range(L)]
        with ExitStack() as es:
            S.op(POOL, lambda h: h.memset(ones32[:], 1.0), writes=[bconst])
            S.op(POOL, lambda h: h.memset(eps6[:, 0:1], 1e-6), writes=[bconst])
            S.op(POOL, lambda h: h.memset(eps6[:, 1:2], 1e-5), writes=[bconst])
            S.dma(SP, lambda h: h.dma_start(out=ident[:], in_=ident_in), bconst, writes=[bconst])
            import os
            for l_ in range(L):
                for g_ in (gw_in[l_], gw_pna[l_], gw_pdf[l_], gw_pcv[l_], gw_out[l_]):
                    g_.gather(nc, S, castbufs)
            late_gathers = []
            for r0 in range(0, TOK, 128):
                S.dma(SP, lambda h, r0=r0: h.dma_start(out=xcur[r0:r0 + 128, :], in_=x_in[(0 if probe else r0):(0 if probe else r0) + 128, :]), castbuf, writes=[bxcur])
            S.dma(SP, lambda h: h.dma_start(out=xcur[TOK:T, :], in_=ctx_in), castbuf, writes=[bxcur])
            S.emit_phase()

        NA3 = 3 * D // NCORE
        if stop < 1:
            return nc
        with ExitStack() as es:
            cT = es.enter_context(nc.sbuf_tensor("cT", [128, KC, 2], F32))
            bcT = S.buf("cT")
            wa = [es.enter_context(nc.sbuf_tensor("wa%d" % i, [128, 8, 512], F32)) for i in range(2)]
            bwa = [S.buf("wa%d" % i) for i in range(2)]
            mo = es.enter_context(nc.sbuf_tensor("mo", [2, L, NA3], F32))
            bb = es.enter_context(nc.sbuf_tensor("bb", [2, L, NA3], F32))
            bmo = S.buf("mo")
            bbb = S.buf("bb")
            for v in range(2):
                S.dma(SP, lambda h, v=v: h.dma_start(out=cT[:, :, v], in_=c2[v].rearrange("(k p) -> p k", p=128), allow_slow_non_contiguous=True),
                      bcT, writes=[bcT])
            for v in range(2):
                S.dma(SP, lambda h, v=v: h.dma_start(out=bb[v:v + 1], in_=b_ada.rearrange("(o l) n -> o l n", o=1)), bbb, writes=[bbb])
            S.op(ACT, lambda h: h.activation(out=cT[:], in_=cT[:], func=AF.Silu), reads=[bcT], writes=[bcT])
            it = 0
            for l in range(L):
                for n0 in range(0, NA3, 512):
                    nn = min(512, NA3 - n0)
                    for k0 in range(0, KC, 8):
                        kk = min(8, KC - k0)
                        sl = it % 2
                        it += 1
                        S.dma(SP, lambda h, sl=sl, l=l, n0=n0, nn=nn, k0=k0, kk=kk: h.dma_start(
                            out=wa[sl][:, 0:kk, 0:nn],
                            in_=w_ada[l, k0 * 128:(k0 + kk) * 128, (0 if probe else n0):(0 if probe else n0) + nn].rearrange("(k p) n -> p k n", p=128)),
                            bwa[sl], writes=[bwa[sl]])
                        for k in range(kk):
                            S.op(PE, lambda h, sl=sl, k=k, k0=k0, nn=nn: h.matmul(banks[0][0:2, 0:nn], cT[:, k0 + k, :], wa[sl][:, k, 0:nn],
                                                                                   start=(k0 + k == 0), stop=(k0 + k == KC - 1)),
                                 reads=[bcT, bwa[sl]], writes=[bbank[0]])
                    S.op(DVE, lambda h, l=l, n0=n0, nn=nn: h.tensor_tensor(mo[:, l, n0:n0 + nn], banks[0][0:2, 0:nn], bb[:, l, n0:n0 + nn], ALU.add),
                         reads=[bbank[0], bbb], writes=[bmo])
            bms = S.buf("modsnd")
            bm4 = S.buf("mod4")
            S.dma(SP, lambda h: h.dma_start(out=modsnd.rearrange("(v l) n -> v l n", v=2), in_=mo[:]), bmo, reads=[bmo], writes=[bms])
            S.cc(lambda h: h.collective_compute("AllGather", ALU.bypass, replica_groups=[[0, 1, 2, 3], [4, 5, 6, 7]], ins=[modsnd], outs=[mod4]),
                 "a", reads=[bms], writes=[bm4])
            S.cc(lambda h: h.collective_compute("AllGather", ALU.bypass, replica_groups=[[0, 4], [1, 5], [2, 6], [3, 7]], ins=[mod4], outs=[mod8]),
                 "b", reads=[bm4], writes=[bmod])
            S.emit_phase()

        if stop < 2:
            return nc
        def mod_cols(v, l, j0, n):
            res = []
            j = j0
            while j < j0 + n:
                r, o = divmod(j, NA3)
                m = min(NA3 - o, j0 + n - j)
                row = (r * 2 + v) * L + l
                res.append((mod8[row, o:o + m], j - j0, m))
                j += m
            return res

        for l in range(L):
            with ExitStack() as es:
                hT = es.enter_context(nc.sbuf_tensor("hT_%d" % l, [128, KC, T], BF16))
                bhT = S.buf("hT")
                sc1 = es.enter_context(nc.sbuf_tensor("sc1_%d" % l, [128, 2, KC], F32))
                sh1 = es.enter_context(nc.sbuf_tensor("sh1_%d" % l, [128, 2, KC], F32))
                bsc = S.buf("sc")
                for v in range(2):
                    for (ap, o, m) in mod_cols(v, l, D, D):
                        S.dma(SP, lambda h, ap=ap, o=o, m=m, v=v: h.dma_start(out=sc1[:, v, o // 128:(o + m) // 128],
                                                                               in_=ap.rearrange("(k p) -> p k", p=128), allow_slow_non_contiguous=True),
                              bsc, reads=[bmod], writes=[bsc])
                    for (ap, o, m) in mod_cols(v, l, 0, D):
                        S.dma(SP, lambda h, ap=ap, o=o, m=m, v=v: h.dma_start(out=sh1[:, v, o // 128:(o + m) // 128],
                                                                               in_=ap.rearrange("(k p) -> p k", p=128), allow_slow_non_contiguous=True),
                              bsc, reads=[bmod], writes=[bsc])
                S.op(DVE, lambda h: h.tensor_scalar(sc1[:], sc1[:], 1.0, None, ALU.add), reads=[bsc], writes=[bsc])
                es2 = ExitStack()
                xt = [es2.enter_context(nc.sbuf_tensor("xt%d_%d" % (i, l), [128, D], F32)) for i in range(2)]
                bxt = [S.buf("xt%d" % i) for i in range(2)]
                xn1 = es2.enter_context(nc.sbuf_tensor("xn_%d" % l, [128, D], BF16))
                xn = [xn1, xn1]
                bxn1 = S.buf("xn")
                bxn = [bxn1, bxn1]
                st = es2.enter_context(nc.sbuf_tensor("st_%d" % l, [128, 8, 6], F32))
                mv = es2.enter_context(nc.sbuf_tensor("mv_%d" % l, [128, 2], F32))
                bst = S.buf("st")
                NT = T // 128
                nch = (D + 511) // 512
                for t in range(NT):
                    sl = t % 2
                    v = 0 if t * 128 < TOK else 1
                    S.dma(SP, lambda h, t=t, sl=sl: h.dma_start(out=xt[sl][:], in_=xcur[t * 128:(t + 1) * 128, :]), bxt[sl],
                          reads=[bxcur], writes=[bxt[sl]])
                    for c in range(nch):
                        S.op(DVE, lambda h, c=c, sl=sl: h.bn_stats(st[:, c, :], xt[sl][:, c * 512:(c + 1) * 512]), reads=[bxt[sl]], writes=[bst])
                    S.op(DVE, lambda h: h.bn_aggr(mv[:], st[:, 0:nch, :].rearrange("p c s -> p (c s)")), reads=[bst], writes=[bst])
                    S.op(ACT, lambda h: h.activation(out=mv[:, 1:2], in_=mv[:, 1:2], func=AF.Sqrt, bias=eps6[:, 0:1]), reads=[bst, bconst], writes=[bst])
                    S.op(DVE, lambda h: h.reciprocal(mv[:, 1:2], mv[:, 1:2]), reads=[bst], writes=[bst])
                    S.op(DVE, lambda h, sl=sl: h.tensor_scalar(xn[sl][:], xt[sl][:], mv[:, 0:1], mv[:, 1:2], ALU.subtract, ALU.mult),
                         reads=[bxt[sl], bst], writes=[bxn[sl]])
                    for k in range(KC):
                        bk = 1 + k % 6
                        S.op(PE, lambda h, k=k, sl=sl, bk=bk: h.matmul(banks[bk][:, 0:128], xn[sl][:, k * 128:(k + 1) * 128], ident[:], start=True, stop=True),
                             reads=[bxn[sl], bconst], writes=[bbank[bk]])
                        S.op(ACT, lambda h, k=k, t=t, v=v, bk=bk: h.activation(out=hT[:, k, t * 128:(t + 1) * 128], in_=banks[bk][:, 0:128],
                                                                                func=AF.Identity, scale=sc1[:, v, k:k + 1], bias=sh1[:, v, k:k + 1]),
                             reads=[bbank[bk], bsc], writes=[bhT])
                S.emit_phase()
                es2.close()
                if stop < 3:
                    return nc
                wt = [es.enter_context(nc.sbuf_tensor("wt%d_%d" % (i, l), [128, KC, 256], BF16)) for i in range(2)]
                bwt = [S.buf("wt%d" % i) for i in range(2)]
                zt = [es.enter_context(nc.sbuf_tensor("zt%d_%d" % (i, l), [128, T], BF16)) for i in range(2)]
                bzt = [S.buf("zt%d" % i) for i in range(2)]
                bia = es.enter_context(nc.sbuf_tensor("bia_%d" % l, [128, NIN // 128], F32))
                bbia = S.buf("bia")
                S.dma(SP, lambda h, l=l: h.dma_start(out=bia[:], in_=b_in[l].rearrange("(n p) -> p n", p=128), allow_slow_non_contiguous=True),
                      bbia, writes=[bbia])
                g = gw_in[l]
                tch = [(t0, min(512, T - t0)) for t0 in range(0, T, 512)]
                cnt = 0
                for s_ in range(g.nst):
                    sl = s_ % 2
                    S.dma_multi(SP, [lambda h, s_=s_, sl=sl, r=r: h.dma_start(out=wt[sl][:, r * g.kcl:(r + 1) * g.kcl, :], in_=g.st_ap(s_)[:, r])
                                     for r in range(NCORE)], bwt[sl], reads=[g.buf], writes=[bwt[sl]])
                    for half in range(2):
                        nt = s_ * 2 + half
                        zs = nt % 2
                        for ci, (t0, tn) in enumerate(tch):
                            bk = 1 + (cnt % 6)
                            cnt += 1
                            for k in range(KC):
                                S.op(PE, lambda h, k=k, sl=sl, half=half, t0=t0, tn=tn, bk=bk: h.matmul(
                                    banks[bk][:, 0:tn], wt[sl][:, k, half * 128:(half + 1) * 128], hT[:, k, t0:t0 + tn], start=(k == 0), stop=(k == KC - 1)),
                                    reads=[bwt[sl], bhT], writes=[bbank[bk]])
                            S.op(ACT, lambda h, zs=zs, t0=t0, tn=tn, bk=bk, nt=nt: h.activation(out=zt[zs][:, t0:t0 + tn], in_=banks[bk][:, 0:tn],
                                                                                             func=AF.Identity, bias=bia[:, nt:nt + 1]),
                                 reads=[bbank[bk], bbia], writes=[bzt[zs]])
                        S.dma(POOL, lambda h, zs=zs, nt=nt: h.dma_start(out=zT[nt * 128:(nt + 1) * 128, :], in_=zt[zs][:]), bzt[zs],
                              reads=[bzt[zs]], writes=[bzT])
                bbcs = [es.enter_context(nc.sbuf_tensor("bbc%d_%d" % (i, l), [128, 256], F32)) for i in range(2)]
                bbbc = [S.buf("bbc%d" % i) for i in range(2)]
                vts = [es.enter_context(nc.sbuf_tensor("vts%d_%d" % (i, l), [128, 256], BF16)) for i in range(2)]
                bvts = [S.buf("vts%d" % i) for i in range(2)]
                vit = 0
                sidx = g.nst
                for (so, vo, sz) in ((cfg.seg["na_v"][0], 0, NAW), (cfg.seg["df_v"][0], NAW, DFW)):
                    for s2 in range(sz // 256):
                        st_ = (so + s2 * 256) // 256
                        sl = sidx % 2
                        sidx += 1
                        S.dma_multi(SP, [lambda h, st_=st_, sl=sl, r=r: h.dma_start(out=wt[sl][:, r * g.kcl:(r + 1) * g.kcl, :], in_=g.st_ap(st_)[:, r])
                                         for r in range(NCORE)], bwt[sl], reads=[g.buf], writes=[bwt[sl]])
                        LD(bbcs[sl][:], bass.AP(b_in.tensor, l * NIN + so + s2 * 256, [[0, 128], [1, 256]]), bbbc[sl], [], [bbbc[sl]])
                        for t in range(NT):
                            bk = 1 + (cnt % 6)
                            cnt += 1
                            v_ = vit % 2
                            vit += 1
                            for k in range(KC):
                                MM(banks[bk][:, 0:256], hT[:, k, t * 128:(t + 1) * 128], wt[sl][:, k, :], k == 0, k == KC - 1, [bhT, bwt[sl]], [bbank[bk]])
                            TT(vts[v_][:], banks[bk][:, 0:256], bbcs[sl][:], ALU.add, [bbank[bk], bbbc[sl]], [bvts[v_]])
                            LD(vtok[t * 128:(t + 1) * 128, vo + s2 * 256:vo + (s2 + 1) * 256], vts[v_][:], bvts[v_], [bvts[v_]], [bvtok], q=POOL)
                S.emit_phase()
            last = (l == L - 1)
            TE = TOK if last else T
            ROWS = TOK // 64
            NP = ROWS // 2
            NH = NAW // 128
            HD = DFW // 256
            CT = CVW // 128
            sc_att = 128.0 ** -0.5
            o_naq, o_nak, o_nag = cfg.seg["na_q"][0], cfg.seg["na_k"][0], cfg.seg["na_gate"][0]
            o_dfq, o_dfk, o_dfg = cfg.seg["df_q"][0], cfg.seg["df_k"][0], cfg.seg["df_gate"][0]
            o_cvv, o_cvg, o_cvgate = cfg.seg["cv_val"][0], cfg.seg["cv_glu"][0], cfg.seg["cv_gate"][0]
            o_mg = [cfg.seg["merge_na"][0], cfg.seg["merge_df"][0], cfg.seg["merge_cv"][0]]
            tch = [(t0, min(512, TOK - t0)) for t0 in range(0, TOK, 512)]

            with ExitStack() as es:
                cosT = es.enter_context(nc.sbuf_tensor("cosT_%d" % l, [128, TOK], F32))
                sinT = es.enter_context(nc.sbuf_tensor("sinT_%d" % l, [128, TOK], F32))
                perm = es.enter_context(nc.sbuf_tensor("perm_%d" % l, [128, 128], BF16))
                btab = S.buf("ropetab")
                LD(cosT[:], cos_in, btab, [], [btab])
                LD(sinT[:], sin_in, btab, [], [btab])
                LD(perm[:], perm_in, btab, [], [btab])
                xr = [es.enter_context(nc.sbuf_tensor("xr%d_%d" % (i, l), [128, TOK], BF16)) for i in range(2)]
                bxr = [S.buf("xr%d" % i) for i in range(2)]
                xo = [es.enter_context(nc.sbuf_tensor("xo%d_%d" % (i, l), [128, TOK], BF16)) for i in range(2)]
                bxo = [S.buf("xo%d" % i) for i in range(2)]
                t1 = es.enter_context(nc.sbuf_tensor("rt1_%d" % l, [128, 512], F32))
                t2 = es.enter_context(nc.sbuf_tensor("rt2_%d" % l, [128, 512], F32))
                bt1, bt2 = S.buf("rt1"), S.buf("rt2")
                it = 0
                for (so, isk) in ((o_dfq, False), (o_dfk, True)):
                    for hh in range(DFW // 128):
                        sl = it % 2
                        it += 1
                        LD(xr[sl][:], zT[so + hh * 128: so + (hh + 1) * 128, 0:TOK], bxr[sl], [bzT], [bxr[sl]])
                        for ci, (t0, tn) in enumerate(tch):
                            bk = ci % 2
                            MM(banks[bk][:, 0:tn], perm[:], xr[sl][:, t0:t0 + tn], True, True, [btab, bxr[sl]], [bbank[bk]])
                            TT(t1[:, 0:tn], xr[sl][:, t0:t0 + tn], cosT[:, t0:t0 + tn], ALU.mult, [bxr[sl], btab], [bt1])
                            TT(t2[:, 0:tn], banks[bk][:, 0:tn], sinT[:, t0:t0 + tn], ALU.mult, [bbank[bk], btab], [bt2])
                            TT(xo[sl][:, t0:t0 + tn], t1[:, 0:tn], t2[:, 0:tn], ALU.add, [bt1, bt2], [bxo[sl]])
                        if isk:
                            LD(kdf.snd_rows(hh * 128, 128), xo[sl][:], bxo[sl], [bxo[sl]], [kdf.bsc(hh * 128)], q=POOL)
                        else:
                            LD(zT[so + hh * 128: so + (hh + 1) * 128, 0:TOK], xo[sl][:], bxo[sl], [bxo[sl]], [bzT], q=POOL)
                for r0 in range(0, TOK, vdf.rpc):
                    LD(vdf.snd_rows(r0, vdf.rpc), vtok[r0:r0 + vdf.rpc, NAW:NAW + DFW], castbuf, [bvtok], [vdf.bsc(r0)])
                for r0 in range(0, NAW, knh.rpc):
                    LD(knh.snd_rows(r0, knh.rpc)[:, 0:256], zT[o_nak + r0:o_nak + r0 + knh.rpc, 0:256], castbuf, [bzT], [knh.bsc(r0)])
                    LD(knh.snd_rows(r0, knh.rpc)[:, 256:512], zT[o_nak + r0:o_nak + r0 + knh.rpc, TOK - 256:TOK], castbuf, [bzT], [knh.bsc(r0)])
                for r0 in range(0, 512, vnh.rpc):
                    tk0 = r0 if r0 < 256 else TOK - 512 + r0
                    LD(vnh.snd_rows(r0, vnh.rpc), vtok[tk0:tk0 + vnh.rpc, 0:NAW], castbuf, [bvtok], [vnh.bsc(r0)])
                uv = [es.enter_context(nc.sbuf_tensor("uv%d_%d" % (i, l), [128, T], BF16)) for i in range(2)]
                ug = [es.enter_context(nc.sbuf_tensor("ug%d_%d" % (i, l), [128, T], BF16)) for i in range(2)]
                usg = es.enter_context(nc.sbuf_tensor("usg_%d" % l, [128, T], F32))
                uo = [es.enter_context(nc.sbuf_tensor("uo%d_%d" % (i, l), [128, T], BF16)) for i in range(2)]
                buv = [S.buf("uv%d" % i) for i in range(2)]
                bug = [S.buf("ug%d" % i) for i in range(2)]
                buo = [S.buf("uo%d" % i) for i in range(2)]
                busg = S.buf("usg")
                for ct in range(CT):
                    sl = ct % 2
                    LD(uv[sl][:], zT[o_cvv + ct * 128:o_cvv + (ct + 1) * 128, :], buv[sl], [bzT], [buv[sl]])
                    LD(ug[sl][:], zT[o_cvg + ct * 128:o_cvg + (ct + 1) * 128, :], bug[sl], [bzT], [bug[sl]])
                    AC(usg[:], ug[sl][:], AF.Sigmoid, [bug[sl]], [busg])
                    TT(uo[sl][:], uv[sl][:], usg[:], ALU.mult, [buv[sl], busg], [buo[sl]])
                    LD(uT[ct * 128:(ct + 1) * 128, :], uo[sl][:], buo[sl], [buo[sl]], [buT], q=POOL)
                    LD(cvh.snd_rows(ct * 128, 128)[:, 0:15], uo[sl][:, 0:15], buo[sl], [buo[sl]], [cvh.bsc(ct * 128)], q=POOL, slow=True)
                    LD(cvh.snd_rows(ct * 128, 128)[:, 16:31], uo[sl][:, TOK - 15:TOK], buo[sl], [buo[sl]], [cvh.bsc(ct * 128)], q=POOL, slow=True)
                for ga in (knh, vnh, cvh):
                    ga.gather(S)
                S.emit_phase()
            if stop < 4:
                return nc

            with ExitStack() as es:
                NB = NP + 8
                sel = es.enter_context(nc.sbuf_tensor("sel_%d" % l, [128, 16], F32))
                rm = es.enter_context(nc.sbuf_tensor("rm_%d" % l, [128, 5, 1024], F32))
                onesb = es.enter_context(nc.sbuf_tensor("onesb_%d" % l, [128, 128], BF16))
                bsel = S.buf("sel")
                LD(sel[:], sel_in, bsel, [], [bsel])
                LD(rm[:], rowmask_in.rearrange("s p c q -> p s (c q)"), bsel, [], [bsel])
                MS(onesb[:], 1.0, [bsel])
                for ga in (kdf, vdf):
                    ga.gather(S)
                rp = [es.enter_context(nc.sbuf_tensor("rp%d_%d" % (i, l), [128, 1024], F32)) for i in range(2)]
                brp = [S.buf("rp%d" % i) for i in range(2)]
                B5 = es.enter_context(nc.sbuf_tensor("B5_%d" % l, [128, 5, 1024], F32))
                bB5 = S.buf("B5")
                kbuf = es.enter_context(nc.sbuf_tensor("kbuf_%d" % l, [128, NB * 128], BF16))
                vbuf = es.enter_context(nc.sbuf_tensor("vbuf_%d" % l, [128, NB, 128], BF16))
                bkb, bvb = S.buf("kbuf"), S.buf("vbuf")
                MS(kbuf[:], 0.0, [bkb])
                MS(vbuf[:], 0.0, [bvb])
                hk = es.enter_context(nc.sbuf_tensor("hk_%d" % l, [128, 8, 512], BF16))
                hv = es.enter_context(nc.sbuf_tensor("hv_%d" % l, [128, 8, 4, 128], BF16))
                bhk, bhv = S.buf("hk"), S.buf("hv")
                kcs = es.enter_context(nc.sbuf_tensor("kcs_%d" % l, [128, CTX], BF16))
                vcs = es.enter_context(nc.sbuf_tensor("vcs_%d" % l, [128, CTX // 128, 128], BF16))
                bkc = S.buf("kcs")
                qT = es.enter_context(nc.sbuf_tensor("qTn_%d" % l, [128, T], BF16))
                gg = es.enter_context(nc.sbuf_tensor("ggn_%d" % l, [128, T], BF16))
                gsil = es.enter_context(nc.sbuf_tensor("gsn_%d" % l, [128, T], F32))
                bq, bgg, bgs = S.buf("qTn"), S.buf("ggn"), S.buf("gsn")
                oh = [es.enter_context(nc.sbuf_tensor("oh%d_%d" % (i, l), [128, T], BF16)) for i in range(2)]
                boh = [S.buf("oh%d" % i) for i in range(2)]
                ein_ = es.enter_context(nc.sbuf_tensor("ein_%d" % l, [128, 1024], F32))
                bein = S.buf("ein")
                E = [es.enter_context(nc.sbuf_tensor("E%d_%d" % (i, l), [128, 1024 + CTX], BF16)) for i in range(2)]
                bE = [S.buf("E%d" % i) for i in range(2)]
                rinv = es.enter_context(nc.sbuf_tensor("rinv_%d" % l, [128, 128], F32))
                otmp = es.enter_context(nc.sbuf_tensor("otmp_%d" % l, [128, 128], F32))
                bri = S.buf("rinv")
                nctx = CTX // 128
                unit = 0
                for h in range(NH):
                    hs = h % 2
                    LD(rp[hs][:], rpb_in[l, h].rearrange("p c q -> p (c q)"), brp[hs], [], [brp[hs]])
                    for s_ in range(5):
                        TT(B5[:, s_, :], rp[hs][:], rm[:, s_, :], ALU.add, [brp[hs], bsel], [bB5], eng=POOL)
                    LD(kbuf[:, 512:512 + TOK], zT[o_nak + h * 128:o_nak + (h + 1) * 128, 0:TOK], bkb, [bzT], [bkb])
                    LD(vbuf[:, 4:4 + NP, :], vtok[0:TOK, h * 128:(h + 1) * 128].rearrange("(c p) d -> p c d", p=128), bvb, [bvtok], [bvb])
                    LD(kcs[:], zT[o_nak + h * 128:o_nak + (h + 1) * 128, TOK:T], bkc, [bzT], [bkc])
                    LD(vcs[:], vtok[TOK:T, h * 128:(h + 1) * 128].rearrange("(c p) d -> p c d", p=128), bkc, [bvtok], [bkc])
                    S.dma_multi(SP, [(lambda hh, r=r, h=h: hh.dma_start(out=hk[:, r, :], in_=knh.rows(r, h * 128, 128))) for r in range(NCORE)],
                                bhk, reads=[knh.buf], writes=[bhk])
                    S.dma_multi(SP, [(lambda hh, r=r, h=h, c0=c0: hh.dma_start(out=hv[:, r, c0 // 128:(c0 + vnh.rpc) // 128, :],
                                                                                 in_=vnh.rows(r, c0, vnh.rpc)[:, h * 128:(h + 1) * 128]
                                                                                 .rearrange("(c p) d -> p c d", p=128)))
                                     for r in range(NCORE) for c0 in range(0, 512, vnh.rpc)],
                                bhv, reads=[vnh.buf], writes=[bhv])
                    for r in range(NCORE):
                        ka, kb_ = kbuf[:, 256:512], kbuf[:, 512 + TOK:512 + TOK + 256]
                        va, vb_ = vbuf[:, 2:4, :], vbuf[:, 4 + NP:6 + NP, :]
                        if r == 0:
                            TS(ka, hk[:, r, 256:512], sel[:, r:r + 1], None, ALU.mult, None, [bhk, bsel], [bkb])
                            TS(kb_, hk[:, r, 0:256], sel[:, 8 + r:9 + r], None, ALU.mult, None, [bhk, bsel], [bkb])
                            TS(va, hv[:, r, 2:4, :], sel[:, r:r + 1], None, ALU.mult, None, [bhv, bsel], [bvb])
                            TS(vb_, hv[:, r, 0:2, :], sel[:, 8 + r:9 + r], None, ALU.mult, None, [bhv, bsel], [bvb])
                        else:
                            STT(ka, hk[:, r, 256:512], sel[:, r:r + 1], ka, ALU.mult, ALU.add, [bhk, bsel], [bkb])
                            STT(kb_, hk[:, r, 0:256], sel[:, 8 + r:9 + r], kb_, ALU.mult, ALU.add, [bhk, bsel], [bkb])
                            STT(va, hv[:, r, 2:4, :], sel[:, r:r + 1], va, ALU.mult, ALU.add, [bhv, bsel], [bvb])
                            STT(vb_, hv[:, r, 0:2, :], sel[:, 8 + r:9 + r], vb_, ALU.mult, ALU.add, [bhv, bsel], [bvb])
                    LD(qT[:], zT[o_naq + h * 128:o_naq + (h + 1) * 128, :], bq, [bzT], [bq])
                    LD(gg[:], zT[o_nag + h * 128:o_nag + (h + 1) * 128, :], bgg, [bzT], [bgg])
                    AC(gsil[:], gg[:], AF.Silu, [bgg], [bgs])
                    qtiles = [(p, True) for p in range(NP)] + ([] if last else [(j, False) for j in range(nctx)])
                    for (p, local) in qtiles:
                        sb = 3 * (unit % 2)
                        es_ = unit % 2
                        unit += 1
                        bS0, bS1, bC = banks[sb], banks[sb + 1], banks[sb + 2]
                        q0 = p * 128 if local else TOK + p * 128
                        qap = qT[:, q0:q0 + 128]
                        if local:
                            slot = 0 if p == 0 else 1 if p == 1 else 3 if p == NP - 2 else 4 if p == NP - 1 else 2
                            for c in range(8):
                                bS = bS0 if c < 4 else bS1
                                MM(bS[:, (c % 4) * 128:(c % 4 + 1) * 128], kbuf[:, (p + c) * 128:(p + c + 1) * 128], qap, True, True,
                                   [bkb, bq], [bbank[sb + c // 4]])
                        for j in range(nctx):
                            MM(bC[:, j * 128:(j + 1) * 128], kcs[:, j * 128:(j + 1) * 128], qap, True, True, [bkc, bq], [bbank[sb + 2]])
                        if local:
                            for half in range(2):
                                STT(ein_[:, half * 512:(half + 1) * 512], banks[sb + half][:], sc_att, B5[:, slot, half * 512:(half + 1) * 512],
                                    ALU.mult, ALU.add, [bbank[sb + half], bB5], [bein])
                            AC(E[es_][:, 0:1024], ein_[:], AF.Exp, [bein], [bE[es_]])
                        AC(E[es_][:, 1024:1024 + CTX], bC[:, 0:CTX], AF.Exp, [bbank[sb + 2]], [bE[es_]], scale=sc_att)
                        chunks = ([(E[es_][:, c * 128:(c + 1) * 128], vbuf[:, p + c, :]) for c in range(8)] if local else []) + \
                                 [(E[es_][:, 1024 + j * 128:1024 + (j + 1) * 128], vcs[:, j, :]) for j in range(nctx)]
                        for i_, (eap, vap) in enumerate(chunks):
                            MM(bC[:, 256:384], onesb[:], eap, i_ == 0, i_ == len(chunks) - 1, [bsel, bE[es_]], [bbank[sb + 2]])
                        for i_, (eap, vap) in enumerate(chunks):
                            MM(bC[:, 384:512], vap, eap, i_ == 0, i_ == len(chunks) - 1, [bvb, bkc, bE[es_]], [bbank[sb + 2]])
                        RCP(rinv[:], bC[:, 256:384], [bbank[sb + 2]], [bri])
                        TT(otmp[:], bC[:, 384:512], rinv[:], ALU.mult, [bbank[sb + 2], bri], [bri])
                        TT(oh[hs][:, q0:q0 + 128], otmp[:], gsil[:, q0:q0 + 128], ALU.mult, [bri, bgs], [boh[hs]])
                    LD(gT[h * 128:(h + 1) * 128, 0:TE], oh[hs][:, 0:TE], boh[hs], [boh[hs]], [bgT], q=SP)
                S.emit_phase()
            if stop < 5:
                return nc

            lam_init = 0.8 - 0.6 * float(np.exp(-0.3 * l))
            with ExitStack() as es:
                onesb = es.enter_context(nc.sbuf_tensor("onesd_%d" % l, [128, 128], BF16))
                bcn = S.buf("dconst")
                MS(onesb[:], 1.0, [bcn])
                if l == 0:
                    for g_ in late_gathers:
                        g_.gather(nc, S, castbufs)
                lv = es.enter_context(nc.sbuf_tensor("lv_%d" % l, [128, 4, 128], F32))
                lsc = es.enter_context(nc.sbuf_tensor("lsc_%d" % l, [128, 4], F32))
                gcol = es.enter_context(nc.sbuf_tensor("gcol_%d" % l, [128, 2], F32))
                for i_ in range(4):
                    LD(lv[:, i_, :], bass.AP(lam_in.tensor, (l * 4 + i_) * 128, [[0, 128], [1, 128]]), bcn, [], [bcn])
                LD(gcol[:], subg_in[l].rearrange("(c p) -> p c", p=128), bcn, [], [bcn], slow=True)
                TT(lv[:, 0, :], lv[:, 0, :], lv[:, 1, :], ALU.mult, [bcn], [bcn])
                TT(lv[:, 2, :], lv[:, 2, :], lv[:, 3, :], ALU.mult, [bcn], [bcn])
                S.op(DVE, lambda hh: hh.tensor_reduce(lsc[:, 0:1], lv[:, 0, :], AX.X, ALU.add), reads=[bcn], writes=[bcn])
                S.op(DVE, lambda hh: hh.tensor_reduce(lsc[:, 1:2], lv[:, 2, :], AX.X, ALU.add), reads=[bcn], writes=[bcn])
                AC(lsc[:, 0:2], lsc[:, 0:2], AF.Exp, [bcn], [bcn])
                TT(lsc[:, 2:3], lsc[:, 0:1], lsc[:, 1:2], ALU.subtract, [bcn], [bcn])
                TS(lsc[:, 2:3], lsc[:, 2:3], lam_init, -1.0, ALU.add, ALU.mult, [bcn], [bcn])
                TS(gcol[:], gcol[:], 1.0 - lam_init, None, ALU.mult, None, [bcn], [bcn])
                qT = es.enter_context(nc.sbuf_tensor("qTd_%d" % l, [128, 2, T], BF16))
                bq = S.buf("qTd")
                gg = es.enter_context(nc.sbuf_tensor("ggd_%d" % l, [128, 2, T], BF16))
                gsil = es.enter_context(nc.sbuf_tensor("gsd_%d" % l, [128, 2, T], F32))
                bgg, bgs = S.buf("ggd"), S.buf("gsd")
                kp = [es.enter_context(nc.sbuf_tensor("kp%d_%d" % (i, l), [128, 2, TOK], BF16)) for i in range(2)]
                vp = [es.enter_context(nc.sbuf_tensor("vp%d_%d" % (i, l), [128, TOK // 128, 256], BF16)) for i in range(2)]
                bkp = [S.buf("kp%d" % i) for i in range(2)]
                bvp = [S.buf("vp%d" % i) for i in range(2)]
                kc_ = es.enter_context(nc.sbuf_tensor("kcd_%d" % l, [128, 2, CTX], BF16))
                vc_ = es.enter_context(nc.sbuf_tensor("vcd_%d" % l, [128, CTX // 128, 256], BF16))
                bkc = S.buf("kcd")
                Et = [es.enter_context(nc.sbuf_tensor("Et%d_%d" % (i, l), [128, 512], BF16)) for i in range(4)]
                bEt = [S.buf("Et%d" % i) for i in range(4)]
                accs = [es.enter_context(nc.sbuf_tensor("acc%d_%d" % (i, l), [128, 512], F32)) for i in range(4)]
                baccs = [S.buf("acc%d" % i) for i in range(4)]
                fr = es.enter_context(nc.sbuf_tensor("fr_%d" % l, [128, 2, 512], F32))
                fo = es.enter_context(nc.sbuf_tensor("fo_%d" % l, [128, 2, 512], F32))
                fu = es.enter_context(nc.sbuf_tensor("fu_%d" % l, [128, 512], F32))
                ft_ = es.enter_context(nc.sbuf_tensor("ft_%d" % l, [128, 512], F32))
                fsq = es.enter_context(nc.sbuf_tensor("fsq_%d" % l, [128, 512], F32))
                frn = es.enter_context(nc.sbuf_tensor("frn_%d" % l, [128, 512], F32))
                fg = [es.enter_context(nc.sbuf_tensor("fg%d_%d" % (i, l), [128, 512], BF16)) for i in range(2)]
                bfr, bfo, bfu, bft, bfsq, bfrn = S.buf("fr"), S.buf("fo"), S.buf("fu"), S.buf("ft"), S.buf("fsq"), S.buf("frn")
                bfg = [S.buf("fg%d" % i) for i in range(2)]
                pit = 0
                eit = 0
                git = 0
                sit = 0
                for hd in range(HD):
                    for half in range(2):
                        LD(qT[:, half, :], zT[o_dfq + (2 * hd + half) * 128:o_dfq + (2 * hd + half + 1) * 128, :], bq, [bzT], [bq])
                        LD(gg[:, half, :], zT[o_dfg + (2 * hd + half) * 128:o_dfg + (2 * hd + half + 1) * 128, :], bgg, [bzT], [bgg])
                        LD(kc_[:, half, :], zT[o_dfk + (2 * hd + half) * 128:o_dfk + (2 * hd + half + 1) * 128, TOK:T], bkc, [bzT], [bkc])
                    LD(vc_[:], vtok[TOK:T, NAW + hd * 256:NAW + (hd + 1) * 256].rearrange("(c p) d -> p c d", p=128), bkc, [bvtok], [bkc])
                    AC(gsil[:].rearrange("p a t -> p (a t)"), gg[:].rearrange("p a t -> p (a t)"), AF.Silu, [bgg], [bgs])
                    qcs = [(t0, tn, True) for (t0, tn) in tch] + ([] if last else [(TOK, CTX, False)])
                    for (t0, tn, local) in qcs:
                        pieces = (list(range(NCORE)) if local else []) + [-1]
                        first = True
                        accn = [0, 0, 0, 0]
                        for pi_, r in enumerate(pieces):
                            if r >= 0:
                                sl = pit % 2
                                pit += 1
                                S.dma_multi(SP, [(lambda hh, half=half, sl=sl, r=r, hd=hd: hh.dma_start(out=kp[sl][:, half, :],
                                                                                                       in_=kdf.rows(r, (2 * hd + half) * 128, 128)))
                                                 for half in range(2)], bkp[sl], reads=[kdf.buf], writes=[bkp[sl]])
                                S.dma_multi(SP, [(lambda hh, sl=sl, r=r, hd=hd, c0=c0: hh.dma_start(
                                    out=vp[sl][:, c0 // 128:(c0 + vdf.rpc) // 128, :],
                                    in_=vdf.rows(r, c0, vdf.rpc)[:, hd * 256:(hd + 1) * 256].rearrange("(c p) d -> p c d", p=128)))
                                    for c0 in range(0, TOK, vdf.rpc)], bvp[sl], reads=[vdf.buf], writes=[bvp[sl]])
                                nk = TOK // 128
                                kget = lambda half, kc, sl=sl: kp[sl][:, half, kc * 128:(kc + 1) * 128]
                                vget = lambda kc, dv, sl=sl: vp[sl][:, kc, dv * 128:(dv + 1) * 128]
                                rb = [bkp[sl], bvp[sl]]
                            else:
                                nk = CTX // 128
                                kget = lambda half, kc: kc_[:, half, kc * 128:(kc + 1) * 128]
                                vget = lambda kc, dv: vc_[:, kc, dv * 128:(dv + 1) * 128]
                                rb = [bkc, bkc]
                            for kc in range(nk):
                                lastu = (pi_ == len(pieces) - 1 and kc == nk - 1)
                                for half in range(2):
                                    sb = sit % 2
                                    sit += 1
                                    e_ = eit % 4
                                    eit += 1
                                    MM(banks[sb][:, 0:tn], kget(half, kc), qT[:, half, t0:t0 + tn], True, True, [rb[0], bq], [bbank[sb]])
                                    AC(Et[e_][:, 0:tn], banks[sb][:, 0:tn], AF.Exp, [bbank[sb]], [bEt[e_]], scale=sc_att)
                                    a_ = half * 2 + (kc % 2)
                                    if accn[a_] == 0:
                                        S.op(DVE, lambda hh, a_=a_, e_=e_, tn=tn: hh.tensor_copy(accs[a_][:, 0:tn], Et[e_][:, 0:tn]),
                                             reads=[bEt[e_]], writes=[baccs[a_]])
                                    else:
                                        TT(accs[a_][:, 0:tn], accs[a_][:, 0:tn], Et[e_][:, 0:tn], ALU.add, [bEt[e_], baccs[a_]], [baccs[a_]])
                                    accn[a_] += 1
                                    for dv in range(2):
                                        MM(banks[4 + half * 2 + dv][:, 0:tn], vget(kc, dv), Et[e_][:, 0:tn], first, lastu, [rb[1], bEt[e_]],
                                           [bbank[4 + half * 2 + dv]])
                                first = False
                        for half in range(2):
                            used = [a_ for a_ in (half * 2, half * 2 + 1) if accn[a_] > 0]
                            for i_, a_ in enumerate(used):
                                MM(banks[2 + half][:, 0:tn], ones32[:], accs[a_][:, 0:tn], i_ == 0, i_ == len(used) - 1, [bconst, baccs[a_]],
                                   [bbank[2 + half]])
                            RCP(fr[:, half, 0:tn], banks[2 + half][:, 0:tn], [bbank[2 + half]], [bfr])
                        for dv in range(2):
                            TT(fu[:, 0:tn], banks[4 + dv][:, 0:tn], fr[:, 0, 0:tn], ALU.mult, [bbank[4 + dv], bfr], [bfu])
                            TT(ft_[:, 0:tn], banks[6 + dv][:, 0:tn], fr[:, 1, 0:tn], ALU.mult, [bbank[6 + dv], bfr], [bft])
                            STT(fo[:, dv, 0:tn], ft_[:, 0:tn], lsc[:, 2:3], fu[:, 0:tn], ALU.mult, ALU.add, [bft, bfu, bcn], [bfo])
                            AC(fsq[:, 0:tn], fo[:, dv, 0:tn], AF.Square, [bfo], [bfsq])
                            MM(banks[0][:, 0:tn], ones32[:], fsq[:, 0:tn], dv == 0, dv == 1, [bconst, bfsq], [bbank[0]])
                        AC(frn[:, 0:tn], banks[0][:, 0:tn], AF.Sqrt, [bbank[0], bconst], [bfrn], scale=1.0 / 256.0, bias=eps6[:, 1:2])
                        RCP(frn[:, 0:tn], frn[:, 0:tn], [bfrn], [bfrn])
                        for dv in range(2):
                            g_ = git % 2
                            git += 1
                            TT(fo[:, dv, 0:tn], fo[:, dv, 0:tn], frn[:, 0:tn], ALU.mult, [bfo, bfrn], [bfo])
                            STT(fg[g_][:, 0:tn], fo[:, dv, 0:tn], gcol[:, dv:dv + 1], gsil[:, dv, t0:t0 + tn], ALU.mult, ALU.mult,
                                [bfo, bcn, bgs], [bfg[g_]])
                            LD(gT[NAW + hd * 256 + dv * 128:NAW + hd * 256 + (dv + 1) * 128, t0:t0 + tn], fg[g_][:, 0:tn], bfg[g_], [bfg[g_]], [bgT],
                               q=SP)
                S.emit_phase()
            if stop < 6:
                return nc

            with ExitStack() as es:
                sel = es.enter_context(nc.sbuf_tensor("selc_%d" % l, [128, 16], F32))
                bsel = S.buf("selc")
                LD(sel[:], sel_in, bsel, [], [bsel])
                cw = es.enter_context(nc.sbuf_tensor("cw_%d" % l, [128, CT, 31], F32))
                cpar = es.enter_context(nc.sbuf_tensor("cpar_%d" % l, [128, 3, CT], F32))
                for ct in range(CT):
                    LD(cw[:, ct, :], convw_in[l][:, ct * 128:(ct + 1) * 128].rearrange("k p -> p k"), bsel, [], [bsel], slow=True)
                for i_, src in enumerate((convb_in, convg_in, convbt_in)):
                    LD(cpar[:, i_, :], src[l].rearrange("(c p) -> p c", p=128), bsel, [], [bsel], slow=True)
                hc = es.enter_context(nc.sbuf_tensor("hc_%d" % l, [128, 8, 32], BF16))
                bhc = S.buf("hc")
                seqs = [(0, TOK, True)] + ([] if last else [(TOK, CTX, False)])
                ub = [es.enter_context(nc.sbuf_tensor("ub%d_%d" % (i, l), [128, 30 + TOK], BF16)) for i in range(2)]
                bub = [S.buf("ub%d" % i) for i in range(2)]
                yT = es.enter_context(nc.sbuf_tensor("yT_%d" % l, [128, CT, TOK], F32))
                byT = S.buf("yT")
                sq = es.enter_context(nc.sbuf_tensor("csq_%d" % l, [128, 512], F32))
                mean = es.enter_context(nc.sbuf_tensor("cmean_%d" % l, [128, 512], F32))
                msq = es.enter_context(nc.sbuf_tensor("cmsq_%d" % l, [128, 512], F32))
                rstd = es.enter_context(nc.sbuf_tensor("crstd_%d" % l, [128, 512], F32))
                tn_ = es.enter_context(nc.sbuf_tensor("ctn_%d" % l, [128, 512], F32))
                gt = [es.enter_context(nc.sbuf_tensor("cgt%d_%d" % (i, l), [128, 512], BF16)) for i in range(2)]
                gs_ = es.enter_context(nc.sbuf_tensor("cgs_%d" % l, [128, 512], F32))
                og = [es.enter_context(nc.sbuf_tensor("cog%d_%d" % (i, l), [128, 512], BF16)) for i in range(2)]
                bsq, bmean, bmsq, brstd, btn, bgs_ = S.buf("csq"), S.buf("cmean"), S.buf("cmsq"), S.buf("crstd"), S.buf("ctn"), S.buf("cgs")
                bgt = [S.buf("cgt%d" % i) for i in range(2)]
                bog = [S.buf("cog%d" % i) for i in range(2)]
                uit = 0
                oit = 0
                for (c0, nt, halo) in seqs:
                    for ct in range(CT):
                        sl = uit % 2
                        uit += 1
                        LD(ub[sl][:, 15:15 + nt], uT[ct * 128:(ct + 1) * 128, c0:c0 + nt], bub[sl], [buT], [bub[sl]])
                        if halo:
                            S.dma_multi(SP, [(lambda hh, r=r, ct=ct: hh.dma_start(out=hc[:, r, :], in_=cvh.rows(r, ct * 128, 128))) for r in range(NCORE)],
                                        bhc, reads=[cvh.buf], writes=[bhc])
                            for r in range(NCORE):
                                ha, hb = ub[sl][:, 0:15], ub[sl][:, 15 + nt:30 + nt]
                                if r == 0:
                                    TS(ha, hc[:, r, 16:31], sel[:, r:r + 1], None, ALU.mult, None, [bhc, bsel], [bub[sl]])
                                    TS(hb, hc[:, r, 0:15], sel[:, 8 + r:9 + r], None, ALU.mult, None, [bhc, bsel], [bub[sl]])
                                else:
                                    STT(ha, hc[:, r, 16:31], sel[:, r:r + 1], ha, ALU.mult, ALU.add, [bhc, bsel], [bub[sl]])
                                    STT(hb, hc[:, r, 0:15], sel[:, 8 + r:9 + r], hb, ALU.mult, ALU.add, [bhc, bsel], [bub[sl]])
                        else:
                            MS(ub[sl][:, 0:15], 0.0, [bub[sl]])
                            MS(ub[sl][:, 15 + nt:30 + nt], 0.0, [bub[sl]])
                        TS(yT[:, ct, 0:nt], ub[sl][:, 0:nt], cw[:, ct, 0:1], cpar[:, 0, ct:ct + 1], ALU.mult, ALU.add, [bub[sl], bsel], [byT])
                        for k in range(1, 31):
                            STT(yT[:, ct, 0:nt], ub[sl][:, k:k + nt], cw[:, ct, k:k + 1], yT[:, ct, 0:nt], ALU.mult, ALU.add, [bub[sl], bsel], [byT])
                    for t0 in range(0, nt, 512):
                        tn = min(512, nt - t0)
                        for ct in range(CT):
                            MM(banks[0][:, 0:tn], ones32[:], yT[:, ct, t0:t0 + tn], ct == 0, ct == CT - 1, [bconst, byT], [bbank[0]])
                        for ct in range(CT):
                            AC(sq[:, 0:tn], yT[:, ct, t0:t0 + tn], AF.Square, [byT], [bsq])
                            MM(banks[1][:, 0:tn], ones32[:], sq[:, 0:tn], ct == 0, ct == CT - 1, [bconst, bsq], [bbank[1]])
                        AC(mean[:, 0:tn], banks[0][:, 0:tn], AF.Identity, [bbank[0]], [bmean], scale=1.0 / CVW)
                        TT(msq[:, 0:tn], mean[:, 0:tn], mean[:, 0:tn], ALU.mult, [bmean], [bmsq])
                        STT(rstd[:, 0:tn], banks[1][:, 0:tn], 1.0 / CVW, msq[:, 0:tn], ALU.mult, ALU.subtract, [bbank[1], bmsq], [brstd])
                        AC(rstd[:, 0:tn], rstd[:, 0:tn], AF.Sqrt, [brstd, bconst], [brstd], bias=eps6[:, 1:2])
                        RCP(rstd[:, 0:tn], rstd[:, 0:tn], [brstd], [brstd])
                        for ct in range(CT):
                            o_ = oit % 2
                            oit += 1
                            LD(gt[o_][:, 0:tn], zT[o_cvgate + ct * 128:o_cvgate + (ct + 1) * 128, c0 + t0:c0 + t0 + tn], bgt[o_], [bzT], [bgt[o_]])
                            AC(gs_[:, 0:tn], gt[o_][:, 0:tn], AF.Silu, [bgt[o_]], [bgs_])
                            TT(tn_[:, 0:tn], yT[:, ct, t0:t0 + tn], mean[:, 0:tn], ALU.subtract, [byT, bmean], [btn])
                            TT(tn_[:, 0:tn], tn_[:, 0:tn], rstd[:, 0:tn], ALU.mult, [btn, brstd], [btn])
                            AC(tn_[:, 0:tn], tn_[:, 0:tn], AF.Silu, [btn, bsel], [btn], scale=cpar[:, 1, ct:ct + 1], bias=cpar[:, 2, ct:ct + 1])
                            TT(og[o_][:, 0:tn], tn_[:, 0:tn], gs_[:, 0:tn], ALU.mult, [btn, bgs_], [bog[o_]])
                            LD(gT[NAW + DFW + ct * 128:NAW + DFW + (ct + 1) * 128, c0 + t0:c0 + t0 + tn], og[o_][:, 0:tn], bog[o_], [bog[o_]], [bgT], q=POOL)
                S.emit_phase()
            if stop < 7:
                return nc

            alpha = (2.0 * L) ** 0.25
            WK = NAW // 128
            TC = 256
            with ExitStack() as es:
                gch = es.enter_context(nc.sbuf_tensor("gch_%d" % l, [128, 3 * WK, TC], BF16))
                bgch = S.buf("gch")
                wps = es.enter_context(nc.sbuf_tensor("wps_%d" % l, [128, 3, WK, 256], BF16))
                bwps = S.buf("wps")
                mgt = [es.enter_context(nc.sbuf_tensor("mgt%d_%d" % (i, l), [128, 3, TC], BF16)) for i in range(2)]
                bmgt = [S.buf("mgt%d" % i) for i in range(2)]
                sg = es.enter_context(nc.sbuf_tensor("msg_%d" % l, [128, 3, TC], F32))
                bsg = S.buf("msg")
                ya = es.enter_context(nc.sbuf_tensor("mya_%d" % l, [128, TC], F32))
                yb = es.enter_context(nc.sbuf_tensor("myb_%d" % l, [128, TC], F32))
                bya, byb = S.buf("mya"), S.buf("myb")
                yT_ = es.enter_context(nc.sbuf_tensor("myT_%d" % l, [128, KC, TC], BF16))
                byT_ = S.buf("myT")
                wo = [es.enter_context(nc.sbuf_tensor("wo%d_%d" % (i, l), [128, KC, 256], BF16)) for i in range(2)]
                bwo = [S.buf("wo%d" % i) for i in range(2)]
                osb = es.enter_context(nc.sbuf_tensor("osb_%d" % l, [128, D], F32))
                bosb = S.buf("osb")
                xt_ = es.enter_context(nc.sbuf_tensor("pxt_%d" % l, [128, D], F32))
                bxt_ = S.buf("pxt")
                gbc = es.enter_context(nc.sbuf_tensor("gbc_%d" % l, [128, D], F32))
                pgb = es.enter_context(nc.sbuf_tensor("pgb_%d" % l, [128, 2, D], F32))
                bgbc, bpgb = S.buf("gbc"), S.buf("pgb")
                st2 = es.enter_context(nc.sbuf_tensor("st2_%d" % l, [128, 8, 6], F32))
                mv2 = es.enter_context(nc.sbuf_tensor("mv2_%d" % l, [128, 2], F32))
                bst2 = S.buf("st2")
                LD(pgb[:, 0, :], bass.AP(plg_in.tensor, l * D, [[0, 128], [1, D]]), bpgb, [], [bpgb])
                LD(pgb[:, 1, :], bass.AP(plb_in.tensor, l * D, [[0, 128], [1, D]]), bpgb, [], [bpgb])
                cur_v = -1
                woit = 0
                mit = 0
                gws = [gw_pna[l], gw_pdf[l], gw_pcv[l]]
                for t0 in range(0, TE, TC):
                    v = 0 if t0 < TOK else 1
                    if v != cur_v:
                        cur_v = v
                        for (ap, o, m) in mod_cols(v, l, 2 * D, D):
                            LD(gbc[:, o:o + m], bass.AP(ap.tensor, ap.offset, [[0, 128], [1, m]]), bgbc, [bmod], [bgbc])
                    for b_ in range(3):
                        LD(gch[:, b_ * WK:(b_ + 1) * WK, :], gT[b_ * NAW:(b_ + 1) * NAW, t0:t0 + TC].rearrange("(k p) t -> p k t", p=128),
                           bgch, [bgT], [bgch])
                    for fst in range(D // 256):
                        for b_ in range(3):
                            S.dma_multi(SP, [(lambda hh, b_=b_, r=r, fst=fst: hh.dma_start(out=wps[:, b_, r * gws[b_].kcl:(r + 1) * gws[b_].kcl, :],
                                                                                          in_=gws[b_].st_ap(fst)[:, r])) for r in range(NCORE)],
                                        bwps, reads=[gws[b_].buf], writes=[bwps])
                        for half in range(2):
                            ftile = fst * 2 + half
                            m_ = mit % 2
                            mit += 1
                            for b_ in range(3):
                                LD(mgt[m_][:, b_, :], zT[o_mg[b_] + ftile * 128:o_mg[b_] + (ftile + 1) * 128, t0:t0 + TC], bmgt[m_], [bzT], [bmgt[m_]])
                            AC(sg[:].rearrange("p a t -> p (a t)"), mgt[m_][:].rearrange("p a t -> p (a t)"), AF.Sigmoid, [bmgt[m_]], [bsg])
                            for b_ in range(3):
                                for k in range(WK):
                                    MM(banks[b_][:, 0:TC], wps[:, b_, k, half * 128:(half + 1) * 128], gch[:, b_ * WK + k, :], k == 0, k == WK - 1,
                                       [bwps, bgch], [bbank[b_]])
                            TT(ya[:], banks[0][:, 0:TC], sg[:, 0, :], ALU.mult, [bbank[0], bsg], [bya])
                            TT(yb[:], banks[1][:, 0:TC], sg[:, 1, :], ALU.mult, [bbank[1], bsg], [byb])
                            TT(ya[:], ya[:], yb[:], ALU.add, [bya, byb], [bya])
                            TT(yb[:], banks[2][:, 0:TC], sg[:, 2, :], ALU.mult, [bbank[2], bsg], [byb])
                            TT(yT_[:, ftile, :], ya[:], yb[:], ALU.add, [bya, byb], [byT_])
                    for tt in range(TC // 128):
                        tok0 = t0 + tt * 128
                        LD(xt_[:], xcur[tok0:tok0 + 128, :], bxt_, [bxcur], [bxt_])
                        for fst in range(D // 256):
                            w_ = woit % 2
                            woit += 1
                            bk = 3 + (woit % 4)
                            S.dma_multi(SP, [(lambda hh, w_=w_, r=r, fst=fst: hh.dma_start(out=wo[w_][:, r * gw_out[l].kcl:(r + 1) * gw_out[l].kcl, :],
                                                                                          in_=gw_out[l].st_ap(fst)[:, r])) for r in range(NCORE)],
                                        bwo[w_], reads=[gw_out[l].buf], writes=[bwo[w_]])
                            for k in range(KC):
                                MM(banks[bk][:, 0:256], yT_[:, k, tt * 128:(tt + 1) * 128], wo[w_][:, k, :], k == 0, k == KC - 1, [byT_, bwo[w_]], [bbank[bk]])
                            TT(osb[:, fst * 256:(fst + 1) * 256], banks[bk][:, 0:256], gbc[:, fst * 256:(fst + 1) * 256], ALU.mult, [bbank[bk], bgbc], [bosb])
                        STT(osb[:], xt_[:], alpha, osb[:], ALU.mult, ALU.add, [bxt_, bosb], [bosb])
                        nchs = (D + 511) // 512
                        for c in range(nchs):
                            S.op(DVE, lambda hh, c=c: hh.bn_stats(st2[:, c, :], osb[:, c * 512:(c + 1) * 512]), reads=[bosb], writes=[bst2])
                        S.op(DVE, lambda hh: hh.bn_aggr(mv2[:], st2[:, 0:nchs, :].rearrange("p c s -> p (c s)")), reads=[bst2], writes=[bst2])
                        AC(mv2[:, 1:2], mv2[:, 1:2], AF.Sqrt, [bst2, bconst], [bst2], bias=eps6[:, 0:1])
                        RCP(mv2[:, 1:2], mv2[:, 1:2], [bst2], [bst2])
                        TS(osb[:], osb[:], mv2[:, 0:1], mv2[:, 1:2], ALU.subtract, ALU.mult, [bosb, bst2], [bosb])
                        TT(osb[:], osb[:], pgb[:, 0, :], ALU.mult, [bosb, bpgb], [bosb])
                        TT(xt_[:], osb[:], pgb[:, 1, :], ALU.add, [bosb, bpgb], [bxt_])
                        LD(xcur[tok0:tok0 + 128, :], xt_[:], bxt_, [bxt_], [bxcur], q=POOL)
                S.emit_phase()

        if stop < 99:
            return nc
        for r0 in range(0, TOK, 128):
            S.dma(SP, lambda h, r0=r0: h.dma_start(out=out[r0:r0 + 128, :], in_=xcur[r0:r0 + 128, :]), castbuf, reads=[bxcur])
        S.emit_phase()
    return nc


NEG = -30000.0


def make_maps(cfg, x, c, ctx, c_ctx, w_ada, b_ada, w_in, b_in, na_rpb, diff_lq1, diff_lk1, diff_lq2, diff_lk2, diff_subln_g,
              conv_w, conv_b, conv_ln_g, conv_ln_b, w_proj_na, w_proj_diff, w_proj_conv, w_out, post_ln_g, post_ln_b):
    D, L, TOK, NAW, DFW, CVW = cfg.D, cfg.L, cfg.TOK, cfg.NAW, cfg.DFW, cfg.CVW
    f32 = np.float32
    NA3 = 3 * D // NCORE
    x2 = np.asarray(x, f32).reshape(cfg.SEQ, D)
    ctx2 = np.ascontiguousarray(np.asarray(ctx, f32).reshape(cfg.CTX, D))
    c2 = np.ascontiguousarray(np.stack([np.asarray(c, f32).reshape(D), np.asarray(c_ctx, f32).reshape(D)]))
    ident = np.eye(128, dtype=ml_dtypes.bfloat16)
    perm = np.zeros((128, 128), f32)
    perm[np.arange(128) ^ 1, np.arange(128)] = 1.0
    perm = perm.astype(ml_dtypes.bfloat16)
    ROWS = TOK // 64
    NP = ROWS // 2
    GROWS = cfg.SEQ // 64
    NH = NAW // 128
    b2 = np.arange(2)[:, None, None, None, None]
    kc = np.arange(64)[None, :, None, None, None]
    cc = np.arange(8)[None, None, :, None, None]
    aa = np.arange(2)[None, None, None, :, None]
    qc = np.arange(64)[None, None, None, None, :]
    dr = 2 * cc + b2 - aa - 1
    dc = kc - qc + 15
    cs = np.clip(qc - 8, 0, 48)
    ok = (dr >= 0) & (dr <= 14) & (kc >= cs) & (kc < cs + 16) & (dc >= 0) & (dc <= 30)
    ok = np.broadcast_to(ok, (2, 64, 8, 2, 64))
    dri = np.broadcast_to(np.clip(dr, 0, 14), ok.shape)
    dci = np.broadcast_to(np.clip(dc, 0, 30), ok.shape)
    rpb = np.asarray(na_rpb, f32)
    rpbT = np.where(ok[None, None], rpb[:, :, dri, dci], f32(NEG)).astype(f32).reshape(L, NH, 128, 8, 128)
    lamv = np.ascontiguousarray(np.stack([diff_lq1, diff_lk1, diff_lq2, diff_lk2], 1).astype(f32).reshape(-1))
    inv_freq = (10000.0 ** (-np.arange(32, dtype=f32) / f32(32))).astype(f32)
    maps = []
    for r in range(NCORE):
        t = np.arange(r * TOK, (r + 1) * TOK)
        row = (t // 64).astype(f32)
        col = (t % 64).astype(f32)
        ang = np.concatenate([row[:, None] * inv_freq, col[:, None] * inv_freq], -1).astype(f32)
        cosT = np.ascontiguousarray(np.repeat(np.cos(ang).astype(f32), 2, axis=1).T)
        sgn = np.where(np.arange(128) % 2 == 0, -1.0, 1.0).astype(f32)
        sinT = np.ascontiguousarray((np.repeat(np.sin(ang).astype(f32), 2, axis=1) * sgn).T)
        sel = np.zeros((128, 16), f32)
        if r > 0:
            sel[:, r - 1] = 1.0
        if r < NCORE - 1:
            sel[:, 8 + r + 1] = 1.0
        rowmask = np.zeros((5, 2, 64, 8, 2, 64), f32)
        for si, p in enumerate((0, 1, 2, NP - 2, NP - 1)):
            q_abs = r * ROWS + 2 * p + aa
            k_abs = r * ROWS + 2 * p + 2 * cc + b2 - 8
            r0 = np.clip(q_abs - 4, 0, GROWS - 8)
            valid = (k_abs >= r0) & (k_abs < r0 + 8)
            rowmask[si] = np.where(np.broadcast_to(valid, (2, 64, 8, 2, 64)), 0.0, NEG)
        rowmask = rowmask.reshape(5, 128, 8, 128)
        maps.append(dict(
            x=np.ascontiguousarray(x2[r * TOK:(r + 1) * TOK]), ctx=ctx2, c2=c2,
            w_ada=np.ascontiguousarray(w_ada[:, :, r * NA3:(r + 1) * NA3]),
            b_ada=np.ascontiguousarray(b_ada[:, r * NA3:(r + 1) * NA3]),
            w_in=np.ascontiguousarray(w_in[:, r * D // NCORE:(r + 1) * D // NCORE, :]),
            b_in=np.ascontiguousarray(b_in), ident=ident, perm=perm, ropecos=cosT, ropesin=sinT, sel=sel, rowmask=rowmask,
            rpbT=rpbT, lamv=lamv, subg=np.ascontiguousarray(diff_subln_g, f32), conv_w=np.ascontiguousarray(conv_w, f32),
            conv_b=np.ascontiguousarray(conv_b, f32), conv_g=np.ascontiguousarray(conv_ln_g, f32), conv_bt=np.ascontiguousarray(conv_ln_b, f32),
            post_g=np.ascontiguousarray(post_ln_g, f32).reshape(-1), post_b=np.ascontiguousarray(post_ln_b, f32).reshape(-1),
            w_pna=np.ascontiguousarray(w_proj_na[:, r * NAW // NCORE:(r + 1) * NAW // NCORE, :]),
            w_pdf=np.ascontiguousarray(w_proj_diff[:, r * DFW // NCORE:(r + 1) * DFW // NCORE, :]),
            w_pcv=np.ascontiguousarray(w_proj_conv[:, r * CVW // NCORE:(r + 1) * CVW // NCORE, :]),
            w_out=np.ascontiguousarray(w_out[:, r * D // NCORE:(r + 1) * D // NCORE, :])))
    return maps


def kernel(**inputs):
    cfg = Cfg()
    nc = build(cfg)
    maps = make_maps(cfg, **{k: np.asarray(v) for k, v in inputs.items()})
    res = run_bass_kernel_spmd(nc, maps, core_ids=list(range(NCORE)))
    out = np.concatenate([res.results[r]["out"] for r in range(NCORE)], axis=0)
    return out.reshape(1, cfg.SEQ, cfg.D).astype(np.float32)
```

```python
import numpy as np
import ml_dtypes
from concourse.bass_utils import run_bass_kernel_spmd
import numpy as np
from contextlib import ExitStack
import concourse.bass as bass
import concourse.mybir as mybir

F32 = mybir.dt.float32
BF16 = mybir.dt.bfloat16
ALU = mybir.AluOpType
AF = mybir.ActivationFunctionType
AX = mybir.AxisListType

PE, DVE, ACT, POOL, SP = "tensor", "vector", "scalar", "gpsimd", "sync"
ENGS = [PE, DVE, ACT, POOL, SP]


class SemE:
    __slots__ = ("h", "cnt")

    def __init__(self):
        self.h = None
        self.cnt = 0


class Buf:
    __slots__ = ("name", "w", "rs", "sem")

    def __init__(self, name):
        self.name = name
        self.w = None
        self.rs = []
        self.sem = None


class Ev:
    __slots__ = ("kind", "eng", "op", "sem", "val")

    def __init__(self, kind, eng=None, op=None, sem=None, val=0):
        self.kind, self.eng, self.op, self.sem, self.val = kind, eng, op, sem, val


class Op:
    __slots__ = ("eng", "emit", "deps", "inc", "ms", "ev", "kind", "sem")

    def __init__(self, eng, emit, kind="c"):
        self.eng, self.emit, self.kind = eng, emit, kind
        self.deps = []
        self.inc = False
        self.ms = 0
        self.ev = None
        self.sem = None


class Sched:
    def __init__(self, nc, same_sync=True):
        self.nc = nc
        self.es = ExitStack()
        self.ops = {e: [] for e in ENGS}
        self.same_sync = same_sync
        self.evs = []
        self.pend = {e: [] for e in ENGS}
        self.esem = {e: SemE() for e in ENGS}
        self.ccsem = {}
        self.free_sems = []
        self.phase_bufs = []
        self.seen = {e: {} for e in ENGS}
        self.nsem = 0

    def buf(self, name, persist=False):
        b = Buf(name)
        if not persist:
            self.phase_bufs.append(b)
        return b

    def _getsem(self, b):
        if b.sem is None:
            b.sem = self.free_sems.pop() if self.free_sems else SemE()
        return b.sem

    def _deps(self, op, reads, writes):
        deps = []
        for b in reads:
            if b.w is not None:
                deps.append(b.w)
        for b in writes:
            if b.w is not None:
                deps.append(b.w)
            deps.extend(b.rs)
        deps.extend(self.pend[op.eng])
        self.pend[op.eng] = []
        for d in deps:
            if d.kind == "c":
                if d.eng == op.eng and (d.eng == PE or not self.same_sync):
                    continue
                d.op.inc = True
            op.deps.append(d)

    def _fin(self, o, ev, reads, writes):
        o.ev = ev
        for b in reads:
            b.rs.append(ev)
        for b in writes:
            b.w = ev
            b.rs = []
        self.ops[o.eng].append(o)

    def op(self, eng, emit, reads=(), writes=()):
        o = Op(eng, emit)
        self._deps(o, reads, writes)
        self._fin(o, Ev("c", eng=eng, op=o), reads, writes)
        return o

    def dma(self, queue, emit, sembuf, reads=(), writes=()):
        o = Op(queue, emit, kind="d")
        self._deps(o, reads, writes)
        s = self._getsem(sembuf)
        s.cnt += 16
        o.sem = s
        ev = Ev("d", sem=s, val=s.cnt)
        self._fin(o, ev, reads, writes)
        self.evs.append(ev)
        return o

    def dma_multi(self, queue, emits, sembuf, reads=(), writes=()):
        s = self._getsem(sembuf)
        first = True
        for em in emits:
            o = Op(queue, em, kind="d")
            if first:
                self._deps(o, reads, writes)
                first = False
            s.cnt += 16
            o.sem = s
            self.ops[queue].append(o)
        ev = Ev("d", sem=s, val=s.cnt)
        for b in reads:
            b.rs.append(ev)
        for b in writes:
            b.w = ev
            b.rs = []
        self.evs.append(ev)

    def cc(self, emit, semname, reads=(), writes=()):
        o = Op(POOL, emit, kind="k")
        self._deps(o, reads, writes)
        s = self.ccsem.setdefault(semname, SemE())
        s.cnt += 1
        o.sem = s
        ev = Ev("k", sem=s, val=s.cnt)
        self._fin(o, ev, reads, writes)
        self.evs.append(ev)
        return o

    def barrier(self):
        evs = list(self.evs)
        self.evs = []
        for e in ENGS:
            for o in reversed(self.ops[e]):
                if o.kind == "c":
                    evs.append(o.ev)
                    break
        for e in ENGS:
            for ev in evs:
                if ev.kind == "c":
                    if ev.eng == e:
                        continue
                    ev.op.inc = True
                self.pend[e].append(ev)

    def _alloc(self, s):
        if s.h is None:
            s.h = self.es.enter_context(self.nc.semaphore("s%d" % self.nsem))
            self.nsem += 1
        return s.h

    def emit_phase(self):
        self.barrier()
        nc = self.nc
        for e in ENGS:
            c = self.esem[e].cnt
            for o in self.ops[e]:
                if o.kind == "c" and o.inc:
                    c += 1
                    o.ms = c
            self.esem[e].cnt = c
            self._alloc(self.esem[e])
        for e in ENGS:
            for o in self.ops[e]:
                if o.sem is not None:
                    self._alloc(o.sem)
                for d in o.deps:
                    if d.sem is not None:
                        self._alloc(d.sem)

        def waitfor(h, e, d):
            if d.kind == "c":
                sem, val = self.esem[d.eng], d.op.ms
            else:
                sem, val = d.sem, d.val
            if self.seen[e].get(id(sem), 0) >= val:
                return
            self.seen[e][id(sem)] = val
            h.wait_ge(sem.h, val)

        def run_engine(e, h):
            for o in self.ops[e]:
                for d in o.deps:
                    waitfor(h, e, d)
                ins = o.emit(h)
                if o.kind == "c":
                    if o.inc:
                        ins.then_inc(self.esem[e].h, 1)
                elif o.kind == "d":
                    ins.then_inc(o.sem.h, 16)
                else:
                    ins.then_inc(o.sem.h)
            for d in self.pend[e]:
                waitfor(h, e, d)
            self.pend[e] = []

        with nc.Block() as block:
            @block.tensor
            def _(h):
                run_engine(PE, h)

            @block.vector
            def _(h):
                run_engine(DVE, h)

            @block.scalar
            def _(h):
                run_engine(ACT, h)

            @block.gpsimd
            def _(h):
                run_engine(POOL, h)

            @block.sync
            def _(h):
                run_engine(SP, h)
        self.ops = {e: [] for e in ENGS}
        for b in self.phase_bufs:
            if b.sem is not None:
                self.free_sems.append(b.sem)
                b.sem = None
        self.phase_bufs = []


NCORE = 8


class Cfg:
    def __init__(self, D=4096, SEQ=16384, CTX=256, NAW=2048, DFW=2048, CVW=2048, L=2):
        self.D, self.SEQ, self.CTX, self.NAW, self.DFW, self.CVW, self.L = D, SEQ, CTX, NAW, DFW, CVW, L
        self.KC = D // 128
        self.TOK = SEQ // NCORE
        self.T = self.TOK + CTX
        segs = [("na_q", NAW), ("na_k", NAW), ("na_v", NAW), ("na_gate", NAW), ("df_q", DFW), ("df_k", DFW),
                ("df_v", DFW), ("df_gate", DFW), ("cv_val", CVW), ("cv_glu", CVW), ("cv_gate", CVW),
                ("merge_na", D), ("merge_df", D), ("merge_cv", D)]
        self.seg = {}
        o = 0
        for n, s in segs:
            self.seg[n] = (o, s)
            o += s
        self.NIN = o


class GW:
    def __init__(self, nc, S, name, K, N, src, probe=False):
        self.probe = probe
        self.K, self.N = K, N
        self.kcl = K // 128 // NCORE
        self.nst = N // 256
        piece = self.kcl * 128 * 256 * 2
        self.spc = max(1, (512 * 1024) // piece)
        while self.nst % self.spc:
            self.spc -= 1
        self.nch = self.nst // self.spc
        rows = self.spc * self.kcl * 128
        self.rows = rows
        self.snd = [nc.dram_tensor("%s_snd%d" % (name, i), [rows, 256], BF16, kind="Internal").ap() for i in range(self.nch)]
        self.g4 = [nc.dram_tensor("%s_g4_%d" % (name, i), [4 * rows, 256], BF16, kind="Internal").ap() for i in range(self.nch)]
        self.g8 = [nc.dram_tensor("%s_g8_%d" % (name, i), [8 * rows, 256], BF16, kind="Internal").ap() for i in range(self.nch)]
        self.buf = S.buf(name + "_g", persist=True)
        self.src = src
        self.name = name

    def gather(self, nc, S, castbuf):
        bs = S.buf(self.name + "_s")
        b4 = S.buf(self.name + "_4")
        for ci in range(self.nch):
            c0 = 0 if self.probe else ci * self.spc * 256
            src = self.src[:, c0:c0 + self.spc * 256].rearrange("r (s n) -> s r n", n=256)
            dst = self.snd[ci].rearrange("(s r) n -> s r n", s=self.spc)
            cb = castbuf[ci % len(castbuf)]
            S.dma(POOL, lambda h, d=dst, s=src: h.dma_start(out=d, in_=s), cb, writes=[bs, cb])
        for ci in range(self.nch):
            S.cc(lambda h, ci=ci: h.collective_compute("AllGather", ALU.bypass, replica_groups=[[0, 1, 2, 3], [4, 5, 6, 7]],
                                                       ins=[self.snd[ci]], outs=[self.g4[ci]]), "a", reads=[bs], writes=[b4])
        for ci in range(self.nch):
            S.cc(lambda h, ci=ci: h.collective_compute("AllGather", ALU.bypass, replica_groups=[[0, 4], [1, 5], [2, 6], [3, 7]],
                                                       ins=[self.g4[ci]], outs=[self.g8[ci]]), "b", reads=[b4], writes=[self.buf])

    def st_ap(self, st):
        ci, s = divmod(st, self.spc)
        v = self.g8[ci].rearrange("(r s k p) n -> s p r k n", r=NCORE, s=self.spc, k=self.kcl, p=128)
        return v[s]


class GA:
    def __init__(self, nc, S, name, R, C):
        rpc = max(1, min(R, (512 * 1024) // (C * 2)))
        while R % rpc:
            rpc -= 1
        self.rpc, self.nch, self.R, self.C, self.name = rpc, R // rpc, R, C, name
        self.snd = [nc.dram_tensor("%s_snd%d" % (name, i), [rpc, C], BF16, kind="Internal").ap() for i in range(self.nch)]
        self.g4 = [nc.dram_tensor("%s_g4_%d" % (name, i), [4 * rpc, C], BF16, kind="Internal").ap() for i in range(self.nch)]
        self.g8 = [nc.dram_tensor("%s_g8_%d" % (name, i), [8 * rpc, C], BF16, kind="Internal").ap() for i in range(self.nch)]
        self.bs = [S.buf("%s_s%d" % (name, i), persist=True) for i in range(self.nch)]
        self.b4 = S.buf(name + "_4", persist=True)
        self.buf = S.buf(name + "_g", persist=True)

    def snd_rows(self, r0, n):
        ci, o = divmod(r0, self.rpc)
        assert o + n <= self.rpc
        return self.snd[ci][o:o + n]

    def bsc(self, r0):
        return self.bs[r0 // self.rpc]

    def rows(self, rank, r0, n):
        ci, o = divmod(r0, self.rpc)
        assert o + n <= self.rpc
        return self.g8[ci][rank * self.rpc + o: rank * self.rpc + o + n]

    def gather(self, S):
        for ci in range(self.nch):
            S.cc(lambda h, ci=ci: h.collective_compute("AllGather", ALU.bypass, replica_groups=[[0, 1, 2, 3], [4, 5, 6, 7]],
                                                       ins=[self.snd[ci]], outs=[self.g4[ci]]), "a", reads=[self.bs[ci]], writes=[self.b4])
        for ci in range(self.nch):
            S.cc(lambda h, ci=ci: h.collective_compute("AllGather", ALU.bypass, replica_groups=[[0, 4], [1, 5], [2, 6], [3, 7]],
                                                       ins=[self.g4[ci]], outs=[self.g8[ci]]), "b", reads=[self.b4], writes=[self.buf])


def build(cfg, debug=None, stop=99, probe=False):
    nc = bass.Bass("TRN2", target_bir_lowering=False)
    D, KC, T, TOK, CTX, L, NIN = cfg.D, cfg.KC, cfg.T, cfg.TOK, cfg.CTX, cfg.L, cfg.NIN
    inp = {}

    def ein(name, shape, dt=F32):
        inp[name] = nc.dram_tensor(name, list(shape), dt, kind="ExternalInput").ap()
        return inp[name]

    x_in = ein("x", [TOK if not probe else 128, D])
    ctx_in = ein("ctx", [CTX, D])
    c2 = ein("c2", [2, D])
    w_ada = ein("w_ada", [L, D, 3 * D // NCORE if not probe else 512])
    b_ada = ein("b_ada", [L, 3 * D // NCORE])
    w_in = ein("w_in", [L, D // NCORE, NIN if not probe else 512])
    b_in = ein("b_in", [L, NIN])
    ident_in = ein("ident", [128, 128], BF16)
    NAW, DFW, CVW = cfg.NAW, cfg.DFW, cfg.CVW
    cos_in = ein("ropecos", [128, TOK])
    sin_in = ein("ropesin", [128, TOK])
    perm_in = ein("perm", [128, 128], BF16)
    sel_in = ein("sel", [128, 16])
    rowmask_in = ein("rowmask", [5, 128, 8, 128])
    rpb_in = ein("rpbT", [L, NAW // 128, 128, 8, 128])
    lam_in = ein("lamv", [L * 4 * 128])
    subg_in = ein("subg", [L, 256])
    convw_in = ein("conv_w", [L, 31, CVW])
    convb_in = ein("conv_b", [L, CVW])
    convg_in = ein("conv_g", [L, CVW])
    convbt_in = ein("conv_bt", [L, CVW])
    plg_in = ein("post_g", [L * D])
    plb_in = ein("post_b", [L * D])
    wpna_in = ein("w_pna", [L, NAW // NCORE, D])
    wpdf_in = ein("w_pdf", [L, DFW // NCORE, D])
    wpcv_in = ein("w_pcv", [L, CVW // NCORE, D])
    wout_in = ein("w_out", [L, D // NCORE, D])
    out = nc.dram_tensor("out", [TOK, D], F32, kind="ExternalOutput").ap()
    dbg = None
    if debug:
        dbg = nc.dram_tensor("dbg", list(debug), F32, kind="ExternalOutput").ap()

    S = Sched(nc)
    with S.es:
        es0 = S.es
        banks = [es0.enter_context(nc.psum_tensor("bank%d" % i, [128, 512], F32)) for i in range(8)]
        bbank = [S.buf("bank%d" % i, persist=True) for i in range(8)]

        def MM(out_, lhsT, rhs, st, sp, R, W):
            S.op(PE, lambda h: h.matmul(out_, lhsT, rhs, start=st, stop=sp), reads=R, writes=W)

        def AC(out_, in_, func, R, W, **kw):
            S.op(ACT, lambda h: h.activation(out=out_, in_=in_, func=func, **kw), reads=R, writes=W)

        def TT(out_, a, b, op, R, W, eng=DVE):
            S.op(eng, lambda h: h.tensor_tensor(out_, a, b, op), reads=R, writes=W)

        def TS(out_, a, s1, s2, op0, op1, R, W, eng=DVE):
            if op1 is None:
                S.op(eng, lambda h: h.tensor_scalar(out_, a, s1, None, op0), reads=R, writes=W)
            else:
                S.op(eng, lambda h: h.tensor_scalar(out_, a, s1, s2, op0, op1), reads=R, writes=W)

        def STT(out_, a, sc, b, op0, op1, R, W):
            S.op(DVE, lambda h: h.scalar_tensor_tensor(out_, a, sc, b, op0, op1), reads=R, writes=W)

        def LD(out_, in_, sb, R, W, q=SP, slow=False):
            S.dma(q, lambda h: h.dma_start(out=out_, in_=in_, allow_slow_non_contiguous=slow), sb, reads=R, writes=W)

        def RCP(out_, in_, R, W):
            S.op(DVE, lambda h: h.reciprocal(out_, in_), reads=R, writes=W)

        def MS(ap, val, W, eng=POOL):
            S.op(eng, lambda h: h.memset(ap, val), writes=W)

        ident = es0.enter_context(nc.sbuf_tensor("ident_sb", [128, 128], BF16))
        ones32 = es0.enter_context(nc.sbuf_tensor("ones32", [128, 128], F32))
        eps6 = es0.enter_context(nc.sbuf_tensor("eps6", [128, 2], F32))
        bconst = S.buf("const", persist=True)
        castbuf = S.buf("castsem", persist=True)
        castbufs = [S.buf("castsem%d" % i, persist=True) for i in range(4)]
        modsnd = nc.dram_tensor("modsnd", [2 * L, 3 * D // NCORE], F32, kind="Internal").ap()
        mod4 = nc.dram_tensor("mod4", [4 * 2 * L, 3 * D // NCORE], F32, kind="Internal").ap()
        mod8 = nc.dram_tensor("mod8", [8 * 2 * L, 3 * D // NCORE], F32, kind="Internal").ap()
        bmod = S.buf("mod8", persist=True)
        zT = nc.dram_tensor("zT", [NIN, T], BF16, kind="Internal").ap()
        bzT = S.buf("zT", persist=True)
        xcur = nc.dram_tensor("xcur", [T, D], F32, kind="Internal").ap()
        bxcur = S.buf("xcur", persist=True)
        vtok = nc.dram_tensor("vtok", [T, NAW + DFW], BF16, kind="Internal").ap()
        bvtok = S.buf("vtok", persist=True)
        gT = nc.dram_tensor("gT", [NAW + DFW + CVW, T], BF16, kind="Internal").ap()
        bgT = S.buf("gT", persist=True)
        uT = nc.dram_tensor("uT", [CVW, T], BF16, kind="Internal").ap()
        buT = S.buf("uT", persist=True)
        kdf = GA(nc, S, "kdf", DFW, TOK)
        vdf = GA(nc, S, "vdf", TOK, DFW)
        knh = GA(nc, S, "knh", NAW, 512)
        vnh = GA(nc, S, "vnh", 512, NAW)
        cvh = GA(nc, S, "cvh", CVW, 32)

        gw_in = [GW(nc, S, "win%d" % l, D, NIN, w_in[l], probe) for l in range(L)]
        gw_pna = [GW(nc, S, "wpna%d" % l, NAW, D, wpna_in[l]) for l in range(L)]
        gw_pdf = [GW(nc, S, "wpdf%d" % l, DFW, D, wpdf_in[l]) for l in range(L)]
        gw_pcv = [GW(nc, S, "wpcv%d" % l, CVW, D, wpcv_in[l]) for l in range(L)]
        gw_out = [GW(nc, S, "wout%d" % l, D, D, wout_in[l]) for l in range(L)]
        with ExitStack() as es:
            S.op(POOL, lambda h: h.memset(ones32[:], 1.0), writes=[bconst])
            S.op(POOL, lambda h: h.memset(eps6[:, 0:1], 1e-6), writes=[bconst])
            S.op(POOL, lambda h: h.memset(eps6[:, 1:2], 1e-5), writes=[bconst])
            S.dma(SP, lambda h: h.dma_start(out=ident[:], in_=ident_in), bconst, writes=[bconst])
            import os
            for l_ in range(L):
                for g_ in (gw_in[l_], gw_pna[l_], gw_pdf[l_], gw_pcv[l_], gw_out[l_]):
                    g_.gather(nc, S, castbufs)
            late_gathers = []
            for r0 in range(0, TOK, 128):
                S.dma(SP, lambda h, r0=r0: h.dma_start(out=xcur[r0:r0 + 128, :], in_=x_in[(0 if probe else r0):(0 if probe else r0) + 128, :]), castbuf, writes=[bxcur])
            S.dma(SP, lambda h: h.dma_start(out=xcur[TOK:T, :], in_=ctx_in), castbuf, writes=[bxcur])
            S.emit_phase()

        NA3 = 3 * D // NCORE
        if stop < 1:
            return nc
        with ExitStack() as es:
            cT = es.enter_context(nc.sbuf_tensor("cT", [128, KC, 2], F32))
            bcT = S.buf("cT")
            wa = [es.enter_context(nc.sbuf_tensor("wa%d" % i, [128, 8, 512], F32)) for i in range(2)]
            bwa = [S.buf("wa%d" % i) for i in range(2)]
            mo = es.enter_context(nc.sbuf_tensor("mo", [2, L, NA3], F32))
            bb = es.enter_context(nc.sbuf_tensor("bb", [2, L, NA3], F32))
            bmo = S.buf("mo")
            bbb = S.buf("bb")
            for v in range(2):
                S.dma(SP, lambda h, v=v: h.dma_start(out=cT[:, :, v], in_=c2[v].rearrange("(k p) -> p k", p=128), allow_slow_non_contiguous=True),
                      bcT, writes=[bcT])
            for v in range(2):
                S.dma(SP, lambda h, v=v: h.dma_start(out=bb[v:v + 1], in_=b_ada.rearrange("(o l) n -> o l n", o=1)), bbb, writes=[bbb])
            S.op(ACT, lambda h: h.activation(out=cT[:], in_=cT[:], func=AF.Silu), reads=[bcT], writes=[bcT])
            it = 0
            for l in range(L):
                for n0 in range(0, NA3, 512):
                    nn = min(512, NA3 - n0)
                    for k0 in range(0, KC, 8):
                        kk = min(8, KC - k0)
                        sl = it % 2
                        it += 1
                        S.dma(SP, lambda h, sl=sl, l=l, n0=n0, nn=nn, k0=k0, kk=kk: h.dma_start(
                            out=wa[sl][:, 0:kk, 0:nn],
                            in_=w_ada[l, k0 * 128:(k0 + kk) * 128, (0 if probe else n0):(0 if probe else n0) + nn].rearrange("(k p) n -> p k n", p=128)),
                            bwa[sl], writes=[bwa[sl]])
                        for k in range(kk):
                            S.op(PE, lambda h, sl=sl, k=k, k0=k0, nn=nn: h.matmul(banks[0][0:2, 0:nn], cT[:, k0 + k, :], wa[sl][:, k, 0:nn],
                                                                                   start=(k0 + k == 0), stop=(k0 + k == KC - 1)),
                                 reads=[bcT, bwa[sl]], writes=[bbank[0]])
                    S.op(DVE, lambda h, l=l, n0=n0, nn=nn: h.tensor_tensor(mo[:, l, n0:n0 + nn], banks[0][0:2, 0:nn], bb[:, l, n0:n0 + nn], ALU.add),
                         reads=[bbank[0], bbb], writes=[bmo])
            bms = S.buf("modsnd")
            bm4 = S.buf("mod4")
            S.dma(SP, lambda h: h.dma_start(out=modsnd.rearrange("(v l) n -> v l n", v=2), in_=mo[:]), bmo, reads=[bmo], writes=[bms])
            S.cc(lambda h: h.collective_compute("AllGather", ALU.bypass, replica_groups=[[0, 1, 2, 3], [4, 5, 6, 7]], ins=[modsnd], outs=[mod4]),
                 "a", reads=[bms], writes=[bm4])
            S.cc(lambda h: h.collective_compute("AllGather", ALU.bypass, replica_groups=[[0, 4], [1, 5], [2, 6], [3, 7]], ins=[mod4], outs=[mod8]),
                 "b", reads=[bm4], writes=[bmod])
            S.emit_phase()

        if stop < 2:
            return nc
        def mod_cols(v, l, j0, n):
            res = []
            j = j0
            while j < j0 + n:
                r, o = divmod(j, NA3)
                m = min(NA3 - o, j0 + n - j)
                row = (r * 2 + v) * L + l
                res.append((mod8[row, o:o + m], j - j0, m))
                j += m
            return res

        for l in range(L):
            with ExitStack() as es:
                hT = es.enter_context(nc.sbuf_tensor("hT_%d" % l, [128, KC, T], BF16))
                bhT = S.buf("hT")
                sc1 = es.enter_context(nc.sbuf_tensor("sc1_%d" % l, [128, 2, KC], F32))
                sh1 = es.enter_context(nc.sbuf_tensor("sh1_%d" % l, [128, 2, KC], F32))
                bsc = S.buf("sc")
                for v in range(2):
                    for (ap, o, m) in mod_cols(v, l, D, D):
                        S.dma(SP, lambda h, ap=ap, o=o, m=m, v=v: h.dma_start(out=sc1[:, v, o // 128:(o + m) // 128],
                                                                               in_=ap.rearrange("(k p) -> p k", p=128), allow_slow_non_contiguous=True),
                              bsc, reads=[bmod], writes=[bsc])
                    for (ap, o, m) in mod_cols(v, l, 0, D):
                        S.dma(SP, lambda h, ap=ap, o=o, m=m, v=v: h.dma_start(out=sh1[:, v, o // 128:(o + m) // 128],
                                                                               in_=ap.rearrange("(k p) -> p k", p=128), allow_slow_non_contiguous=True),
                              bsc, reads=[bmod], writes=[bsc])
                S.op(DVE, lambda h: h.tensor_scalar(sc1[:], sc1[:], 1.0, None, ALU.add), reads=[bsc], writes=[bsc])
                es2 = ExitStack()
                xt = [es2.enter_context(nc.sbuf_tensor("xt%d_%d" % (i, l), [128, D], F32)) for i in range(2)]
                bxt = [S.buf("xt%d" % i) for i in range(2)]
                xn1 = es2.enter_context(nc.sbuf_tensor("xn_%d" % l, [128, D], BF16))
                xn = [xn1, xn1]
                bxn1 = S.buf("xn")
                bxn = [bxn1, bxn1]
                st = es2.enter_context(nc.sbuf_tensor("st_%d" % l, [128, 8, 6], F32))
                mv = es2.enter_context(nc.sbuf_tensor("mv_%d" % l, [128, 2], F32))
                bst = S.buf("st")
                NT = T // 128
                nch = (D + 511) // 512
                for t in range(NT):
                    sl = t % 2
                    v = 0 if t * 128 < TOK else 1
                    S.dma(SP, lambda h, t=t, sl=sl: h.dma_start(out=xt[sl][:], in_=xcur[t * 128:(t + 1) * 128, :]), bxt[sl],
                          reads=[bxcur], writes=[bxt[sl]])
                    for c in range(nch):
                        S.op(DVE, lambda h, c=c, sl=sl: h.bn_stats(st[:, c, :], xt[sl][:, c * 512:(c + 1) * 512]), reads=[bxt[sl]], writes=[bst])
                    S.op(DVE, lambda h: h.bn_aggr(mv[:], st[:, 0:nch, :].rearrange("p c s -> p (c s)")), reads=[bst], writes=[bst])
                    S.op(ACT, lambda h: h.activation(out=mv[:, 1:2], in_=mv[:, 1:2], func=AF.Sqrt, bias=eps6[:, 0:1]), reads=[bst, bconst], writes=[bst])
                    S.op(DVE, lambda h: h.reciprocal(mv[:, 1:2], mv[:, 1:2]), reads=[bst], writes=[bst])
                    S.op(DVE, lambda h, sl=sl: h.tensor_scalar(xn[sl][:], xt[sl][:], mv[:, 0:1], mv[:, 1:2], ALU.subtract, ALU.mult),
                         reads=[bxt[sl], bst], writes=[bxn[sl]])
                    for k in range(KC):
                        bk = 1 + k % 6
                        S.op(PE, lambda h, k=k, sl=sl, bk=bk: h.matmul(banks[bk][:, 0:128], xn[sl][:, k * 128:(k + 1) * 128], ident[:], start=True, stop=True),
                             reads=[bxn[sl], bconst], writes=[bbank[bk]])
                        S.op(ACT, lambda h, k=k, t=t, v=v, bk=bk: h.activation(out=hT[:, k, t * 128:(t + 1) * 128], in_=banks[bk][:, 0:128],
                                                                                func=AF.Identity, scale=sc1[:, v, k:k + 1], bias=sh1[:, v, k:k + 1]),
                             reads=[bbank[bk], bsc], writes=[bhT])
                S.emit_phase()
                es2.close()
                if stop < 3:
                    return nc
                wt = [es.enter_context(nc.sbuf_tensor("wt%d_%d" % (i, l), [128, KC, 256], BF16)) for i in range(2)]
                bwt = [S.buf("wt%d" % i) for i in range(2)]
                zt = [es.enter_context(nc.sbuf_tensor("zt%d_%d" % (i, l), [128, T], BF16)) for i in range(2)]
                bzt = [S.buf("zt%d" % i) for i in range(2)]
                bia = es.enter_context(nc.sbuf_tensor("bia_%d" % l, [128, NIN // 128], F32))
                bbia = S.buf("bia")
                S.dma(SP, lambda h, l=l: h.dma_start(out=bia[:], in_=b_in[l].rearrange("(n p) -> p n", p=128), allow_slow_non_contiguous=True),
                      bbia, writes=[bbia])
                g = gw_in[l]
                tch = [(t0, min(512, T - t0)) for t0 in range(0, T, 512)]
                cnt = 0
                for s_ in range(g.nst):
                    sl = s_ % 2
                    S.dma_multi(SP, [lambda h, s_=s_, sl=sl, r=r: h.dma_start(out=wt[sl][:, r * g.kcl:(r + 1) * g.kcl, :], in_=g.st_ap(s_)[:, r])
                                     for r in range(NCORE)], bwt[sl], reads=[g.buf], writes=[bwt[sl]])
                    for half in range(2):
                        nt = s_ * 2 + half
                        zs = nt % 2
                        for ci, (t0, tn) in enumerate(tch):
                            bk = 1 + (cnt % 6)
                            cnt += 1
                            for k in range(KC):
                                S.op(PE, lambda h, k=k, sl=sl, half=half, t0=t0, tn=tn, bk=bk: h.matmul(
                                    banks[bk][:, 0:tn], wt[sl][:, k, half * 128:(half + 1) * 128], hT[:, k, t0:t0 + tn], start=(k == 0), stop=(k == KC - 1)),
                                    reads=[bwt[sl], bhT], writes=[bbank[bk]])
                            S.op(ACT, lambda h, zs=zs, t0=t0, tn=tn, bk=bk, nt=nt: h.activation(out=zt[zs][:, t0:t0 + tn], in_=banks[bk][:, 0:tn],
                                                                                             func=AF.Identity, bias=bia[:, nt:nt + 1]),
                                 reads=[bbank[bk], bbia], writes=[bzt[zs]])
                        S.dma(POOL, lambda h, zs=zs, nt=nt: h.dma_start(out=zT[nt * 128:(nt + 1) * 128, :], in_=zt[zs][:]), bzt[zs],
                              reads=[bzt[zs]], writes=[bzT])
                bbcs = [es.enter_context(nc.sbuf_tensor("bbc%d_%d" % (i, l), [128, 256], F32)) for i in range(2)]
                bbbc = [S.buf("bbc%d" % i) for i in range(2)]
                vts = [es.enter_context(nc.sbuf_tensor("vts%d_%d" % (i, l), [128, 256], BF16)) for i in range(2)]
                bvts = [S.buf("vts%d" % i) for i in range(2)]
                vit = 0
                sidx = g.nst
                for (so, vo, sz) in ((cfg.seg["na_v"][0], 0, NAW), (cfg.seg["df_v"][0], NAW, DFW)):
                    for s2 in range(sz // 256):
                        st_ = (so + s2 * 256) // 256
                        sl = sidx % 2
                        sidx += 1
                        S.dma_multi(SP, [lambda h, st_=st_, sl=sl, r=r: h.dma_start(out=wt[sl][:, r * g.kcl:(r + 1) * g.kcl, :], in_=g.st_ap(st_)[:, r])
                                         for r in range(NCORE)], bwt[sl], reads=[g.buf], writes=[bwt[sl]])
                        LD(bbcs[sl][:], bass.AP(b_in.tensor, l * NIN + so + s2 * 256, [[0, 128], [1, 256]]), bbbc[sl], [], [bbbc[sl]])
                        for t in range(NT):
                            bk = 1 + (cnt % 6)
                            cnt += 1
                            v_ = vit % 2
                            vit += 1
                            for k in range(KC):
                                MM(banks[bk][:, 0:256], hT[:, k, t * 128:(t + 1) * 128], wt[sl][:, k, :], k == 0, k == KC - 1, [bhT, bwt[sl]], [bbank[bk]])
                            TT(vts[v_][:], banks[bk][:, 0:256], bbcs[sl][:], ALU.add, [bbank[bk], bbbc[sl]], [bvts[v_]])
                            LD(vtok[t * 128:(t + 1) * 128, vo + s2 * 256:vo + (s2 + 1) * 256], vts[v_][:], bvts[v_], [bvts[v_]], [bvtok], q=POOL)
                S.emit_phase()
            last = (l == L - 1)
            TE = TOK if last else T
            ROWS = TOK // 64
            NP = ROWS // 2
            NH = NAW // 128
            HD = DFW // 256
            CT = CVW // 128
            sc_att = 128.0 ** -0.5
            o_naq, o_nak, o_nag = cfg.seg["na_q"][0], cfg.seg["na_k"][0], cfg.seg["na_gate"][0]
            o_dfq, o_dfk, o_dfg = cfg.seg["df_q"][0], cfg.seg["df_k"][0], cfg.seg["df_gate"][0]
            o_cvv, o_cvg, o_cvgate = cfg.seg["cv_val"][0], cfg.seg["cv_glu"][0], cfg.seg["cv_gate"][0]
            o_mg = [cfg.seg["merge_na"][0], cfg.seg["merge_df"][0], cfg.seg["merge_cv"][0]]
            tch = [(t0, min(512, TOK - t0)) for t0 in range(0, TOK, 512)]

            with ExitStack() as es:
                cosT = es.enter_context(nc.sbuf_tensor("cosT_%d" % l, [128, TOK], F32))
                sinT = es.enter_context(nc.sbuf_tensor("sinT_%d" % l, [128, TOK], F32))
                perm = es.enter_context(nc.sbuf_tensor("perm_%d" % l, [128, 128], BF16))
                btab = S.buf("ropetab")
                LD(cosT[:], cos_in, btab, [], [btab])
                LD(sinT[:], sin_in, btab, [], [btab])
                LD(perm[:], perm_in, btab, [], [btab])
                xr = [es.enter_context(nc.sbuf_tensor("xr%d_%d" % (i, l), [128, TOK], BF16)) for i in range(2)]
                bxr = [S.buf("xr%d" % i) for i in range(2)]
                xo = [es.enter_context(nc.sbuf_tensor("xo%d_%d" % (i, l), [128, TOK], BF16)) for i in range(2)]
                bxo = [S.buf("xo%d" % i) for i in range(2)]
                t1 = es.enter_context(nc.sbuf_tensor("rt1_%d" % l, [128, 512], F32))
                t2 = es.enter_context(nc.sbuf_tensor("rt2_%d" % l, [128, 512], F32))
                bt1, bt2 = S.buf("rt1"), S.buf("rt2")
                it = 0
                for (so, isk) in ((o_dfq, False), (o_dfk, True)):
                    for hh in range(DFW // 128):
                        sl = it % 2
                        it += 1
                        LD(xr[sl][:], zT[so + hh * 128: so + (hh + 1) * 128, 0:TOK], bxr[sl], [bzT], [bxr[sl]])
                        for ci, (t0, tn) in enumerate(tch):
                            bk = ci % 2
                            MM(banks[bk][:, 0:tn], perm[:], xr[sl][:, t0:t0 + tn], True, True, [btab, bxr[sl]], [bbank[bk]])
                            TT(t1[:, 0:tn], xr[sl][:, t0:t0 + tn], cosT[:, t0:t0 + tn], ALU.mult, [bxr[sl], btab], [bt1])
                            TT(t2[:, 0:tn], banks[bk][:, 0:tn], sinT[:, t0:t0 + tn], ALU.mult, [bbank[bk], btab], [bt2])
                            TT(xo[sl][:, t0:t0 + tn], t1[:, 0:tn], t2[:, 0:tn], ALU.add, [bt1, bt2], [bxo[sl]])
                        if isk:
                            LD(kdf.snd_rows(hh * 128, 128), xo[sl][:], bxo[sl], [bxo[sl]], [kdf.bsc(hh * 128)], q=POOL)
                        else:
                            LD(zT[so + hh * 128: so + (hh + 1) * 128, 0:TOK], xo[sl][:], bxo[sl], [bxo[sl]], [bzT], q=POOL)
                for r0 in range(0, TOK, vdf.rpc):
                    LD(vdf.snd_rows(r0, vdf.rpc), vtok[r0:r0 + vdf.rpc, NAW:NAW + DFW], castbuf, [bvtok], [vdf.bsc(r0)])
                for r0 in range(0, NAW, knh.rpc):
                    LD(knh.snd_rows(r0, knh.rpc)[:, 0:256], zT[o_nak + r0:o_nak + r0 + knh.rpc, 0:256], castbuf, [bzT], [knh.bsc(r0)])
                    LD(knh.snd_rows(r0, knh.rpc)[:, 256:512], zT[o_nak + r0:o_nak + r0 + knh.rpc, TOK - 256:TOK], castbuf, [bzT], [knh.bsc(r0)])
                for r0 in range(0, 512, vnh.rpc):
                    tk0 = r0 if r0 < 256 else TOK - 512 + r0
                    LD(vnh.snd_rows(r0, vnh.rpc), vtok[tk0:tk0 + vnh.rpc, 0:NAW], castbuf, [bvtok], [vnh.bsc(r0)])
                uv = [es.enter_context(nc.sbuf_tensor("uv%d_%d" % (i, l), [128, T], BF16)) for i in range(2)]
                ug = [es.enter_context(nc.sbuf_tensor("ug%d_%d" % (i, l), [128, T], BF16)) for i in range(2)]
                usg = es.enter_context(nc.sbuf_tensor("usg_%d" % l, [128, T], F32))
                uo = [es.enter_context(nc.sbuf_tensor("uo%d_%d" % (i, l), [128, T], BF16)) for i in range(2)]
                buv = [S.buf("uv%d" % i) for i in range(2)]
                bug = [S.buf("ug%d" % i) for i in range(2)]
                buo = [S.buf("uo%d" % i) for i in range(2)]
                busg = S.buf("usg")
                for ct in range(CT):
                    sl = ct % 2
                    LD(uv[sl][:], zT[o_cvv + ct * 128:o_cvv + (ct + 1) * 128, :], buv[sl], [bzT], [buv[sl]])
                    LD(ug[sl][:], zT[o_cvg + ct * 128:o_cvg + (ct + 1) * 128, :], bug[sl], [bzT], [bug[sl]])
                    AC(usg[:], ug[sl][:], AF.Sigmoid, [bug[sl]], [busg])
                    TT(uo[sl][:], uv[sl][:], usg[:], ALU.mult, [buv[sl], busg], [buo[sl]])
                    LD(uT[ct * 128:(ct + 1) * 128, :], uo[sl][:], buo[sl], [buo[sl]], [buT], q=POOL)
                    LD(cvh.snd_rows(ct * 128, 128)[:, 0:15], uo[sl][:, 0:15], buo[sl], [buo[sl]], [cvh.bsc(ct * 128)], q=POOL, slow=True)
                    LD(cvh.snd_rows(ct * 128, 128)[:, 16:31], uo[sl][:, TOK - 15:TOK], buo[sl], [buo[sl]], [cvh.bsc(ct * 128)], q=POOL, slow=True)
                for ga in (knh, vnh, cvh):
                    ga.gather(S)
                S.emit_phase()
            if stop < 4:
                return nc

            with ExitStack() as es:
                NB = NP + 8
                sel = es.enter_context(nc.sbuf_tensor("sel_%d" % l, [128, 16], F32))
                rm = es.enter_context(nc.sbuf_tensor("rm_%d" % l, [128, 5, 1024], F32))
                onesb = es.enter_context(nc.sbuf_tensor("onesb_%d" % l, [128, 128], BF16))
                bsel = S.buf("sel")
                LD(sel[:], sel_in, bsel, [], [bsel])
                LD(rm[:], rowmask_in.rearrange("s p c q -> p s (c q)"), bsel, [], [bsel])
                MS(onesb[:], 1.0, [bsel])
                for ga in (kdf, vdf):
                    ga.gather(S)
                rp = [es.enter_context(nc.sbuf_tensor("rp%d_%d" % (i, l), [128, 1024], F32)) for i in range(2)]
                brp = [S.buf("rp%d" % i) for i in range(2)]
                B5 = es.enter_context(nc.sbuf_tensor("B5_%d" % l, [128, 5, 1024], F32))
                bB5 = S.buf("B5")
                kbuf = es.enter_context(nc.sbuf_tensor("kbuf_%d" % l, [128, NB * 128], BF16))
                vbuf = es.enter_context(nc.sbuf_tensor("vbuf_%d" % l, [128, NB, 128], BF16))
                bkb, bvb = S.buf("kbuf"), S.buf("vbuf")
                MS(kbuf[:], 0.0, [bkb])
                MS(vbuf[:], 0.0, [bvb])
                hk = es.enter_context(nc.sbuf_tensor("hk_%d" % l, [128, 8, 512], BF16))
                hv = es.enter_context(nc.sbuf_tensor("hv_%d" % l, [128, 8, 4, 128], BF16))
                bhk, bhv = S.buf("hk"), S.buf("hv")
                kcs = es.enter_context(nc.sbuf_tensor("kcs_%d" % l, [128, CTX], BF16))
                vcs = es.enter_context(nc.sbuf_tensor("vcs_%d" % l, [128, CTX // 128, 128], BF16))
                bkc = S.buf("kcs")
                qT = es.enter_context(nc.sbuf_tensor("qTn_%d" % l, [128, T], BF16))
                gg = es.enter_context(nc.sbuf_tensor("ggn_%d" % l, [128, T], BF16))
                gsil = es.enter_context(nc.sbuf_tensor("gsn_%d" % l, [128, T], F32))
                bq, bgg, bgs = S.buf("qTn"), S.buf("ggn"), S.buf("gsn")
                oh = [es.enter_context(nc.sbuf_tensor("oh%d_%d" % (i, l), [128, T], BF16)) for i in range(2)]
                boh = [S.buf("oh%d" % i) for i in range(2)]
                ein_ = es.enter_context(nc.sbuf_tensor("ein_%d" % l, [128, 1024], F32))
                bein = S.buf("ein")
                E = [es.enter_context(nc.sbuf_tensor("E%d_%d" % (i, l), [128, 1024 + CTX], BF16)) for i in range(2)]
                bE = [S.buf("E%d" % i) for i in range(2)]
                rinv = es.enter_context(nc.sbuf_tensor("rinv_%d" % l, [128, 128], F32))
                otmp = es.enter_context(nc.sbuf_tensor("otmp_%d" % l, [128, 128], F32))
                bri = S.buf("rinv")
                nctx = CTX // 128
                unit = 0
                for h in range(NH):
                    hs = h % 2
                    LD(rp[hs][:], rpb_in[l, h].rearrange("p c q -> p (c q)"), brp[hs], [], [brp[hs]])
                    for s_ in range(5):
                        TT(B5[:, s_, :], rp[hs][:], rm[:, s_, :], ALU.add, [brp[hs], bsel], [bB5], eng=POOL)
                    LD(kbuf[:, 512:512 + TOK], zT[o_nak + h * 128:o_nak + (h + 1) * 128, 0:TOK], bkb, [bzT], [bkb])
                    LD(vbuf[:, 4:4 + NP, :], vtok[0:TOK, h * 128:(h + 1) * 128].rearrange("(c p) d -> p c d", p=128), bvb, [bvtok], [bvb])
                    LD(kcs[:], zT[o_nak + h * 128:o_nak + (h + 1) * 128, TOK:T], bkc, [bzT], [bkc])
                    LD(vcs[:], vtok[TOK:T, h * 128:(h + 1) * 128].rearrange("(c p) d -> p c d", p=128), bkc, [bvtok], [bkc])
                    S.dma_multi(SP, [(lambda hh, r=r, h=h: hh.dma_start(out=hk[:, r, :], in_=knh.rows(r, h * 128, 128))) for r in range(NCORE)],
                                bhk, reads=[knh.buf], writes=[bhk])
                    S.dma_multi(SP, [(lambda hh, r=r, h=h, c0=c0: hh.dma_start(out=hv[:, r, c0 // 128:(c0 + vnh.rpc) // 128, :],
                                                                                 in_=vnh.rows(r, c0, vnh.rpc)[:, h * 128:(h + 1) * 128]
                                                                                 .rearrange("(c p) d -> p c d", p=128)))
                                     for r in range(NCORE) for c0 in range(0, 512, vnh.rpc)],
                                bhv, reads=[vnh.buf], writes=[bhv])
                    for r in range(NCORE):
                        ka, kb_ = kbuf[:, 256:512], kbuf[:, 512 + TOK:512 + TOK + 256]
                        va, vb_ = vbuf[:, 2:4, :], vbuf[:, 4 + NP:6 + NP, :]
                        if r == 0:
                            TS(ka, hk[:, r, 256:512], sel[:, r:r + 1], None, ALU.mult, None, [bhk, bsel], [bkb])
                            TS(kb_, hk[:, r, 0:256], sel[:, 8 + r:9 + r], None, ALU.mult, None, [bhk, bsel], [bkb])
                            TS(va, hv[:, r, 2:4, :], sel[:, r:r + 1], None, ALU.mult, None, [bhv, bsel], [bvb])
                            TS(vb_, hv[:, r, 0:2, :], sel[:, 8 + r:9 + r], None, ALU.mult, None, [bhv, bsel], [bvb])
                        else:
                            STT(ka, hk[:, r, 256:512], sel[:, r:r + 1], ka, ALU.mult, ALU.add, [bhk, bsel], [bkb])
                            STT(kb_, hk[:, r, 0:256], sel[:, 8 + r:9 + r], kb_, ALU.mult, ALU.add, [bhk, bsel], [bkb])
                            STT(va, hv[:, r, 2:4, :], sel[:, r:r + 1], va, ALU.mult, ALU.add, [bhv, bsel], [bvb])
                            STT(vb_, hv[:, r, 0:2, :], sel[:, 8 + r:9 + r], vb_, ALU.mult, ALU.add, [bhv, bsel], [bvb])
                    LD(qT[:], zT[o_naq + h * 128:o_naq + (h + 1) * 128, :], bq, [bzT], [bq])
                    LD(gg[:], zT[o_nag + h * 128:o_nag + (h + 1) * 128, :], bgg, [bzT], [bgg])
                    AC(gsil[:], gg[:], AF.Silu, [bgg], [bgs])
                    qtiles = [(p, True) for p in range(NP)] + ([] if last else [(j, False) for j in range(nctx)])
                    for (p, local) in qtiles:
                        sb = 3 * (unit % 2)
                        es_ = unit % 2
                        unit += 1
                        bS0, bS1, bC = banks[sb], banks[sb + 1], banks[sb + 2]
                        q0 = p * 128 if local else TOK + p * 128
                        qap = qT[:, q0:q0 + 128]
                        if local:
                            slot = 0 if p == 0 else 1 if p == 1 else 3 if p == NP - 2 else 4 if p == NP - 1 else 2
                            for c in range(8):
                                bS = bS0 if c < 4 else bS1
                                MM(bS[:, (c % 4) * 128:(c % 4 + 1) * 128], kbuf[:, (p + c) * 128:(p + c + 1) * 128], qap, True, True,
                                   [bkb, bq], [bbank[sb + c // 4]])
                        for j in range(nctx):
                            MM(bC[:, j * 128:(j + 1) * 128], kcs[:, j * 128:(j + 1) * 128], qap, True, True, [bkc, bq], [bbank[sb + 2]])
                        if local:
                            for half in range(2):
                                STT(ein_[:, half * 512:(half + 1) * 512], banks[sb + half][:], sc_att, B5[:, slot, half * 512:(half + 1) * 512],
                                    ALU.mult, ALU.add, [bbank[sb + half], bB5], [bein])
                            AC(E[es_][:, 0:1024], ein_[:], AF.Exp, [bein], [bE[es_]])
                        AC(E[es_][:, 1024:1024 + CTX], bC[:, 0:CTX], AF.Exp, [bbank[sb + 2]], [bE[es_]], scale=sc_att)
                        chunks = ([(E[es_][:, c * 128:(c + 1) * 128], vbuf[:, p + c, :]) for c in range(8)] if local else []) + \
                                 [(E[es_][:, 1024 + j * 128:1024 + (j + 1) * 128], vcs[:, j, :]) for j in range(nctx)]
                        for i_, (eap, vap) in enumerate(chunks):
                            MM(bC[:, 256:384], onesb[:], eap, i_ == 0, i_ == len(chunks) - 1, [bsel, bE[es_]], [bbank[sb + 2]])
                        for i_, (eap, vap) in enumerate(chunks):
                            MM(bC[:, 384:512], vap, eap, i_ == 0, i_ == len(chunks) - 1, [bvb, bkc, bE[es_]], [bbank[sb + 2]])
                        RCP(rinv[:], bC[:, 256:384], [bbank[sb + 2]], [bri])
                        TT(otmp[:], bC[:, 384:512], rinv[:], ALU.mult, [bbank[sb + 2], bri], [bri])
                        TT(oh[hs][:, q0:q0 + 128], otmp[:], gsil[:, q0:q0 + 128], ALU.mult, [bri, bgs], [boh[hs]])
                    LD(gT[h * 128:(h + 1) * 128, 0:TE], oh[hs][:, 0:TE], boh[hs], [boh[hs]], [bgT], q=SP)
                S.emit_phase()
            if stop < 5:
                return nc

            lam_init = 0.8 - 0.6 * float(np.exp(-0.3 * l))
            with ExitStack() as es:
                onesb = es.enter_context(nc.sbuf_tensor("onesd_%d" % l, [128, 128], BF16))
                bcn = S.buf("dconst")
                MS(onesb[:], 1.0, [bcn])
                if l == 0:
                    for g_ in late_gathers:
                        g_.gather(nc, S, castbufs)
                lv = es.enter_context(nc.sbuf_tensor("lv_%d" % l, [128, 4, 128], F32))
                lsc = es.enter_context(nc.sbuf_tensor("lsc_%d" % l, [128, 4], F32))
                gcol = es.enter_context(nc.sbuf_tensor("gcol_%d" % l, [128, 2], F32))
                for i_ in range(4):
                    LD(lv[:, i_, :], bass.AP(lam_in.tensor, (l * 4 + i_) * 128, [[0, 128], [1, 128]]), bcn, [], [bcn])
                LD(gcol[:], subg_in[l].rearrange("(c p) -> p c", p=128), bcn, [], [bcn], slow=True)
                TT(lv[:, 0, :], lv[:, 0, :], lv[:, 1, :], ALU.mult, [bcn], [bcn])
                TT(lv[:, 2, :], lv[:, 2, :], lv[:, 3, :], ALU.mult, [bcn], [bcn])
                S.op(DVE, lambda hh: hh.tensor_reduce(lsc[:, 0:1], lv[:, 0, :], AX.X, ALU.add), reads=[bcn], writes=[bcn])
                S.op(DVE, lambda hh: hh.tensor_reduce(lsc[:, 1:2], lv[:, 2, :], AX.X, ALU.add), reads=[bcn], writes=[bcn])
                AC(lsc[:, 0:2], lsc[:, 0:2], AF.Exp, [bcn], [bcn])
                TT(lsc[:, 2:3], lsc[:, 0:1], lsc[:, 1:2], ALU.subtract, [bcn], [bcn])
                TS(lsc[:, 2:3], lsc[:, 2:3], lam_init, -1.0, ALU.add, ALU.mult, [bcn], [bcn])
                TS(gcol[:], gcol[:], 1.0 - lam_init, None, ALU.mult, None, [bcn], [bcn])
                qT = es.enter_context(nc.sbuf_tensor("qTd_%d" % l, [128, 2, T], BF16))
                bq = S.buf("qTd")
                gg = es.enter_context(nc.sbuf_tensor("ggd_%d" % l, [128, 2, T], BF16))
                gsil = es.enter_context(nc.sbuf_tensor("gsd_%d" % l, [128, 2, T], F32))
                bgg, bgs = S.buf("ggd"), S.buf("gsd")
                kp = [es.enter_context(nc.sbuf_tensor("kp%d_%d" % (i, l), [128, 2, TOK], BF16)) for i in range(2)]
                vp = [es.enter_context(nc.sbuf_tensor("vp%d_%d" % (i, l), [128, TOK // 128, 256], BF16)) for i in range(2)]
                bkp = [S.buf("kp%d" % i) for i in range(2)]
                bvp = [S.buf("vp%d" % i) for i in range(2)]
                kc_ = es.enter_context(nc.sbuf_tensor("kcd_%d" % l, [128, 2, CTX], BF16))
                vc_ = es.enter_context(nc.sbuf_tensor("vcd_%d" % l, [128, CTX // 128, 256], BF16))
                bkc = S.buf("kcd")
                Et = [es.enter_context(nc.sbuf_tensor("Et%d_%d" % (i, l), [128, 512], BF16)) for i in range(6)]
                bEt = [S.buf("Et%d" % i) for i in range(6)]
                accs = [es.enter_context(nc.sbuf_tensor("acc%d_%d" % (i, l), [128, 512], F32)) for i in range(4)]
                baccs = [S.buf("acc%d" % i) for i in range(4)]
                fr = es.enter_context(nc.sbuf_tensor("fr_%d" % l, [128, 2, 512], F32))
                fo = es.enter_context(nc.sbuf_tensor("fo_%d" % l, [128, 2, 512], F32))
                fu = es.enter_context(nc.sbuf_tensor("fu_%d" % l, [128, 512], F32))
                ft_ = es.enter_context(nc.sbuf_tensor("ft_%d" % l, [128, 512], F32))
                fsq = es.enter_context(nc.sbuf_tensor("fsq_%d" % l, [128, 512], F32))
                frn = es.enter_context(nc.sbuf_tensor("frn_%d" % l, [128, 512], F32))
                fg = [es.enter_context(nc.sbuf_tensor("fg%d_%d" % (i, l), [128, 512], BF16)) for i in range(2)]
                bfr, bfo, bfu, bft, bfsq, bfrn = S.buf("fr"), S.buf("fo"), S.buf("fu"), S.buf("ft"), S.buf("fsq"), S.buf("frn")
                bfg = [S.buf("fg%d" % i) for i in range(2)]
                pit = 0
                eit = 0
                git = 0
                sit = 0
                for hd in range(HD):
                    for half in range(2):
                        LD(qT[:, half, :], zT[o_dfq + (2 * hd + half) * 128:o_dfq + (2 * hd + half + 1) * 128, :], bq, [bzT], [bq])
                        LD(gg[:, half, :], zT[o_dfg + (2 * hd + half) * 128:o_dfg + (2 * hd + half + 1) * 128, :], bgg, [bzT], [bgg])
                        LD(kc_[:, half, :], zT[o_dfk + (2 * hd + half) * 128:o_dfk + (2 * hd + half + 1) * 128, TOK:T], bkc, [bzT], [bkc])
                    LD(vc_[:], vtok[TOK:T, NAW + hd * 256:NAW + (hd + 1) * 256].rearrange("(c p) d -> p c d", p=128), bkc, [bvtok], [bkc])
                    AC(gsil[:].rearrange("p a t -> p (a t)"), gg[:].rearrange("p a t -> p (a t)"), AF.Silu, [bgg], [bgs])
                    qcs = [(t0, tn, True) for (t0, tn) in tch] + ([] if last else [(TOK, CTX, False)])
                    for (t0, tn, local) in qcs:
                        pieces = (list(range(NCORE)) if local else []) + [-1]
                        first = True
                        accn = [0, 0, 0, 0]
                        pend = []

                        def flush_one():
                            half_, vaps, e2, f2, l2, rbv = pend.pop(0)
                            for dv in range(2):
                                MM(banks[4 + half_ * 2 + dv][:, 0:tn], vaps[dv], Et[e2][:, 0:tn], f2, l2, [rbv, bEt[e2]], [bbank[4 + half_ * 2 + dv]])

                        for pi_, r in enumerate(pieces):
                            if r >= 0:
                                sl = pit % 2
                                pit += 1
                                S.dma_multi(SP, [(lambda hh, half=half, sl=sl, r=r, hd=hd: hh.dma_start(out=kp[sl][:, half, :],
                                                                                                       in_=kdf.rows(r, (2 * hd + half) * 128, 128)))
                                                 for half in range(2)], bkp[sl], reads=[kdf.buf], writes=[bkp[sl]])
                                S.dma_multi(SP, [(lambda hh, sl=sl, r=r, hd=hd, c0=c0: hh.dma_start(
                                    out=vp[sl][:, c0 // 128:(c0 + vdf.rpc) // 128, :],
                                    in_=vdf.rows(r, c0, vdf.rpc)[:, hd * 256:(hd + 1) * 256].rearrange("(c p) d -> p c d", p=128)))
                                    for c0 in range(0, TOK, vdf.rpc)], bvp[sl], reads=[vdf.buf], writes=[bvp[sl]])
                                nk = TOK // 128
                                kget = lambda half, kc, sl=sl: kp[sl][:, half, kc * 128:(kc + 1) * 128]
                                vget = lambda kc, dv, sl=sl: vp[sl][:, kc, dv * 128:(dv + 1) * 128]
                                rb = [bkp[sl], bvp[sl]]
                            else:
                                nk = CTX // 128
                                kget = lambda half, kc: kc_[:, half, kc * 128:(kc + 1) * 128]
                                vget = lambda kc, dv: vc_[:, kc, dv * 128:(dv + 1) * 128]
                                rb = [bkc, bkc]
                            for kc in range(nk):
                                lastu = (pi_ == len(pieces) - 1 and kc == nk - 1)
                                for half in range(2):
                                    sb = sit % 4
                                    sit += 1
                                    e_ = eit % 6
                                    eit += 1
                                    MM(banks[sb][:, 0:tn], kget(half, kc), qT[:, half, t0:t0 + tn], True, True, [rb[0], bq], [bbank[sb]])
                                    AC(Et[e_][:, 0:tn], banks[sb][:, 0:tn], AF.Exp, [bbank[sb]], [bEt[e_]], scale=sc_att)
                                    a_ = half * 2 + (kc % 2)
                                    if accn[a_] == 0:
                                        S.op(DVE, lambda hh, a_=a_, e_=e_, tn=tn: hh.tensor_copy(accs[a_][:, 0:tn], Et[e_][:, 0:tn]),
                                             reads=[bEt[e_]], writes=[baccs[a_]])
                                    else:
                                        TT(accs[a_][:, 0:tn], accs[a_][:, 0:tn], Et[e_][:, 0:tn], ALU.add, [bEt[e_], baccs[a_]], [baccs[a_]])
                                    accn[a_] += 1
                                    pend.append((half, [vget(kc, 0), vget(kc, 1)], e_, first, lastu, rb[1]))
                                    if len(pend) > 2:
                                        flush_one()
                                first = False
                        while pend:
                            flush_one()
                        for half in range(2):
                            used = [a_ for a_ in (half * 2, half * 2 + 1) if accn[a_] > 0]
                            for i_, a_ in enumerate(used):
                                MM(banks[half][:, 0:tn], ones32[:], accs[a_][:, 0:tn], i_ == 0, i_ == len(used) - 1, [bconst, baccs[a_]],
                                   [bbank[half]])
                            RCP(fr[:, half, 0:tn], banks[half][:, 0:tn], [bbank[half]], [bfr])
                        for dv in range(2):
                            TT(fu[:, 0:tn], banks[4 + dv][:, 0:tn], fr[:, 0, 0:tn], ALU.mult, [bbank[4 + dv], bfr], [bfu])
                            TT(ft_[:, 0:tn], banks[6 + dv][:, 0:tn], fr[:, 1, 0:tn], ALU.mult, [bbank[6 + dv], bfr], [bft])
                            STT(fo[:, dv, 0:tn], ft_[:, 0:tn], lsc[:, 2:3], fu[:, 0:tn], ALU.mult, ALU.add, [bft, bfu, bcn], [bfo])
                            AC(fsq[:, 0:tn], fo[:, dv, 0:tn], AF.Square, [bfo], [bfsq])
                            MM(banks[2][:, 0:tn], ones32[:], fsq[:, 0:tn], dv == 0, dv == 1, [bconst, bfsq], [bbank[2]])
                        AC(frn[:, 0:tn], banks[2][:, 0:tn], AF.Sqrt, [bbank[2], bconst], [bfrn], scale=1.0 / 256.0, bias=eps6[:, 1:2])
                        RCP(frn[:, 0:tn], frn[:, 0:tn], [bfrn], [bfrn])
                        for dv in range(2):
                            g_ = git % 2
                            git += 1
                            TT(fo[:, dv, 0:tn], fo[:, dv, 0:tn], frn[:, 0:tn], ALU.mult, [bfo, bfrn], [bfo])
                            STT(fg[g_][:, 0:tn], fo[:, dv, 0:tn], gcol[:, dv:dv + 1], gsil[:, dv, t0:t0 + tn], ALU.mult, ALU.mult,
                                [bfo, bcn, bgs], [bfg[g_]])
                            LD(gT[NAW + hd * 256 + dv * 128:NAW + hd * 256 + (dv + 1) * 128, t0:t0 + tn], fg[g_][:, 0:tn], bfg[g_], [bfg[g_]], [bgT],
                               q=SP)
                S.emit_phase()
            if stop < 6:
                return nc

            with ExitStack() as es:
                sel = es.enter_context(nc.sbuf_tensor("selc_%d" % l, [128, 16], F32))
                bsel = S.buf("selc")
                LD(sel[:], sel_in, bsel, [], [bsel])
                cw = es.enter_context(nc.sbuf_tensor("cw_%d" % l, [128, CT, 31], F32))
                cpar = es.enter_context(nc.sbuf_tensor("cpar_%d" % l, [128, 3, CT], F32))
                for ct in range(CT):
                    LD(cw[:, ct, :], convw_in[l][:, ct * 128:(ct + 1) * 128].rearrange("k p -> p k"), bsel, [], [bsel], slow=True)
                for i_, src in enumerate((convb_in, convg_in, convbt_in)):
                    LD(cpar[:, i_, :], src[l].rearrange("(c p) -> p c", p=128), bsel, [], [bsel], slow=True)
                hc = es.enter_context(nc.sbuf_tensor("hc_%d" % l, [128, 8, 32], BF16))
                bhc = S.buf("hc")
                seqs = [(0, TOK, True)] + ([] if last else [(TOK, CTX, False)])
                ub = [es.enter_context(nc.sbuf_tensor("ub%d_%d" % (i, l), [128, 30 + TOK], BF16)) for i in range(2)]
                bub = [S.buf("ub%d" % i) for i in range(2)]
                yT = es.enter_context(nc.sbuf_tensor("yT_%d" % l, [128, CT, TOK], F32))
                byT = S.buf("yT")
                sq = es.enter_context(nc.sbuf_tensor("csq_%d" % l, [128, 512], F32))
                mean = es.enter_context(nc.sbuf_tensor("cmean_%d" % l, [128, 512], F32))
                msq = es.enter_context(nc.sbuf_tensor("cmsq_%d" % l, [128, 512], F32))
                rstd = es.enter_context(nc.sbuf_tensor("crstd_%d" % l, [128, 512], F32))
                tn_ = es.enter_context(nc.sbuf_tensor("ctn_%d" % l, [128, 512], F32))
                gt = [es.enter_context(nc.sbuf_tensor("cgt%d_%d" % (i, l), [128, 512], BF16)) for i in range(2)]
                gs_ = es.enter_context(nc.sbuf_tensor("cgs_%d" % l, [128, 512], F32))
                og = [es.enter_context(nc.sbuf_tensor("cog%d_%d" % (i, l), [128, 512], BF16)) for i in range(2)]
                bsq, bmean, bmsq, brstd, btn, bgs_ = S.buf("csq"), S.buf("cmean"), S.buf("cmsq"), S.buf("crstd"), S.buf("ctn"), S.buf("cgs")
                bgt = [S.buf("cgt%d" % i) for i in range(2)]
                bog = [S.buf("cog%d" % i) for i in range(2)]
                uit = 0
                oit = 0
                for (c0, nt, halo) in seqs:
                    for ct in range(CT):
                        sl = uit % 2
                        uit += 1
                        LD(ub[sl][:, 15:15 + nt], uT[ct * 128:(ct + 1) * 128, c0:c0 + nt], bub[sl], [buT], [bub[sl]])
                        if halo:
                            S.dma_multi(SP, [(lambda hh, r=r, ct=ct: hh.dma_start(out=hc[:, r, :], in_=cvh.rows(r, ct * 128, 128))) for r in range(NCORE)],
                                        bhc, reads=[cvh.buf], writes=[bhc])
                            for r in range(NCORE):
                                ha, hb = ub[sl][:, 0:15], ub[sl][:, 15 + nt:30 + nt]
                                if r == 0:
                                    TS(ha, hc[:, r, 16:31], sel[:, r:r + 1], None, ALU.mult, None, [bhc, bsel], [bub[sl]])
                                    TS(hb, hc[:, r, 0:15], sel[:, 8 + r:9 + r], None, ALU.mult, None, [bhc, bsel], [bub[sl]])
                                else:
                                    STT(ha, hc[:, r, 16:31], sel[:, r:r + 1], ha, ALU.mult, ALU.add, [bhc, bsel], [bub[sl]])
                                    STT(hb, hc[:, r, 0:15], sel[:, 8 + r:9 + r], hb, ALU.mult, ALU.add, [bhc, bsel], [bub[sl]])
                        else:
                            MS(ub[sl][:, 0:15], 0.0, [bub[sl]])
                            MS(ub[sl][:, 15 + nt:30 + nt], 0.0, [bub[sl]])
                        TS(yT[:, ct, 0:nt], ub[sl][:, 0:nt], cw[:, ct, 0:1], cpar[:, 0, ct:ct + 1], ALU.mult, ALU.add, [bub[sl], bsel], [byT])
                        for k in range(1, 31):
                            STT(yT[:, ct, 0:nt], ub[sl][:, k:k + nt], cw[:, ct, k:k + 1], yT[:, ct, 0:nt], ALU.mult, ALU.add, [bub[sl], bsel], [byT])
                    for t0 in range(0, nt, 512):
                        tn = min(512, nt - t0)
                        for ct in range(CT):
                            MM(banks[0][:, 0:tn], ones32[:], yT[:, ct, t0:t0 + tn], ct == 0, ct == CT - 1, [bconst, byT], [bbank[0]])
                        for ct in range(CT):
                            AC(sq[:, 0:tn], yT[:, ct, t0:t0 + tn], AF.Square, [byT], [bsq])
                            MM(banks[1][:, 0:tn], ones32[:], sq[:, 0:tn], ct == 0, ct == CT - 1, [bconst, bsq], [bbank[1]])
                        AC(mean[:, 0:tn], banks[0][:, 0:tn], AF.Identity, [bbank[0]], [bmean], scale=1.0 / CVW)
                        TT(msq[:, 0:tn], mean[:, 0:tn], mean[:, 0:tn], ALU.mult, [bmean], [bmsq])
                        STT(rstd[:, 0:tn], banks[1][:, 0:tn], 1.0 / CVW, msq[:, 0:tn], ALU.mult, ALU.subtract, [bbank[1], bmsq], [brstd])
                        AC(rstd[:, 0:tn], rstd[:, 0:tn], AF.Sqrt, [brstd, bconst], [brstd], bias=eps6[:, 1:2])
                        RCP(rstd[:, 0:tn], rstd[:, 0:tn], [brstd], [brstd])
                        for ct in range(CT):
                            o_ = oit % 2
                            oit += 1
                            LD(gt[o_][:, 0:tn], zT[o_cvgate + ct * 128:o_cvgate + (ct + 1) * 128, c0 + t0:c0 + t0 + tn], bgt[o_], [bzT], [bgt[o_]])
                            AC(gs_[:, 0:tn], gt[o_][:, 0:tn], AF.Silu, [bgt[o_]], [bgs_])
                            TT(tn_[:, 0:tn], yT[:, ct, t0:t0 + tn], mean[:, 0:tn], ALU.subtract, [byT, bmean], [btn])
                            TT(tn_[:, 0:tn], tn_[:, 0:tn], rstd[:, 0:tn], ALU.mult, [btn, brstd], [btn])
                            AC(tn_[:, 0:tn], tn_[:, 0:tn], AF.Silu, [btn, bsel], [btn], scale=cpar[:, 1, ct:ct + 1], bias=cpar[:, 2, ct:ct + 1])
                            TT(og[o_][:, 0:tn], tn_[:, 0:tn], gs_[:, 0:tn], ALU.mult, [btn, bgs_], [bog[o_]])
                            LD(gT[NAW + DFW + ct * 128:NAW + DFW + (ct + 1) * 128, c0 + t0:c0 + t0 + tn], og[o_][:, 0:tn], bog[o_], [bog[o_]], [bgT], q=POOL)
                S.emit_phase()
            if stop < 7:
                return nc

            alpha = (2.0 * L) ** 0.25
            WK = NAW // 128
            TC = 256
            with ExitStack() as es:
                gch = es.enter_context(nc.sbuf_tensor("gch_%d" % l, [128, 3 * WK, TC], BF16))
                bgch = S.buf("gch")
                wps = es.enter_context(nc.sbuf_tensor("wps_%d" % l, [128, 3, WK, 256], BF16))
                bwps = S.buf("wps")
                mgt = [es.enter_context(nc.sbuf_tensor("mgt%d_%d" % (i, l), [128, 3, TC], BF16)) for i in range(2)]
                bmgt = [S.buf("mgt%d" % i) for i in range(2)]
                sg = es.enter_context(nc.sbuf_tensor("msg_%d" % l, [128, 3, TC], F32))
                bsg = S.buf("msg")
                ya = es.enter_context(nc.sbuf_tensor("mya_%d" % l, [128, TC], F32))
                yb = es.enter_context(nc.sbuf_tensor("myb_%d" % l, [128, TC], F32))
                bya, byb = S.buf("mya"), S.buf("myb")
                yT_ = es.enter_context(nc.sbuf_tensor("myT_%d" % l, [128, KC, TC], BF16))
                byT_ = S.buf("myT")
                wo = [es.enter_context(nc.sbuf_tensor("wo%d_%d" % (i, l), [128, KC, 256], BF16)) for i in range(2)]
                bwo = [S.buf("wo%d" % i) for i in range(2)]
                osb = es.enter_context(nc.sbuf_tensor("osb_%d" % l, [128, D], F32))
                bosb = S.buf("osb")
                xt_ = es.enter_context(nc.sbuf_tensor("pxt_%d" % l, [128, D], F32))
                bxt_ = S.buf("pxt")
                gbc = es.enter_context(nc.sbuf_tensor("gbc_%d" % l, [128, D], F32))
                pgb = es.enter_context(nc.sbuf_tensor("pgb_%d" % l, [128, 2, D], F32))
                bgbc, bpgb = S.buf("gbc"), S.buf("pgb")
                st2 = es.enter_context(nc.sbuf_tensor("st2_%d" % l, [128, 8, 6], F32))
                mv2 = es.enter_context(nc.sbuf_tensor("mv2_%d" % l, [128, 2], F32))
                bst2 = S.buf("st2")
                LD(pgb[:, 0, :], bass.AP(plg_in.tensor, l * D, [[0, 128], [1, D]]), bpgb, [], [bpgb])
                LD(pgb[:, 1, :], bass.AP(plb_in.tensor, l * D, [[0, 128], [1, D]]), bpgb, [], [bpgb])
                cur_v = -1
                woit = 0
                mit = 0
                gws = [gw_pna[l], gw_pdf[l], gw_pcv[l]]
                for t0 in range(0, TE, TC):
                    v = 0 if t0 < TOK else 1
                    if v != cur_v:
                        cur_v = v
                        for (ap, o, m) in mod_cols(v, l, 2 * D, D):
                            LD(gbc[:, o:o + m], bass.AP(ap.tensor, ap.offset, [[0, 128], [1, m]]), bgbc, [bmod], [bgbc])
                    for b_ in range(3):
                        LD(gch[:, b_ * WK:(b_ + 1) * WK, :], gT[b_ * NAW:(b_ + 1) * NAW, t0:t0 + TC].rearrange("(k p) t -> p k t", p=128),
                           bgch, [bgT], [bgch])
                    for fst in range(D // 256):
                        for b_ in range(3):
                            S.dma_multi(SP, [(lambda hh, b_=b_, r=r, fst=fst: hh.dma_start(out=wps[:, b_, r * gws[b_].kcl:(r + 1) * gws[b_].kcl, :],
                                                                                          in_=gws[b_].st_ap(fst)[:, r])) for r in range(NCORE)],
                                        bwps, reads=[gws[b_].buf], writes=[bwps])
                        for half in range(2):
                            ftile = fst * 2 + half
                            m_ = mit % 2
                            mit += 1
                            for b_ in range(3):
                                LD(mgt[m_][:, b_, :], zT[o_mg[b_] + ftile * 128:o_mg[b_] + (ftile + 1) * 128, t0:t0 + TC], bmgt[m_], [bzT], [bmgt[m_]])
                            AC(sg[:].rearrange("p a t -> p (a t)"), mgt[m_][:].rearrange("p a t -> p (a t)"), AF.Sigmoid, [bmgt[m_]], [bsg])
                            for b_ in range(3):
                                for k in range(WK):
                                    MM(banks[b_][:, 0:TC], wps[:, b_, k, half * 128:(half + 1) * 128], gch[:, b_ * WK + k, :], k == 0, k == WK - 1,
                                       [bwps, bgch], [bbank[b_]])
                            TT(ya[:], banks[0][:, 0:TC], sg[:, 0, :], ALU.mult, [bbank[0], bsg], [bya])
                            TT(yb[:], banks[1][:, 0:TC], sg[:, 1, :], ALU.mult, [bbank[1], bsg], [byb])
                            TT(ya[:], ya[:], yb[:], ALU.add, [bya, byb], [bya])
                            TT(yb[:], banks[2][:, 0:TC], sg[:, 2, :], ALU.mult, [bbank[2], bsg], [byb])
                            TT(yT_[:, ftile, :], ya[:], yb[:], ALU.add, [bya, byb], [byT_])
                    for tt in range(TC // 128):
                        tok0 = t0 + tt * 128
                        LD(xt_[:], xcur[tok0:tok0 + 128, :], bxt_, [bxcur], [bxt_])
                        for fst in range(D // 256):
                            w_ = woit % 2
                            woit += 1
                            bk = 3 + (woit % 4)
                            S.dma_multi(SP, [(lambda hh, w_=w_, r=r, fst=fst: hh.dma_start(out=wo[w_][:, r * gw_out[l].kcl:(r + 1) * gw_out[l].kcl, :],
                                                                                          in_=gw_out[l].st_ap(fst)[:, r])) for r in range(NCORE)],
                                        bwo[w_], reads=[gw_out[l].buf], writes=[bwo[w_]])
                            for k in range(KC):
                                MM(banks[bk][:, 0:256], yT_[:, k, tt * 128:(tt + 1) * 128], wo[w_][:, k, :], k == 0, k == KC - 1, [byT_, bwo[w_]], [bbank[bk]])
                            TT(osb[:, fst * 256:(fst + 1) * 256], banks[bk][:, 0:256], gbc[:, fst * 256:(fst + 1) * 256], ALU.mult, [bbank[bk], bgbc], [bosb])
                        STT(osb[:], xt_[:], alpha, osb[:], ALU.mult, ALU.add, [bxt_, bosb], [bosb])
                        nchs = (D + 511) // 512
                        for c in range(nchs):
                            S.op(DVE, lambda hh, c=c: hh.bn_stats(st2[:, c, :], osb[:, c * 512:(c + 1) * 512]), reads=[bosb], writes=[bst2])
                        S.op(DVE, lambda hh: hh.bn_aggr(mv2[:], st2[:, 0:nchs, :].rearrange("p c s -> p (c s)")), reads=[bst2], writes=[bst2])
                        AC(mv2[:, 1:2], mv2[:, 1:2], AF.Sqrt, [bst2, bconst], [bst2], bias=eps6[:, 0:1])
                        RCP(mv2[:, 1:2], mv2[:, 1:2], [bst2], [bst2])
                        TS(osb[:], osb[:], mv2[:, 0:1], mv2[:, 1:2], ALU.subtract, ALU.mult, [bosb, bst2], [bosb])
                        TT(osb[:], osb[:], pgb[:, 0, :], ALU.mult, [bosb, bpgb], [bosb])
                        TT(xt_[:], osb[:], pgb[:, 1, :], ALU.add, [bosb, bpgb], [bxt_])
                        LD(xcur[tok0:tok0 + 128, :], xt_[:], bxt_, [bxt_], [bxcur], q=POOL)
                S.emit_phase()

        if stop < 99:
            return nc
        for r0 in range(0, TOK, 128):
            S.dma(SP, lambda h, r0=r0: h.dma_start(out=out[r0:r0 + 128, :], in_=xcur[r0:r0 + 128, :]), castbuf, reads=[bxcur])
        S.emit_phase()
    return nc


NEG = -30000.0


def make_maps(cfg, x, c, ctx, c_ctx, w_ada, b_ada, w_in, b_in, na_rpb, diff_lq1, diff_lk1, diff_lq2, diff_lk2, diff_subln_g,
              conv_w, conv_b, conv_ln_g, conv_ln_b, w_proj_na, w_proj_diff, w_proj_conv, w_out, post_ln_g, post_ln_b):
    D, L, TOK, NAW, DFW, CVW = cfg.D, cfg.L, cfg.TOK, cfg.NAW, cfg.DFW, cfg.CVW
    f32 = np.float32
    NA3 = 3 * D // NCORE
    x2 = np.asarray(x, f32).reshape(cfg.SEQ, D)
    ctx2 = np.ascontiguousarray(np.asarray(ctx, f32).reshape(cfg.CTX, D))
    c2 = np.ascontiguousarray(np.stack([np.asarray(c, f32).reshape(D), np.asarray(c_ctx, f32).reshape(D)]))
    ident = np.eye(128, dtype=ml_dtypes.bfloat16)
    perm = np.zeros((128, 128), f32)
    perm[np.arange(128) ^ 1, np.arange(128)] = 1.0
    perm = perm.astype(ml_dtypes.bfloat16)
    ROWS = TOK // 64
    NP = ROWS // 2
    GROWS = cfg.SEQ // 64
    NH = NAW // 128
    b2 = np.arange(2)[:, None, None, None, None]
    kc = np.arange(64)[None, :, None, None, None]
    cc = np.arange(8)[None, None, :, None, None]
    aa = np.arange(2)[None, None, None, :, None]
    qc = np.arange(64)[None, None, None, None, :]
    dr = 2 * cc + b2 - aa - 1
    dc = kc - qc + 15
    cs = np.clip(qc - 8, 0, 48)
    ok = (dr >= 0) & (dr <= 14) & (kc >= cs) & (kc < cs + 16) & (dc >= 0) & (dc <= 30)
    ok = np.broadcast_to(ok, (2, 64, 8, 2, 64))
    dri = np.broadcast_to(np.clip(dr, 0, 14), ok.shape)
    dci = np.broadcast_to(np.clip(dc, 0, 30), ok.shape)
    rpb = np.asarray(na_rpb, f32)
    rpbT = np.where(ok[None, None], rpb[:, :, dri, dci], f32(NEG)).astype(f32).reshape(L, NH, 128, 8, 128)
    lamv = np.ascontiguousarray(np.stack([diff_lq1, diff_lk1, diff_lq2, diff_lk2], 1).astype(f32).reshape(-1))
    inv_freq = (10000.0 ** (-np.arange(32, dtype=f32) / f32(32))).astype(f32)
    maps = []
    for r in range(NCORE):
        t = np.arange(r * TOK, (r + 1) * TOK)
        row = (t // 64).astype(f32)
        col = (t % 64).astype(f32)
        ang = np.concatenate([row[:, None] * inv_freq, col[:, None] * inv_freq], -1).astype(f32)
        cosT = np.ascontiguousarray(np.repeat(np.cos(ang).astype(f32), 2, axis=1).T)
        sgn = np.where(np.arange(128) % 2 == 0, -1.0, 1.0).astype(f32)
        sinT = np.ascontiguousarray((np.repeat(np.sin(ang).astype(f32), 2, axis=1) * sgn).T)
        sel = np.zeros((128, 16), f32)
        if r > 0:
            sel[:, r - 1] = 1.0
        if r < NCORE - 1:
            sel[:, 8 + r + 1] = 1.0
        rowmask = np.zeros((5, 2, 64, 8, 2, 64), f32)
        for si, p in enumerate((0, 1, 2, NP - 2, NP - 1)):
            q_abs = r * ROWS + 2 * p + aa
            k_abs = r * ROWS + 2 * p + 2 * cc + b2 - 8
            r0 = np.clip(q_abs - 4, 0, GROWS - 8)
            valid = (k_abs >= r0) & (k_abs < r0 + 8)
            rowmask[si] = np.where(np.broadcast_to(valid, (2, 64, 8, 2, 64)), 0.0, NEG)
        rowmask = rowmask.reshape(5, 128, 8, 128)
        maps.append(dict(
            x=np.ascontiguousarray(x2[r * TOK:(r + 1) * TOK]), ctx=ctx2, c2=c2,
            w_ada=np.ascontiguousarray(w_ada[:, :, r * NA3:(r + 1) * NA3]),
            b_ada=np.ascontiguousarray(b_ada[:, r * NA3:(r + 1) * NA3]),
            w_in=np.ascontiguousarray(w_in[:, r * D // NCORE:(r + 1) * D // NCORE, :]),
            b_in=np.ascontiguousarray(b_in), ident=ident, perm=perm, ropecos=cosT, ropesin=sinT, sel=sel, rowmask=rowmask,
            rpbT=rpbT, lamv=lamv, subg=np.ascontiguousarray(diff_subln_g, f32), conv_w=np.ascontiguousarray(conv_w, f32),
            conv_b=np.ascontiguousarray(conv_b, f32), conv_g=np.ascontiguousarray(conv_ln_g, f32), conv_bt=np.ascontiguousarray(conv_ln_b, f32),
            post_g=np.ascontiguousarray(post_ln_g, f32).reshape(-1), post_b=np.ascontiguousarray(post_ln_b, f32).reshape(-1),
            w_pna=np.ascontiguousarray(w_proj_na[:, r * NAW // NCORE:(r + 1) * NAW // NCORE, :]),
            w_pdf=np.ascontiguousarray(w_proj_diff[:, r * DFW // NCORE:(r + 1) * DFW // NCORE, :]),
            w_pcv=np.ascontiguousarray(w_proj_conv[:, r * CVW // NCORE:(r + 1) * CVW // NCORE, :]),
            w_out=np.ascontiguousarray(w_out[:, r * D // NCORE:(r + 1) * D // NCORE, :])))
    return maps


def kernel(**inputs):
    cfg = Cfg()
    nc = build(cfg)
    maps = make_maps(cfg, **{k: np.asarray(v) for k, v in inputs.items()})
    res = run_bass_kernel_spmd(nc, maps, core_ids=list(range(NCORE)))
    out = np.concatenate([res.results[r]["out"] for r in range(NCORE)], axis=0)
    return out.reshape(1, cfg.SEQ, cfg.D).astype(np.float32)
```

```python
import numpy as np
import ml_dtypes
from concourse.bass_utils import run_bass_kernel_spmd
import numpy as np
from contextlib import ExitStack
import concourse.bass as bass
import concourse.mybir as mybir

F32 = mybir.dt.float32
BF16 = mybir.dt.bfloat16
ALU = mybir.AluOpType
AF = mybir.ActivationFunctionType
AX = mybir.AxisListType

PE, DVE, ACT, POOL, SP = "tensor", "vector", "scalar", "gpsimd", "sync"
ENGS = [PE, DVE, ACT, POOL, SP]


class SemE:
    __slots__ = ("h", "cnt")

    def __init__(self):
        self.h = None
        self.cnt = 0


class Buf:
    __slots__ = ("name", "w", "rs", "sem")

    def __init__(self, name):
        self.name = name
        self.w = None
        self.rs = []
        self.sem = None


class Ev:
    __slots__ = ("kind", "eng", "op", "sem", "val")

    def __init__(self, kind, eng=None, op=None, sem=None, val=0):
        self.kind, self.eng, self.op, self.sem, self.val = kind, eng, op, sem, val


class Op:
    __slots__ = ("eng", "emit", "deps", "inc", "ms", "ev", "kind", "sem")

    def __init__(self, eng, emit, kind="c"):
        self.eng, self.emit, self.kind = eng, emit, kind
        self.deps = []
        self.inc = False
        self.ms = 0
        self.ev = None
        self.sem = None


class Sched:
    def __init__(self, nc, same_sync=True):
        self.nc = nc
        self.es = ExitStack()
        self.ops = {e: [] for e in ENGS}
        self.same_sync = same_sync
        self.evs = []
        self.pend = {e: [] for e in ENGS}
        self.esem = {e: SemE() for e in ENGS}
        self.ccsem = {}
        self.free_sems = []
        self.phase_bufs = []
        self.seen = {e: {} for e in ENGS}
        self.nsem = 0

    def buf(self, name, persist=False):
        b = Buf(name)
        if not persist:
            self.phase_bufs.append(b)
        return b

    def _getsem(self, b):
        if b.sem is None:
            b.sem = self.free_sems.pop() if self.free_sems else SemE()
        return b.sem

    def _deps(self, op, reads, writes):
        deps = []
        for b in reads:
            if b.w is not None:
                deps.append(b.w)
        for b in writes:
            if b.w is not None:
                deps.append(b.w)
            deps.extend(b.rs)
        deps.extend(self.pend[op.eng])
        self.pend[op.eng] = []
        for d in deps:
            if d.kind == "c":
                if d.eng == op.eng and (d.eng == PE or not self.same_sync):
                    continue
                d.op.inc = True
            op.deps.append(d)

    def _fin(self, o, ev, reads, writes):
        o.ev = ev
        for b in reads:
            b.rs.append(ev)
        for b in writes:
            b.w = ev
            b.rs = []
        self.ops[o.eng].append(o)

    def op(self, eng, emit, reads=(), writes=()):
        o = Op(eng, emit)
        self._deps(o, reads, writes)
        self._fin(o, Ev("c", eng=eng, op=o), reads, writes)
        return o

    def dma(self, queue, emit, sembuf, reads=(), writes=()):
        o = Op(queue, emit, kind="d")
        self._deps(o, reads, writes)
        s = self._getsem(sembuf)
        s.cnt += 16
        o.sem = s
        ev = Ev("d", sem=s, val=s.cnt)
        self._fin(o, ev, reads, writes)
        self.evs.append(ev)
        return o

    def dma_multi(self, queue, emits, sembuf, reads=(), writes=()):
        s = self._getsem(sembuf)
        first = True
        for em in emits:
            o = Op(queue, em, kind="d")
            if first:
                self._deps(o, reads, writes)
                first = False
            s.cnt += 16
            o.sem = s
            self.ops[queue].append(o)
        ev = Ev("d", sem=s, val=s.cnt)
        for b in reads:
            b.rs.append(ev)
        for b in writes:
            b.w = ev
            b.rs = []
        self.evs.append(ev)

    def cc(self, emit, semname, reads=(), writes=()):
        o = Op(POOL, emit, kind="k")
        self._deps(o, reads, writes)
        s = self.ccsem.setdefault(semname, SemE())
        s.cnt += 1
        o.sem = s
        ev = Ev("k", sem=s, val=s.cnt)
        self._fin(o, ev, reads, writes)
        self.evs.append(ev)
        return o

    def barrier(self):
        evs = list(self.evs)
        self.evs = []
        for e in ENGS:
            for o in reversed(self.ops[e]):
                if o.kind == "c":
                    evs.append(o.ev)
                    break
        for e in ENGS:
            for ev in evs:
                if ev.kind == "c":
                    if ev.eng == e:
                        continue
                    ev.op.inc = True
                self.pend[e].append(ev)

    def _alloc(self, s):
        if s.h is None:
            s.h = self.es.enter_context(self.nc.semaphore("s%d" % self.nsem))
            self.nsem += 1
        return s.h

    def emit_phase(self):
        self.barrier()
        nc = self.nc
        for e in ENGS:
            c = self.esem[e].cnt
            for o in self.ops[e]:
                if o.kind == "c" and o.inc:
                    c += 1
                    o.ms = c
            self.esem[e].cnt = c
            self._alloc(self.esem[e])
        for e in ENGS:
            for o in self.ops[e]:
                if o.sem is not None:
                    self._alloc(o.sem)
                for d in o.deps:
                    if d.sem is not None:
                        self._alloc(d.sem)

        def waitfor(h, e, d):
            if d.kind == "c":
                sem, val = self.esem[d.eng], d.op.ms
            else:
                sem, val = d.sem, d.val
            if self.seen[e].get(id(sem), 0) >= val:
                return
            self.seen[e][id(sem)] = val
            h.wait_ge(sem.h, val)

        def run_engine(e, h):
            for o in self.ops[e]:
                for d in o.deps:
                    waitfor(h, e, d)
                ins = o.emit(h)
                if o.kind == "c":
                    if o.inc:
                        ins.then_inc(self.esem[e].h, 1)
                elif o.kind == "d":
                    ins.then_inc(o.sem.h, 16)
                else:
                    ins.then_inc(o.sem.h)
            for d in self.pend[e]:
                waitfor(h, e, d)
            self.pend[e] = []

        with nc.Block() as block:
            @block.tensor
            def _(h):
                run_engine(PE, h)

            @block.vector
            def _(h):
                run_engine(DVE, h)

            @block.scalar
            def _(h):
                run_engine(ACT, h)

            @block.gpsimd
            def _(h):
                run_engine(POOL, h)

            @block.sync
            def _(h):
                run_engine(SP, h)
        self.ops = {e: [] for e in ENGS}
        for b in self.phase_bufs:
            if b.sem is not None:
                self.free_sems.append(b.sem)
                b.sem = None
        self.phase_bufs = []


NCORE = 8


class Cfg:
    def __init__(self, D=4096, SEQ=16384, CTX=256, NAW=2048, DFW=2048, CVW=2048, L=2):
        self.D, self.SEQ, self.CTX, self.NAW, self.DFW, self.CVW, self.L = D, SEQ, CTX, NAW, DFW, CVW, L
        self.KC = D // 128
        self.TOK = SEQ // NCORE
        self.T = self.TOK + CTX
        segs = [("na_q", NAW), ("na_k", NAW), ("na_v", NAW), ("na_gate", NAW), ("df_q", DFW), ("df_k", DFW),
                ("df_v", DFW), ("df_gate", DFW), ("cv_val", CVW), ("cv_glu", CVW), ("cv_gate", CVW),
                ("merge_na", D), ("merge_df", D), ("merge_cv", D)]
        self.seg = {}
        o = 0
        for n, s in segs:
            self.seg[n] = (o, s)
            o += s
        self.NIN = o


class GW:
    def __init__(self, nc, S, name, K, N, src, probe=False):
        self.probe = probe
        self.K, self.N = K, N
        self.kcl = K // 128 // NCORE
        self.nst = N // 256
        piece = self.kcl * 128 * 256 * 2
        self.spc = max(1, (512 * 1024) // piece)
        while self.nst % self.spc:
            self.spc -= 1
        self.nch = self.nst // self.spc
        rows = self.spc * self.kcl * 128
        self.rows = rows
        self.snd = [nc.dram_tensor("%s_snd%d" % (name, i), [rows, 256], BF16, kind="Internal").ap() for i in range(self.nch)]
        self.g4 = [nc.dram_tensor("%s_g4_%d" % (name, i), [4 * rows, 256], BF16, kind="Internal").ap() for i in range(self.nch)]
        self.g8 = [nc.dram_tensor("%s_g8_%d" % (name, i), [8 * rows, 256], BF16, kind="Internal").ap() for i in range(self.nch)]
        self.buf = S.buf(name + "_g", persist=True)
        self.src = src
        self.name = name

    def gather(self, nc, S, castbuf):
        bs = S.buf(self.name + "_s")
        b4 = S.buf(self.name + "_4")
        for ci in range(self.nch):
            c0 = 0 if self.probe else ci * self.spc * 256
            src = self.src[:, c0:c0 + self.spc * 256].rearrange("r (s n) -> s r n", n=256)
            dst = self.snd[ci].rearrange("(s r) n -> s r n", s=self.spc)
            cb = castbuf[ci % len(castbuf)]
            S.dma(POOL, lambda h, d=dst, s=src: h.dma_start(out=d, in_=s), cb, writes=[bs, cb])
        for ci in range(self.nch):
            S.cc(lambda h, ci=ci: h.collective_compute("AllGather", ALU.bypass, replica_groups=[[0, 1, 2, 3], [4, 5, 6, 7]],
                                                       ins=[self.snd[ci]], outs=[self.g4[ci]]), "a", reads=[bs], writes=[b4])
        for ci in range(self.nch):
            S.cc(lambda h, ci=ci: h.collective_compute("AllGather", ALU.bypass, replica_groups=[[0, 4], [1, 5], [2, 6], [3, 7]],
                                                       ins=[self.g4[ci]], outs=[self.g8[ci]]), "b", reads=[b4], writes=[self.buf])

    def st_ap(self, st):
        ci, s = divmod(st, self.spc)
        v = self.g8[ci].rearrange("(r s k p) n -> s p r k n", r=NCORE, s=self.spc, k=self.kcl, p=128)
        return v[s]


class GA:
    def __init__(self, nc, S, name, R, C):
        rpc = max(1, min(R, (512 * 1024) // (C * 2)))
        while R % rpc:
            rpc -= 1
        self.rpc, self.nch, self.R, self.C, self.name = rpc, R // rpc, R, C, name
        self.snd = [nc.dram_tensor("%s_snd%d" % (name, i), [rpc, C], BF16, kind="Internal").ap() for i in range(self.nch)]
        self.g4 = [nc.dram_tensor("%s_g4_%d" % (name, i), [4 * rpc, C], BF16, kind="Internal").ap() for i in range(self.nch)]
        self.g8 = [nc.dram_tensor("%s_g8_%d" % (name, i), [8 * rpc, C], BF16, kind="Internal").ap() for i in range(self.nch)]
        self.bs = [S.buf("%s_s%d" % (name, i), persist=True) for i in range(self.nch)]
        self.b4 = S.buf(name + "_4", persist=True)
        self.buf = S.buf(name + "_g", persist=True)

    def snd_rows(self, r0, n):
        ci, o = divmod(r0, self.rpc)
        assert o + n <= self.rpc
        return self.snd[ci][o:o + n]

    def bsc(self, r0):
        return self.bs[r0 // self.rpc]

    def rows(self, rank, r0, n):
        ci, o = divmod(r0, self.rpc)
        assert o + n <= self.rpc
        return self.g8[ci][rank * self.rpc + o: rank * self.rpc + o + n]

    def gather(self, S):
        for ci in range(self.nch):
            S.cc(lambda h, ci=ci: h.collective_compute("AllGather", ALU.bypass, replica_groups=[[0, 1, 2, 3], [4, 5, 6, 7]],
                                                       ins=[self.snd[ci]], outs=[self.g4[ci]]), "a", reads=[self.bs[ci]], writes=[self.b4])
        for ci in range(self.nch):
            S.cc(lambda h, ci=ci: h.collective_compute("AllGather", ALU.bypass, replica_groups=[[0, 4], [1, 5], [2, 6], [3, 7]],
                                                       ins=[self.g4[ci]], outs=[self.g8[ci]]), "b", reads=[self.b4], writes=[self.buf])


def build(cfg, debug=None, stop=99, probe=False):
    nc = bass.Bass("TRN2", target_bir_lowering=False)
    D, KC, T, TOK, CTX, L, NIN = cfg.D, cfg.KC, cfg.T, cfg.TOK, cfg.CTX, cfg.L, cfg.NIN
    inp = {}

    def ein(name, shape, dt=F32):
        inp[name] = nc.dram_tensor(name, list(shape), dt, kind="ExternalInput").ap()
        return inp[name]

    x_in = ein("x", [TOK if not probe else 128, D])
    ctx_in = ein("ctx", [CTX, D])
    c2 = ein("c2", [2, D])
    w_ada = ein("w_ada", [L, D, 3 * D // NCORE if not probe else 512])
    b_ada = ein("b_ada", [L, 3 * D // NCORE])
    w_in = ein("w_in", [L, D // NCORE, NIN if not probe else 512])
    b_in = ein("b_in", [L, NIN])
    ident_in = ein("ident", [128, 128], BF16)
    NAW, DFW, CVW = cfg.NAW, cfg.DFW, cfg.CVW
    cos_in = ein("ropecos", [128, TOK])
    sin_in = ein("ropesin", [128, TOK])
    perm_in = ein("perm", [128, 128], BF16)
    sel_in = ein("sel", [128, 16])
    rowmask_in = ein("rowmask", [5, 128, 8, 128])
    rpb_in = ein("rpbT", [L, NAW // 128, 128, 8, 128])
    lam_in = ein("lamv", [L * 4 * 128])
    subg_in = ein("subg", [L, 256])
    convw_in = ein("conv_w", [L, 31, CVW])
    convb_in = ein("conv_b", [L, CVW])
    convg_in = ein("conv_g", [L, CVW])
    convbt_in = ein("conv_bt", [L, CVW])
    plg_in = ein("post_g", [L * D])
    plb_in = ein("post_b", [L * D])
    wpna_in = ein("w_pna", [L, NAW // NCORE, D])
    wpdf_in = ein("w_pdf", [L, DFW // NCORE, D])
    wpcv_in = ein("w_pcv", [L, CVW // NCORE, D])
    wout_in = ein("w_out", [L, D // NCORE, D])
    out = nc.dram_tensor("out", [TOK, D], F32, kind="ExternalOutput").ap()
    dbg = None
    if debug:
        dbg = nc.dram_tensor("dbg", list(debug), F32, kind="ExternalOutput").ap()

    S = Sched(nc)
    with S.es:
        es0 = S.es
        banks = [es0.enter_context(nc.psum_tensor("bank%d" % i, [128, 512], F32)) for i in range(8)]
        bbank = [S.buf("bank%d" % i, persist=True) for i in range(8)]

        def MM(out_, lhsT, rhs, st, sp, R, W):
            S.op(PE, lambda h: h.matmul(out_, lhsT, rhs, start=st, stop=sp), reads=R, writes=W)

        def AC(out_, in_, func, R, W, **kw):
            S.op(ACT, lambda h: h.activation(out=out_, in_=in_, func=func, **kw), reads=R, writes=W)

        def TT(out_, a, b, op, R, W, eng=DVE):
            S.op(eng, lambda h: h.tensor_tensor(out_, a, b, op), reads=R, writes=W)

        def TS(out_, a, s1, s2, op0, op1, R, W, eng=DVE):
            if op1 is None:
                S.op(eng, lambda h: h.tensor_scalar(out_, a, s1, None, op0), reads=R, writes=W)
            else:
                S.op(eng, lambda h: h.tensor_scalar(out_, a, s1, s2, op0, op1), reads=R, writes=W)

        def STT(out_, a, sc, b, op0, op1, R, W):
            S.op(DVE, lambda h: h.scalar_tensor_tensor(out_, a, sc, b, op0, op1), reads=R, writes=W)

        def LD(out_, in_, sb, R, W, q=SP, slow=False):
            S.dma(q, lambda h: h.dma_start(out=out_, in_=in_, allow_slow_non_contiguous=slow), sb, reads=R, writes=W)

        def RCP(out_, in_, R, W):
            S.op(DVE, lambda h: h.reciprocal(out_, in_), reads=R, writes=W)

        def MS(ap, val, W, eng=POOL):
            S.op(eng, lambda h: h.memset(ap, val), writes=W)

        ident = es0.enter_context(nc.sbuf_tensor("ident_sb", [128, 128], BF16))
        ones32 = es0.enter_context(nc.sbuf_tensor("ones32", [128, 128], F32))
        eps6 = es0.enter_context(nc.sbuf_tensor("eps6", [128, 2], F32))
        bconst = S.buf("const", persist=True)
        castbuf = S.buf("castsem", persist=True)
        castbufs = [S.buf("castsem%d" % i, persist=True) for i in range(4)]
        modsnd = nc.dram_tensor("modsnd", [2 * L, 3 * D // NCORE], F32, kind="Internal").ap()
        mod4 = nc.dram_tensor("mod4", [4 * 2 * L, 3 * D // NCORE], F32, kind="Internal").ap()
        mod8 = nc.dram_tensor("mod8", [8 * 2 * L, 3 * D // NCORE], F32, kind="Internal").ap()
        bmod = S.buf("mod8", persist=True)
        zT = nc.dram_tensor("zT", [NIN, T], BF16, kind="Internal").ap()
        bzT = S.buf("zT", persist=True)
        xcur = nc.dram_tensor("xcur", [T, D], F32, kind="Internal").ap()
        bxcur = S.buf("xcur", persist=True)
        vtok = nc.dram_tensor("vtok", [T, NAW + DFW], BF16, kind="Internal").ap()
        bvtok = S.buf("vtok", persist=True)
        gT = nc.dram_tensor("gT", [NAW + DFW + CVW, T], BF16, kind="Internal").ap()
        bgT = S.buf("gT", persist=True)
        uT = nc.dram_tensor("uT", [CVW, T], BF16, kind="Internal").ap()
        buT = S.buf("uT", persist=True)
        kdf = GA(nc, S, "kdf", DFW, TOK)
        vdf = GA(nc, S, "vdf", TOK, DFW)
        knh = GA(nc, S, "knh", NAW, 512)
        vnh = GA(nc, S, "vnh", 512, NAW)
        cvh = GA(nc, S, "cvh", CVW, 32)

        gw_in = [GW(nc, S, "win%d" % l, D, NIN, w_in[l], probe) for l in range(L)]
        gw_pna = [GW(nc, S, "wpna%d" % l, NAW, D, wpna_in[l]) for l in range(L)]
        gw_pdf = [GW(nc, S, "wpdf%d" % l, DFW, D, wpdf_in[l]) for l in range(L)]
        gw_pcv = [GW(nc, S, "wpcv%d" % l, CVW, D, wpcv_in[l]) for l in range(L)]
        gw_out = [GW(nc, S, "wout%d" % l, D, D, wout_in[l]) for l in range(L)]
        with ExitStack() as es:
            S.op(POOL, lambda h: h.memset(ones32[:], 1.0), writes=[bconst])
            S.op(POOL, lambda h: h.memset(eps6[:, 0:1], 1e-6), writes=[bconst])
            S.op(POOL, lambda h: h.memset(eps6[:, 1:2], 1e-5), writes=[bconst])
            S.dma(SP, lambda h: h.dma_start(out=ident[:], in_=ident_in), bconst, writes=[bconst])
            import os
            gw_in[0].gather(nc, S, castbufs)
            late_gathers = [gw_pna[0], gw_pdf[0], gw_pcv[0], gw_out[0]]
            for l_ in range(1, L):
                late_gathers += [gw_in[l_], gw_pna[l_], gw_pdf[l_], gw_pcv[l_], gw_out[l_]]
            for r0 in range(0, TOK, 128):
                S.dma(SP, lambda h, r0=r0: h.dma_start(out=xcur[r0:r0 + 128, :], in_=x_in[(0 if probe else r0):(0 if probe else r0) + 128, :]), castbuf, writes=[bxcur])
            S.dma(SP, lambda h: h.dma_start(out=xcur[TOK:T, :], in_=ctx_in), castbuf, writes=[bxcur])
            S.emit_phase()

        NA3 = 3 * D // NCORE
        if stop < 1:
            return nc
        with ExitStack() as es:
            cT = es.enter_context(nc.sbuf_tensor("cT", [128, KC, 2], F32))
            bcT = S.buf("cT")
            wa = [es.enter_context(nc.sbuf_tensor("wa%d" % i, [128, 8, 512], F32)) for i in range(2)]
            bwa = [S.buf("wa%d" % i) for i in range(2)]
            mo = es.enter_context(nc.sbuf_tensor("mo", [2, L, NA3], F32))
            bb = es.enter_context(nc.sbuf_tensor("bb", [2, L, NA3], F32))
            bmo = S.buf("mo")
            bbb = S.buf("bb")
            for v in range(2):
                S.dma(SP, lambda h, v=v: h.dma_start(out=cT[:, :, v], in_=c2[v].rearrange("(k p) -> p k", p=128), allow_slow_non_contiguous=True),
                      bcT, writes=[bcT])
            for v in range(2):
                S.dma(SP, lambda h, v=v: h.dma_start(out=bb[v:v + 1], in_=b_ada.rearrange("(o l) n -> o l n", o=1)), bbb, writes=[bbb])
            S.op(ACT, lambda h: h.activation(out=cT[:], in_=cT[:], func=AF.Silu), reads=[bcT], writes=[bcT])
            it = 0
            for l in range(L):
                for n0 in range(0, NA3, 512):
                    nn = min(512, NA3 - n0)
                    for k0 in range(0, KC, 8):
                        kk = min(8, KC - k0)
                        sl = it % 2
                        it += 1
                        S.dma(SP, lambda h, sl=sl, l=l, n0=n0, nn=nn, k0=k0, kk=kk: h.dma_start(
                            out=wa[sl][:, 0:kk, 0:nn],
                            in_=w_ada[l, k0 * 128:(k0 + kk) * 128, (0 if probe else n0):(0 if probe else n0) + nn].rearrange("(k p) n -> p k n", p=128)),
                            bwa[sl], writes=[bwa[sl]])
                        for k in range(kk):
                            S.op(PE, lambda h, sl=sl, k=k, k0=k0, nn=nn: h.matmul(banks[0][0:2, 0:nn], cT[:, k0 + k, :], wa[sl][:, k, 0:nn],
                                                                                   start=(k0 + k == 0), stop=(k0 + k == KC - 1)),
                                 reads=[bcT, bwa[sl]], writes=[bbank[0]])
                    S.op(DVE, lambda h, l=l, n0=n0, nn=nn: h.tensor_tensor(mo[:, l, n0:n0 + nn], banks[0][0:2, 0:nn], bb[:, l, n0:n0 + nn], ALU.add),
                         reads=[bbank[0], bbb], writes=[bmo])
            bms = S.buf("modsnd")
            bm4 = S.buf("mod4")
            S.dma(SP, lambda h: h.dma_start(out=modsnd.rearrange("(v l) n -> v l n", v=2), in_=mo[:]), bmo, reads=[bmo], writes=[bms])
            S.cc(lambda h: h.collective_compute("AllGather", ALU.bypass, replica_groups=[[0, 1, 2, 3], [4, 5, 6, 7]], ins=[modsnd], outs=[mod4]),
                 "a", reads=[bms], writes=[bm4])
            S.cc(lambda h: h.collective_compute("AllGather", ALU.bypass, replica_groups=[[0, 4], [1, 5], [2, 6], [3, 7]], ins=[mod4], outs=[mod8]),
                 "b", reads=[bm4], writes=[bmod])
            S.emit_phase()

        if stop < 2:
            return nc
        def mod_cols(v, l, j0, n):
            res = []
            j = j0
            while j < j0 + n:
                r, o = divmod(j, NA3)
                m = min(NA3 - o, j0 + n - j)
                row = (r * 2 + v) * L + l
                res.append((mod8[row, o:o + m], j - j0, m))
                j += m
            return res

        for l in range(L):
            with ExitStack() as es:
                hT = es.enter_context(nc.sbuf_tensor("hT_%d" % l, [128, KC, T], BF16))
                bhT = S.buf("hT")
                sc1 = es.enter_context(nc.sbuf_tensor("sc1_%d" % l, [128, 2, KC], F32))
                sh1 = es.enter_context(nc.sbuf_tensor("sh1_%d" % l, [128, 2, KC], F32))
                bsc = S.buf("sc")
                for v in range(2):
                    for (ap, o, m) in mod_cols(v, l, D, D):
                        S.dma(SP, lambda h, ap=ap, o=o, m=m, v=v: h.dma_start(out=sc1[:, v, o // 128:(o + m) // 128],
                                                                               in_=ap.rearrange("(k p) -> p k", p=128), allow_slow_non_contiguous=True),
                              bsc, reads=[bmod], writes=[bsc])
                    for (ap, o, m) in mod_cols(v, l, 0, D):
                        S.dma(SP, lambda h, ap=ap, o=o, m=m, v=v: h.dma_start(out=sh1[:, v, o // 128:(o + m) // 128],
                                                                               in_=ap.rearrange("(k p) -> p k", p=128), allow_slow_non_contiguous=True),
                              bsc, reads=[bmod], writes=[bsc])
                S.op(DVE, lambda h: h.tensor_scalar(sc1[:], sc1[:], 1.0, None, ALU.add), reads=[bsc], writes=[bsc])
                es2 = ExitStack()
                xt = [es2.enter_context(nc.sbuf_tensor("xt%d_%d" % (i, l), [128, D], F32)) for i in range(2)]
                bxt = [S.buf("xt%d" % i) for i in range(2)]
                xn1 = es2.enter_context(nc.sbuf_tensor("xn_%d" % l, [128, D], BF16))
                xn = [xn1, xn1]
                bxn1 = S.buf("xn")
                bxn = [bxn1, bxn1]
                st = es2.enter_context(nc.sbuf_tensor("st_%d" % l, [128, 8, 6], F32))
                mv = es2.enter_context(nc.sbuf_tensor("mv_%d" % l, [128, 2], F32))
                bst = S.buf("st")
                NT = T // 128
                nch = (D + 511) // 512
                for t in range(NT):
                    sl = t % 2
                    v = 0 if t * 128 < TOK else 1
                    S.dma(SP, lambda h, t=t, sl=sl: h.dma_start(out=xt[sl][:], in_=xcur[t * 128:(t + 1) * 128, :]), bxt[sl],
                          reads=[bxcur], writes=[bxt[sl]])
                    for c in range(nch):
                        S.op(DVE, lambda h, c=c, sl=sl: h.bn_stats(st[:, c, :], xt[sl][:, c * 512:(c + 1) * 512]), reads=[bxt[sl]], writes=[bst])
                    S.op(DVE, lambda h: h.bn_aggr(mv[:], st[:, 0:nch, :].rearrange("p c s -> p (c s)")), reads=[bst], writes=[bst])
                    S.op(ACT, lambda h: h.activation(out=mv[:, 1:2], in_=mv[:, 1:2], func=AF.Sqrt, bias=eps6[:, 0:1]), reads=[bst, bconst], writes=[bst])
                    S.op(DVE, lambda h: h.reciprocal(mv[:, 1:2], mv[:, 1:2]), reads=[bst], writes=[bst])
                    S.op(DVE, lambda h, sl=sl: h.tensor_scalar(xn[sl][:], xt[sl][:], mv[:, 0:1], mv[:, 1:2], ALU.subtract, ALU.mult),
                         reads=[bxt[sl], bst], writes=[bxn[sl]])
                    for k in range(KC):
                        bk = 1 + k % 6
                        S.op(PE, lambda h, k=k, sl=sl, bk=bk: h.matmul(banks[bk][:, 0:128], xn[sl][:, k * 128:(k + 1) * 128], ident[:], start=True, stop=True),
                             reads=[bxn[sl], bconst], writes=[bbank[bk]])
                        S.op(ACT, lambda h, k=k, t=t, v=v, bk=bk: h.activation(out=hT[:, k, t * 128:(t + 1) * 128], in_=banks[bk][:, 0:128],
                                                                                func=AF.Identity, scale=sc1[:, v, k:k + 1], bias=sh1[:, v, k:k + 1]),
                             reads=[bbank[bk], bsc], writes=[bhT])
                S.emit_phase()
                es2.close()
                if stop < 3:
                    return nc
                wt = [es.enter_context(nc.sbuf_tensor("wt%d_%d" % (i, l), [128, KC, 256], BF16)) for i in range(2)]
                bwt = [S.buf("wt%d" % i) for i in range(2)]
                zt = [es.enter_context(nc.sbuf_tensor("zt%d_%d" % (i, l), [128, T], BF16)) for i in range(2)]
                bzt = [S.buf("zt%d" % i) for i in range(2)]
                bia = es.enter_context(nc.sbuf_tensor("bia_%d" % l, [128, NIN // 128], F32))
                bbia = S.buf("bia")
                S.dma(SP, lambda h, l=l: h.dma_start(out=bia[:], in_=b_in[l].rearrange("(n p) -> p n", p=128), allow_slow_non_contiguous=True),
                      bbia, writes=[bbia])
                g = gw_in[l]
                tch = [(t0, min(512, T - t0)) for t0 in range(0, T, 512)]
                cnt = 0
                for s_ in range(g.nst):
                    sl = s_ % 2
                    S.dma_multi(SP, [lambda h, s_=s_, sl=sl, r=r: h.dma_start(out=wt[sl][:, r * g.kcl:(r + 1) * g.kcl, :], in_=g.st_ap(s_)[:, r])
                                     for r in range(NCORE)], bwt[sl], reads=[g.buf], writes=[bwt[sl]])
                    for half in range(2):
                        nt = s_ * 2 + half
                        zs = nt % 2
                        for ci, (t0, tn) in enumerate(tch):
                            bk = 1 + (cnt % 6)
                            cnt += 1
                            for k in range(KC):
                                S.op(PE, lambda h, k=k, sl=sl, half=half, t0=t0, tn=tn, bk=bk: h.matmul(
                                    banks[bk][:, 0:tn], wt[sl][:, k, half * 128:(half + 1) * 128], hT[:, k, t0:t0 + tn], start=(k == 0), stop=(k == KC - 1)),
                                    reads=[bwt[sl], bhT], writes=[bbank[bk]])
                            S.op(ACT, lambda h, zs=zs, t0=t0, tn=tn, bk=bk, nt=nt: h.activation(out=zt[zs][:, t0:t0 + tn], in_=banks[bk][:, 0:tn],
                                                                                             func=AF.Identity, bias=bia[:, nt:nt + 1]),
                                 reads=[bbank[bk], bbia], writes=[bzt[zs]])
                        S.dma(POOL, lambda h, zs=zs, nt=nt: h.dma_start(out=zT[nt * 128:(nt + 1) * 128, :], in_=zt[zs][:]), bzt[zs],
                              reads=[bzt[zs]], writes=[bzT])
                bbcs = [es.enter_context(nc.sbuf_tensor("bbc%d_%d" % (i, l), [128, 256], F32)) for i in range(2)]
                bbbc = [S.buf("bbc%d" % i) for i in range(2)]
                vts = [es.enter_context(nc.sbuf_tensor("vts%d_%d" % (i, l), [128, 256], BF16)) for i in range(2)]
                bvts = [S.buf("vts%d" % i) for i in range(2)]
                vit = 0
                sidx = g.nst
                for (so, vo, sz) in ((cfg.seg["na_v"][0], 0, NAW), (cfg.seg["df_v"][0], NAW, DFW)):
                    for s2 in range(sz // 256):
                        st_ = (so + s2 * 256) // 256
                        sl = sidx % 2
                        sidx += 1
                        S.dma_multi(SP, [lambda h, st_=st_, sl=sl, r=r: h.dma_start(out=wt[sl][:, r * g.kcl:(r + 1) * g.kcl, :], in_=g.st_ap(st_)[:, r])
                                         for r in range(NCORE)], bwt[sl], reads=[g.buf], writes=[bwt[sl]])
                        LD(bbcs[sl][:], bass.AP(b_in.tensor, l * NIN + so + s2 * 256, [[0, 128], [1, 256]]), bbbc[sl], [], [bbbc[sl]])
                        for t in range(NT):
                            bk = 1 + (cnt % 6)
                            cnt += 1
                            v_ = vit % 2
                            vit += 1
                            for k in range(KC):
                                MM(banks[bk][:, 0:256], hT[:, k, t * 128:(t + 1) * 128], wt[sl][:, k, :], k == 0, k == KC - 1, [bhT, bwt[sl]], [bbank[bk]])
                            TT(vts[v_][:], banks[bk][:, 0:256], bbcs[sl][:], ALU.add, [bbank[bk], bbbc[sl]], [bvts[v_]])
                            LD(vtok[t * 128:(t + 1) * 128, vo + s2 * 256:vo + (s2 + 1) * 256], vts[v_][:], bvts[v_], [bvts[v_]], [bvtok], q=POOL)
                S.emit_phase()
            last = (l == L - 1)
            TE = TOK if last else T
            ROWS = TOK // 64
            NP = ROWS // 2
            NH = NAW // 128
            HD = DFW // 256
            CT = CVW // 128
            sc_att = 128.0 ** -0.5
            o_naq, o_nak, o_nag = cfg.seg["na_q"][0], cfg.seg["na_k"][0], cfg.seg["na_gate"][0]
            o_dfq, o_dfk, o_dfg = cfg.seg["df_q"][0], cfg.seg["df_k"][0], cfg.seg["df_gate"][0]
            o_cvv, o_cvg, o_cvgate = cfg.seg["cv_val"][0], cfg.seg["cv_glu"][0], cfg.seg["cv_gate"][0]
            o_mg = [cfg.seg["merge_na"][0], cfg.seg["merge_df"][0], cfg.seg["merge_cv"][0]]
            tch = [(t0, min(512, TOK - t0)) for t0 in range(0, TOK, 512)]

            with ExitStack() as es:
                cosT = es.enter_context(nc.sbuf_tensor("cosT_%d" % l, [128, TOK], F32))
                sinT = es.enter_context(nc.sbuf_tensor("sinT_%d" % l, [128, TOK], F32))
                perm = es.enter_context(nc.sbuf_tensor("perm_%d" % l, [128, 128], BF16))
                btab = S.buf("ropetab")
                LD(cosT[:], cos_in, btab, [], [btab])
                LD(sinT[:], sin_in, btab, [], [btab])
                LD(perm[:], perm_in, btab, [], [btab])
                xr = [es.enter_context(nc.sbuf_tensor("xr%d_%d" % (i, l), [128, TOK], BF16)) for i in range(2)]
                bxr = [S.buf("xr%d" % i) for i in range(2)]
                xo = [es.enter_context(nc.sbuf_tensor("xo%d_%d" % (i, l), [128, TOK], BF16)) for i in range(2)]
                bxo = [S.buf("xo%d" % i) for i in range(2)]
                t1 = es.enter_context(nc.sbuf_tensor("rt1_%d" % l, [128, 512], F32))
                t2 = es.enter_context(nc.sbuf_tensor("rt2_%d" % l, [128, 512], F32))
                bt1, bt2 = S.buf("rt1"), S.buf("rt2")
                it = 0
                for (so, isk) in ((o_dfq, False), (o_dfk, True)):
                    for hh in range(DFW // 128):
                        sl = it % 2
                        it += 1
                        LD(xr[sl][:], zT[so + hh * 128: so + (hh + 1) * 128, 0:TOK], bxr[sl], [bzT], [bxr[sl]])
                        for ci, (t0, tn) in enumerate(tch):
                            bk = ci % 2
                            MM(banks[bk][:, 0:tn], perm[:], xr[sl][:, t0:t0 + tn], True, True, [btab, bxr[sl]], [bbank[bk]])
                            TT(t1[:, 0:tn], xr[sl][:, t0:t0 + tn], cosT[:, t0:t0 + tn], ALU.mult, [bxr[sl], btab], [bt1])
                            TT(t2[:, 0:tn], banks[bk][:, 0:tn], sinT[:, t0:t0 + tn], ALU.mult, [bbank[bk], btab], [bt2])
                            TT(xo[sl][:, t0:t0 + tn], t1[:, 0:tn], t2[:, 0:tn], ALU.add, [bt1, bt2], [bxo[sl]])
                        if isk:
                            LD(kdf.snd_rows(hh * 128, 128), xo[sl][:], bxo[sl], [bxo[sl]], [kdf.bsc(hh * 128)], q=POOL)
                        else:
                            LD(zT[so + hh * 128: so + (hh + 1) * 128, 0:TOK], xo[sl][:], bxo[sl], [bxo[sl]], [bzT], q=POOL)
                for r0 in range(0, TOK, vdf.rpc):
                    LD(vdf.snd_rows(r0, vdf.rpc), vtok[r0:r0 + vdf.rpc, NAW:NAW + DFW], castbuf, [bvtok], [vdf.bsc(r0)])
                for r0 in range(0, NAW, knh.rpc):
                    LD(knh.snd_rows(r0, knh.rpc)[:, 0:256], zT[o_nak + r0:o_nak + r0 + knh.rpc, 0:256], castbuf, [bzT], [knh.bsc(r0)])
                    LD(knh.snd_rows(r0, knh.rpc)[:, 256:512], zT[o_nak + r0:o_nak + r0 + knh.rpc, TOK - 256:TOK], castbuf, [bzT], [knh.bsc(r0)])
                for r0 in range(0, 512, vnh.rpc):
                    tk0 = r0 if r0 < 256 else TOK - 512 + r0
                    LD(vnh.snd_rows(r0, vnh.rpc), vtok[tk0:tk0 + vnh.rpc, 0:NAW], castbuf, [bvtok], [vnh.bsc(r0)])
                uv = [es.enter_context(nc.sbuf_tensor("uv%d_%d" % (i, l), [128, T], BF16)) for i in range(2)]
                ug = [es.enter_context(nc.sbuf_tensor("ug%d_%d" % (i, l), [128, T], BF16)) for i in range(2)]
                usg = es.enter_context(nc.sbuf_tensor("usg_%d" % l, [128, T], F32))
                uo = [es.enter_context(nc.sbuf_tensor("uo%d_%d" % (i, l), [128, T], BF16)) for i in range(2)]
                buv = [S.buf("uv%d" % i) for i in range(2)]
                bug = [S.buf("ug%d" % i) for i in range(2)]
                buo = [S.buf("uo%d" % i) for i in range(2)]
                busg = S.buf("usg")
                for ct in range(CT):
                    sl = ct % 2
                    LD(uv[sl][:], zT[o_cvv + ct * 128:o_cvv + (ct + 1) * 128, :], buv[sl], [bzT], [buv[sl]])
                    LD(ug[sl][:], zT[o_cvg + ct * 128:o_cvg + (ct + 1) * 128, :], bug[sl], [bzT], [bug[sl]])
                    AC(usg[:], ug[sl][:], AF.Sigmoid, [bug[sl]], [busg])
                    TT(uo[sl][:], uv[sl][:], usg[:], ALU.mult, [buv[sl], busg], [buo[sl]])
                    LD(uT[ct * 128:(ct + 1) * 128, :], uo[sl][:], buo[sl], [buo[sl]], [buT], q=POOL)
                    LD(cvh.snd_rows(ct * 128, 128)[:, 0:15], uo[sl][:, 0:15], buo[sl], [buo[sl]], [cvh.bsc(ct * 128)], q=POOL, slow=True)
                    LD(cvh.snd_rows(ct * 128, 128)[:, 16:31], uo[sl][:, TOK - 15:TOK], buo[sl], [buo[sl]], [cvh.bsc(ct * 128)], q=POOL, slow=True)
                for ga in (knh, vnh, cvh):
                    ga.gather(S)
                S.emit_phase()
            if stop < 4:
                return nc

            with ExitStack() as es:
                NB = NP + 8
                sel = es.enter_context(nc.sbuf_tensor("sel_%d" % l, [128, 16], F32))
                rm = es.enter_context(nc.sbuf_tensor("rm_%d" % l, [128, 5, 1024], F32))
                onesb = es.enter_context(nc.sbuf_tensor("onesb_%d" % l, [128, 128], BF16))
                bsel = S.buf("sel")
                LD(sel[:], sel_in, bsel, [], [bsel])
                LD(rm[:], rowmask_in.rearrange("s p c q -> p s (c q)"), bsel, [], [bsel])
                MS(onesb[:], 1.0, [bsel])
                for ga in (kdf, vdf):
                    ga.gather(S)
                rp = [es.enter_context(nc.sbuf_tensor("rp%d_%d" % (i, l), [128, 1024], F32)) for i in range(2)]
                brp = [S.buf("rp%d" % i) for i in range(2)]
                B5 = es.enter_context(nc.sbuf_tensor("B5_%d" % l, [128, 5, 1024], F32))
                bB5 = S.buf("B5")
                kbuf = es.enter_context(nc.sbuf_tensor("kbuf_%d" % l, [128, NB * 128], BF16))
                vbuf = es.enter_context(nc.sbuf_tensor("vbuf_%d" % l, [128, NB, 128], BF16))
                bkb, bvb = S.buf("kbuf"), S.buf("vbuf")
                MS(kbuf[:], 0.0, [bkb])
                MS(vbuf[:], 0.0, [bvb])
                hk = es.enter_context(nc.sbuf_tensor("hk_%d" % l, [128, 8, 512], BF16))
                hv = es.enter_context(nc.sbuf_tensor("hv_%d" % l, [128, 8, 4, 128], BF16))
                bhk, bhv = S.buf("hk"), S.buf("hv")
                kcs = es.enter_context(nc.sbuf_tensor("kcs_%d" % l, [128, CTX], BF16))
                vcs = es.enter_context(nc.sbuf_tensor("vcs_%d" % l, [128, CTX // 128, 128], BF16))
                bkc = S.buf("kcs")
                qT = es.enter_context(nc.sbuf_tensor("qTn_%d" % l, [128, T], BF16))
                gg = es.enter_context(nc.sbuf_tensor("ggn_%d" % l, [128, T], BF16))
                gsil = es.enter_context(nc.sbuf_tensor("gsn_%d" % l, [128, T], F32))
                bq, bgg, bgs = S.buf("qTn"), S.buf("ggn"), S.buf("gsn")
                oh = [es.enter_context(nc.sbuf_tensor("oh%d_%d" % (i, l), [128, T], BF16)) for i in range(2)]
                boh = [S.buf("oh%d" % i) for i in range(2)]
                ein_ = es.enter_context(nc.sbuf_tensor("ein_%d" % l, [128, 1024], F32))
                bein = S.buf("ein")
                E = [es.enter_context(nc.sbuf_tensor("E%d_%d" % (i, l), [128, 1024 + CTX], BF16)) for i in range(2)]
                bE = [S.buf("E%d" % i) for i in range(2)]
                rinv = es.enter_context(nc.sbuf_tensor("rinv_%d" % l, [128, 128], F32))
                otmp = es.enter_context(nc.sbuf_tensor("otmp_%d" % l, [128, 128], F32))
                bri = S.buf("rinv")
                nctx = CTX // 128
                unit = 0
                for h in range(NH):
                    hs = h % 2
                    LD(rp[hs][:], rpb_in[l, h].rearrange("p c q -> p (c q)"), brp[hs], [], [brp[hs]])
                    for s_ in range(5):
                        TT(B5[:, s_, :], rp[hs][:], rm[:, s_, :], ALU.add, [brp[hs], bsel], [bB5], eng=POOL)
                    LD(kbuf[:, 512:512 + TOK], zT[o_nak + h * 128:o_nak + (h + 1) * 128, 0:TOK], bkb, [bzT], [bkb])
                    LD(vbuf[:, 4:4 + NP, :], vtok[0:TOK, h * 128:(h + 1) * 128].rearrange("(c p) d -> p c d", p=128), bvb, [bvtok], [bvb])
                    LD(kcs[:], zT[o_nak + h * 128:o_nak + (h + 1) * 128, TOK:T], bkc, [bzT], [bkc])
                    LD(vcs[:], vtok[TOK:T, h * 128:(h + 1) * 128].rearrange("(c p) d -> p c d", p=128), bkc, [bvtok], [bkc])
                    S.dma_multi(SP, [(lambda hh, r=r, h=h: hh.dma_start(out=hk[:, r, :], in_=knh.rows(r, h * 128, 128))) for r in range(NCORE)],
                                bhk, reads=[knh.buf], writes=[bhk])
                    S.dma_multi(SP, [(lambda hh, r=r, h=h, c0=c0: hh.dma_start(out=hv[:, r, c0 // 128:(c0 + vnh.rpc) // 128, :],
                                                                                 in_=vnh.rows(r, c0, vnh.rpc)[:, h * 128:(h + 1) * 128]
                                                                                 .rearrange("(c p) d -> p c d", p=128)))
                                     for r in range(NCORE) for c0 in range(0, 512, vnh.rpc)],
                                bhv, reads=[vnh.buf], writes=[bhv])
                    for r in range(NCORE):
                        ka, kb_ = kbuf[:, 256:512], kbuf[:, 512 + TOK:512 + TOK + 256]
                        va, vb_ = vbuf[:, 2:4, :], vbuf[:, 4 + NP:6 + NP, :]
                        if r == 0:
                            TS(ka, hk[:, r, 256:512], sel[:, r:r + 1], None, ALU.mult, None, [bhk, bsel], [bkb])
                            TS(kb_, hk[:, r, 0:256], sel[:, 8 + r:9 + r], None, ALU.mult, None, [bhk, bsel], [bkb])
                            TS(va, hv[:, r, 2:4, :], sel[:, r:r + 1], None, ALU.mult, None, [bhv, bsel], [bvb])
                            TS(vb_, hv[:, r, 0:2, :], sel[:, 8 + r:9 + r], None, ALU.mult, None, [bhv, bsel], [bvb])
                        else:
                            STT(ka, hk[:, r, 256:512], sel[:, r:r + 1], ka, ALU.mult, ALU.add, [bhk, bsel], [bkb])
                            STT(kb_, hk[:, r, 0:256], sel[:, 8 + r:9 + r], kb_, ALU.mult, ALU.add, [bhk, bsel], [bkb])
                            STT(va, hv[:, r, 2:4, :], sel[:, r:r + 1], va, ALU.mult, ALU.add, [bhv, bsel], [bvb])
                            STT(vb_, hv[:, r, 0:2, :], sel[:, 8 + r:9 + r], vb_, ALU.mult, ALU.add, [bhv, bsel], [bvb])
                    LD(qT[:], zT[o_naq + h * 128:o_naq + (h + 1) * 128, :], bq, [bzT], [bq])
                    LD(gg[:], zT[o_nag + h * 128:o_nag + (h + 1) * 128, :], bgg, [bzT], [bgg])
                    AC(gsil[:], gg[:], AF.Silu, [bgg], [bgs])
                    qtiles = [(p, True) for p in range(NP)] + ([] if last else [(j, False) for j in range(nctx)])
                    for (p, local) in qtiles:
                        sb = 3 * (unit % 2)
                        es_ = unit % 2
                        unit += 1
                        bS0, bS1, bC = banks[sb], banks[sb + 1], banks[sb + 2]
                        q0 = p * 128 if local else TOK + p * 128
                        qap = qT[:, q0:q0 + 128]
                        if local:
                            slot = 0 if p == 0 else 1 if p == 1 else 3 if p == NP - 2 else 4 if p == NP - 1 else 2
                            for c in range(8):
                                bS = bS0 if c < 4 else bS1
                                MM(bS[:, (c % 4) * 128:(c % 4 + 1) * 128], kbuf[:, (p + c) * 128:(p + c + 1) * 128], qap, True, True,
                                   [bkb, bq], [bbank[sb + c // 4]])
                        for j in range(nctx):
                            MM(bC[:, j * 128:(j + 1) * 128], kcs[:, j * 128:(j + 1) * 128], qap, True, True, [bkc, bq], [bbank[sb + 2]])
                        if local:
                            for half in range(2):
                                STT(ein_[:, half * 512:(half + 1) * 512], banks[sb + half][:], sc_att, B5[:, slot, half * 512:(half + 1) * 512],
                                    ALU.mult, ALU.add, [bbank[sb + half], bB5], [bein])
                            AC(E[es_][:, 0:1024], ein_[:], AF.Exp, [bein], [bE[es_]])
                        AC(E[es_][:, 1024:1024 + CTX], bC[:, 0:CTX], AF.Exp, [bbank[sb + 2]], [bE[es_]], scale=sc_att)
                        chunks = ([(E[es_][:, c * 128:(c + 1) * 128], vbuf[:, p + c, :]) for c in range(8)] if local else []) + \
                                 [(E[es_][:, 1024 + j * 128:1024 + (j + 1) * 128], vcs[:, j, :]) for j in range(nctx)]
                        for i_, (eap, vap) in enumerate(chunks):
                            MM(bC[:, 256:384], onesb[:], eap, i_ == 0, i_ == len(chunks) - 1, [bsel, bE[es_]], [bbank[sb + 2]])
                        for i_, (eap, vap) in enumerate(chunks):
                            MM(bC[:, 384:512], vap, eap, i_ == 0, i_ == len(chunks) - 1, [bvb, bkc, bE[es_]], [bbank[sb + 2]])
                        RCP(rinv[:], bC[:, 256:384], [bbank[sb + 2]], [bri])
                        TT(otmp[:], bC[:, 384:512], rinv[:], ALU.mult, [bbank[sb + 2], bri], [bri])
                        TT(oh[hs][:, q0:q0 + 128], otmp[:], gsil[:, q0:q0 + 128], ALU.mult, [bri, bgs], [boh[hs]])
                    LD(gT[h * 128:(h + 1) * 128, 0:TE], oh[hs][:, 0:TE], boh[hs], [boh[hs]], [bgT], q=SP)
                S.emit_phase()
            if stop < 5:
                return nc

            lam_init = 0.8 - 0.6 * float(np.exp(-0.3 * l))
            with ExitStack() as es:
                onesb = es.enter_context(nc.sbuf_tensor("onesd_%d" % l, [128, 128], BF16))
                bcn = S.buf("dconst")
                MS(onesb[:], 1.0, [bcn])
                if l == 0:
                    for g_ in late_gathers:
                        g_.gather(nc, S, castbufs)
                lv = es.enter_context(nc.sbuf_tensor("lv_%d" % l, [128, 4, 128], F32))
                lsc = es.enter_context(nc.sbuf_tensor("lsc_%d" % l, [128, 4], F32))
                gcol = es.enter_context(nc.sbuf_tensor("gcol_%d" % l, [128, 2], F32))
                for i_ in range(4):
                    LD(lv[:, i_, :], bass.AP(lam_in.tensor, (l * 4 + i_) * 128, [[0, 128], [1, 128]]), bcn, [], [bcn])
                LD(gcol[:], subg_in[l].rearrange("(c p) -> p c", p=128), bcn, [], [bcn], slow=True)
                TT(lv[:, 0, :], lv[:, 0, :], lv[:, 1, :], ALU.mult, [bcn], [bcn])
                TT(lv[:, 2, :], lv[:, 2, :], lv[:, 3, :], ALU.mult, [bcn], [bcn])
                S.op(DVE, lambda hh: hh.tensor_reduce(lsc[:, 0:1], lv[:, 0, :], AX.X, ALU.add), reads=[bcn], writes=[bcn])
                S.op(DVE, lambda hh: hh.tensor_reduce(lsc[:, 1:2], lv[:, 2, :], AX.X, ALU.add), reads=[bcn], writes=[bcn])
                AC(lsc[:, 0:2], lsc[:, 0:2], AF.Exp, [bcn], [bcn])
                TT(lsc[:, 2:3], lsc[:, 0:1], lsc[:, 1:2], ALU.subtract, [bcn], [bcn])
                TS(lsc[:, 2:3], lsc[:, 2:3], lam_init, -1.0, ALU.add, ALU.mult, [bcn], [bcn])
                TS(gcol[:], gcol[:], 1.0 - lam_init, None, ALU.mult, None, [bcn], [bcn])
                qT = es.enter_context(nc.sbuf_tensor("qTd_%d" % l, [128, 2, T], BF16))
                bq = S.buf("qTd")
                gg = es.enter_context(nc.sbuf_tensor("ggd_%d" % l, [128, 2, T], BF16))
                gsil = es.enter_context(nc.sbuf_tensor("gsd_%d" % l, [128, 2, T], F32))
                bgg, bgs = S.buf("ggd"), S.buf("gsd")
                kp = [es.enter_context(nc.sbuf_tensor("kp%d_%d" % (i, l), [128, 2, TOK], BF16)) for i in range(2)]
                vp = [es.enter_context(nc.sbuf_tensor("vp%d_%d" % (i, l), [128, TOK // 128, 256], BF16)) for i in range(2)]
                bkp = [S.buf("kp%d" % i) for i in range(2)]
                bvp = [S.buf("vp%d" % i) for i in range(2)]
                kc_ = es.enter_context(nc.sbuf_tensor("kcd_%d" % l, [128, 2, CTX], BF16))
                vc_ = es.enter_context(nc.sbuf_tensor("vcd_%d" % l, [128, CTX // 128, 256], BF16))
                bkc = S.buf("kcd")
                Et = [es.enter_context(nc.sbuf_tensor("Et%d_%d" % (i, l), [128, 512], BF16)) for i in range(6)]
                bEt = [S.buf("Et%d" % i) for i in range(6)]
                accs = [es.enter_context(nc.sbuf_tensor("acc%d_%d" % (i, l), [128, 512], F32)) for i in range(4)]
                baccs = [S.buf("acc%d" % i) for i in range(4)]
                fr = es.enter_context(nc.sbuf_tensor("fr_%d" % l, [128, 2, 512], F32))
                fo = es.enter_context(nc.sbuf_tensor("fo_%d" % l, [128, 2, 512], F32))
                fu = es.enter_context(nc.sbuf_tensor("fu_%d" % l, [128, 512], F32))
                ft_ = es.enter_context(nc.sbuf_tensor("ft_%d" % l, [128, 512], F32))
                fsq = es.enter_context(nc.sbuf_tensor("fsq_%d" % l, [128, 512], F32))
                frn = es.enter_context(nc.sbuf_tensor("frn_%d" % l, [128, 512], F32))
                fg = [es.enter_context(nc.sbuf_tensor("fg%d_%d" % (i, l), [128, 512], BF16)) for i in range(2)]
                bfr, bfo, bfu, bft, bfsq, bfrn = S.buf("fr"), S.buf("fo"), S.buf("fu"), S.buf("ft"), S.buf("fsq"), S.buf("frn")
                bfg = [S.buf("fg%d" % i) for i in range(2)]
                pit = 0
                eit = 0
                git = 0
                sit = 0
                for hd in range(HD):
                    for half in range(2):
                        LD(qT[:, half, :], zT[o_dfq + (2 * hd + half) * 128:o_dfq + (2 * hd + half + 1) * 128, :], bq, [bzT], [bq])
                        LD(gg[:, half, :], zT[o_dfg + (2 * hd + half) * 128:o_dfg + (2 * hd + half + 1) * 128, :], bgg, [bzT], [bgg])
                        LD(kc_[:, half, :], zT[o_dfk + (2 * hd + half) * 128:o_dfk + (2 * hd + half + 1) * 128, TOK:T], bkc, [bzT], [bkc])
                    LD(vc_[:], vtok[TOK:T, NAW + hd * 256:NAW + (hd + 1) * 256].rearrange("(c p) d -> p c d", p=128), bkc, [bvtok], [bkc])
                    AC(gsil[:].rearrange("p a t -> p (a t)"), gg[:].rearrange("p a t -> p (a t)"), AF.Silu, [bgg], [bgs])
                    qcs = [(t0, tn, True) for (t0, tn) in tch] + ([] if last else [(TOK, CTX, False)])
                    for (t0, tn, local) in qcs:
                        pieces = (list(range(NCORE)) if local else []) + [-1]
                        first = True
                        accn = [0, 0, 0, 0]
                        pend = []

                        def flush_one():
                            half_, vaps, e2, f2, l2, rbv = pend.pop(0)
                            for dv in range(2):
                                MM(banks[4 + half_ * 2 + dv][:, 0:tn], vaps[dv], Et[e2][:, 0:tn], f2, l2, [rbv, bEt[e2]], [bbank[4 + half_ * 2 + dv]])

                        for pi_, r in enumerate(pieces):
                            if r >= 0:
                                sl = pit % 2
                                pit += 1
                                S.dma_multi(SP, [(lambda hh, half=half, sl=sl, r=r, hd=hd: hh.dma_start(out=kp[sl][:, half, :],
                                                                                                       in_=kdf.rows(r, (2 * hd + half) * 128, 128)))
                                                 for half in range(2)], bkp[sl], reads=[kdf.buf], writes=[bkp[sl]])
                                S.dma_multi(SP, [(lambda hh, sl=sl, r=r, hd=hd, c0=c0: hh.dma_start(
                                    out=vp[sl][:, c0 // 128:(c0 + vdf.rpc) // 128, :],
                                    in_=vdf.rows(r, c0, vdf.rpc)[:, hd * 256:(hd + 1) * 256].rearrange("(c p) d -> p c d", p=128)))
                                    for c0 in range(0, TOK, vdf.rpc)], bvp[sl], reads=[vdf.buf], writes=[bvp[sl]])
                                nk = TOK // 128
                                kget = lambda half, kc, sl=sl: kp[sl][:, half, kc * 128:(kc + 1) * 128]
                                vget = lambda kc, dv, sl=sl: vp[sl][:, kc, dv * 128:(dv + 1) * 128]
                                rb = [bkp[sl], bvp[sl]]
                            else:
                                nk = CTX // 128
                                kget = lambda half, kc: kc_[:, half, kc * 128:(kc + 1) * 128]
                                vget = lambda kc, dv: vc_[:, kc, dv * 128:(dv + 1) * 128]
                                rb = [bkc, bkc]
                            for kc in range(nk):
                                lastu = (pi_ == len(pieces) - 1 and kc == nk - 1)
                                for half in range(2):
                                    sb = sit % 4
                                    sit += 1
                                    e_ = eit % 6
                                    eit += 1
                                    MM(banks[sb][:, 0:tn], kget(half, kc), qT[:, half, t0:t0 + tn], True, True, [rb[0], bq], [bbank[sb]])
                                    AC(Et[e_][:, 0:tn], banks[sb][:, 0:tn], AF.Exp, [bbank[sb]], [bEt[e_]], scale=sc_att)
                                    a_ = half * 2 + (kc % 2)
                                    if accn[a_] == 0:
                                        S.op(DVE, lambda hh, a_=a_, e_=e_, tn=tn: hh.tensor_copy(accs[a_][:, 0:tn], Et[e_][:, 0:tn]),
                                             reads=[bEt[e_]], writes=[baccs[a_]])
                                    else:
                                        TT(accs[a_][:, 0:tn], accs[a_][:, 0:tn], Et[e_][:, 0:tn], ALU.add, [bEt[e_], baccs[a_]], [baccs[a_]])
                                    accn[a_] += 1
                                    pend.append((half, [vget(kc, 0), vget(kc, 1)], e_, first, lastu, rb[1]))
                                    if len(pend) > 2:
                                        flush_one()
                                first = False
                        while pend:
                            flush_one()
                        for half in range(2):
                            used = [a_ for a_ in (half * 2, half * 2 + 1) if accn[a_] > 0]
                            for i_, a_ in enumerate(used):
                                MM(banks[half][:, 0:tn], ones32[:], accs[a_][:, 0:tn], i_ == 0, i_ == len(used) - 1, [bconst, baccs[a_]],
                                   [bbank[half]])
                            RCP(fr[:, half, 0:tn], banks[half][:, 0:tn], [bbank[half]], [bfr])
                        for dv in range(2):
                            TT(fu[:, 0:tn], banks[4 + dv][:, 0:tn], fr[:, 0, 0:tn], ALU.mult, [bbank[4 + dv], bfr], [bfu])
                            TT(ft_[:, 0:tn], banks[6 + dv][:, 0:tn], fr[:, 1, 0:tn], ALU.mult, [bbank[6 + dv], bfr], [bft])
                            STT(fo[:, dv, 0:tn], ft_[:, 0:tn], lsc[:, 2:3], fu[:, 0:tn], ALU.mult, ALU.add, [bft, bfu, bcn], [bfo])
                            AC(fsq[:, 0:tn], fo[:, dv, 0:tn], AF.Square, [bfo], [bfsq])
                            MM(banks[2][:, 0:tn], ones32[:], fsq[:, 0:tn], dv == 0, dv == 1, [bconst, bfsq], [bbank[2]])
                        AC(frn[:, 0:tn], banks[2][:, 0:tn], AF.Sqrt, [bbank[2], bconst], [bfrn], scale=1.0 / 256.0, bias=eps6[:, 1:2])
                        RCP(frn[:, 0:tn], frn[:, 0:tn], [bfrn], [bfrn])
                        for dv in range(2):
                            g_ = git % 2
                            git += 1
                            TT(fo[:, dv, 0:tn], fo[:, dv, 0:tn], frn[:, 0:tn], ALU.mult, [bfo, bfrn], [bfo])
                            STT(fg[g_][:, 0:tn], fo[:, dv, 0:tn], gcol[:, dv:dv + 1], gsil[:, dv, t0:t0 + tn], ALU.mult, ALU.mult,
                                [bfo, bcn, bgs], [bfg[g_]])
                            LD(gT[NAW + hd * 256 + dv * 128:NAW + hd * 256 + (dv + 1) * 128, t0:t0 + tn], fg[g_][:, 0:tn], bfg[g_], [bfg[g_]], [bgT],
                               q=SP)
                S.emit_phase()
            if stop < 6:
                return nc

            with ExitStack() as es:
                sel = es.enter_context(nc.sbuf_tensor("selc_%d" % l, [128, 16], F32))
                bsel = S.buf("selc")
                LD(sel[:], sel_in, bsel, [], [bsel])
                cw = es.enter_context(nc.sbuf_tensor("cw_%d" % l, [128, CT, 31], F32))
                cpar = es.enter_context(nc.sbuf_tensor("cpar_%d" % l, [128, 3, CT], F32))
                for ct in range(CT):
                    LD(cw[:, ct, :], convw_in[l][:, ct * 128:(ct + 1) * 128].rearrange("k p -> p k"), bsel, [], [bsel], slow=True)
                for i_, src in enumerate((convb_in, convg_in, convbt_in)):
                    LD(cpar[:, i_, :], src[l].rearrange("(c p) -> p c", p=128), bsel, [], [bsel], slow=True)
                hc = es.enter_context(nc.sbuf_tensor("hc_%d" % l, [128, 8, 32], BF16))
                bhc = S.buf("hc")
                seqs = [(0, TOK, True)] + ([] if last else [(TOK, CTX, False)])
                ub = [es.enter_context(nc.sbuf_tensor("ub%d_%d" % (i, l), [128, 30 + TOK], BF16)) for i in range(2)]
                bub = [S.buf("ub%d" % i) for i in range(2)]
                yT = es.enter_context(nc.sbuf_tensor("yT_%d" % l, [128, CT, TOK], F32))
                byT = S.buf("yT")
                sq = es.enter_context(nc.sbuf_tensor("csq_%d" % l, [128, 512], F32))
                mean = es.enter_context(nc.sbuf_tensor("cmean_%d" % l, [128, 512], F32))
                msq = es.enter_context(nc.sbuf_tensor("cmsq_%d" % l, [128, 512], F32))
                rstd = es.enter_context(nc.sbuf_tensor("crstd_%d" % l, [128, 512], F32))
                tn_ = es.enter_context(nc.sbuf_tensor("ctn_%d" % l, [128, 512], F32))
                gt = [es.enter_context(nc.sbuf_tensor("cgt%d_%d" % (i, l), [128, 512], BF16)) for i in range(2)]
                gs_ = es.enter_context(nc.sbuf_tensor("cgs_%d" % l, [128, 512], F32))
                og = [es.enter_context(nc.sbuf_tensor("cog%d_%d" % (i, l), [128, 512], BF16)) for i in range(2)]
                bsq, bmean, bmsq, brstd, btn, bgs_ = S.buf("csq"), S.buf("cmean"), S.buf("cmsq"), S.buf("crstd"), S.buf("ctn"), S.buf("cgs")
                bgt = [S.buf("cgt%d" % i) for i in range(2)]
                bog = [S.buf("cog%d" % i) for i in range(2)]
                uit = 0
                oit = 0
                for (c0, nt, halo) in seqs:
                    for ct in range(CT):
                        sl = uit % 2
                        uit += 1
                        LD(ub[sl][:, 15:15 + nt], uT[ct * 128:(ct + 1) * 128, c0:c0 + nt], bub[sl], [buT], [bub[sl]])
                        if halo:
                            S.dma_multi(SP, [(lambda hh, r=r, ct=ct: hh.dma_start(out=hc[:, r, :], in_=cvh.rows(r, ct * 128, 128))) for r in range(NCORE)],
                                        bhc, reads=[cvh.buf], writes=[bhc])
                            for r in range(NCORE):
                                ha, hb = ub[sl][:, 0:15], ub[sl][:, 15 + nt:30 + nt]
                                if r == 0:
                                    TS(ha, hc[:, r, 16:31], sel[:, r:r + 1], None, ALU.mult, None, [bhc, bsel], [bub[sl]])
                                    TS(hb, hc[:, r, 0:15], sel[:, 8 + r:9 + r], None, ALU.mult, None, [bhc, bsel], [bub[sl]])
                                else:
                                    STT(ha, hc[:, r, 16:31], sel[:, r:r + 1], ha, ALU.mult, ALU.add, [bhc, bsel], [bub[sl]])
                                    STT(hb, hc[:, r, 0:15], sel[:, 8 + r:9 + r], hb, ALU.mult, ALU.add, [bhc, bsel], [bub[sl]])
                        else:
                            MS(ub[sl][:, 0:15], 0.0, [bub[sl]])
                            MS(ub[sl][:, 15 + nt:30 + nt], 0.0, [bub[sl]])
                        TS(yT[:, ct, 0:nt], ub[sl][:, 0:nt], cw[:, ct, 0:1], cpar[:, 0, ct:ct + 1], ALU.mult, ALU.add, [bub[sl], bsel], [byT])
                        for k in range(1, 31):
                            STT(yT[:, ct, 0:nt], ub[sl][:, k:k + nt], cw[:, ct, k:k + 1], yT[:, ct, 0:nt], ALU.mult, ALU.add, [bub[sl], bsel], [byT])
                    for t0 in range(0, nt, 512):
                        tn = min(512, nt - t0)
                        for ct in range(CT):
                            MM(banks[0][:, 0:tn], ones32[:], yT[:, ct, t0:t0 + tn], ct == 0, ct == CT - 1, [bconst, byT], [bbank[0]])
                        for ct in range(CT):
                            AC(sq[:, 0:tn], yT[:, ct, t0:t0 + tn], AF.Square, [byT], [bsq])
                            MM(banks[1][:, 0:tn], ones32[:], sq[:, 0:tn], ct == 0, ct == CT - 1, [bconst, bsq], [bbank[1]])
                        AC(mean[:, 0:tn], banks[0][:, 0:tn], AF.Identity, [bbank[0]], [bmean], scale=1.0 / CVW)
                        TT(msq[:, 0:tn], mean[:, 0:tn], mean[:, 0:tn], ALU.mult, [bmean], [bmsq])
                        STT(rstd[:, 0:tn], banks[1][:, 0:tn], 1.0 / CVW, msq[:, 0:tn], ALU.mult, ALU.subtract, [bbank[1], bmsq], [brstd])
                        AC(rstd[:, 0:tn], rstd[:, 0:tn], AF.Sqrt, [brstd, bconst], [brstd], bias=eps6[:, 1:2])
                        RCP(rstd[:, 0:tn], rstd[:, 0:tn], [brstd], [brstd])
                        for ct in range(CT):
                            o_ = oit % 2
                            oit += 1
                            LD(gt[o_][:, 0:tn], zT[o_cvgate + ct * 128:o_cvgate + (ct + 1) * 128, c0 + t0:c0 + t0 + tn], bgt[o_], [bzT], [bgt[o_]])
                            AC(gs_[:, 0:tn], gt[o_][:, 0:tn], AF.Silu, [bgt[o_]], [bgs_])
                            TT(tn_[:, 0:tn], yT[:, ct, t0:t0 + tn], mean[:, 0:tn], ALU.subtract, [byT, bmean], [btn])
                            TT(tn_[:, 0:tn], tn_[:, 0:tn], rstd[:, 0:tn], ALU.mult, [btn, brstd], [btn])
                            AC(tn_[:, 0:tn], tn_[:, 0:tn], AF.Silu, [btn, bsel], [btn], scale=cpar[:, 1, ct:ct + 1], bias=cpar[:, 2, ct:ct + 1])
                            TT(og[o_][:, 0:tn], tn_[:, 0:tn], gs_[:, 0:tn], ALU.mult, [btn, bgs_], [bog[o_]])
                            LD(gT[NAW + DFW + ct * 128:NAW + DFW + (ct + 1) * 128, c0 + t0:c0 + t0 + tn], og[o_][:, 0:tn], bog[o_], [bog[o_]], [bgT], q=POOL)
                S.emit_phase()
            if stop < 7:
                return nc

            alpha = (2.0 * L) ** 0.25
            WK = NAW // 128
            TC = 256
            with ExitStack() as es:
                gch = es.enter_context(nc.sbuf_tensor("gch_%d" % l, [128, 3 * WK, TC], BF16))
                bgch = S.buf("gch")
                wps = es.enter_context(nc.sbuf_tensor("wps_%d" % l, [128, 3, WK, 256], BF16))
                bwps = S.buf("wps")
                mgt = [es.enter_context(nc.sbuf_tensor("mgt%d_%d" % (i, l), [128, 3, TC], BF16)) for i in range(2)]
                bmgt = [S.buf("mgt%d" % i) for i in range(2)]
                sg = es.enter_context(nc.sbuf_tensor("msg_%d" % l, [128, 3, TC], F32))
                bsg = S.buf("msg")
                ya = es.enter_context(nc.sbuf_tensor("mya_%d" % l, [128, TC], F32))
                yb = es.enter_context(nc.sbuf_tensor("myb_%d" % l, [128, TC], F32))
                bya, byb = S.buf("mya"), S.buf("myb")
                yT_ = es.enter_context(nc.sbuf_tensor("myT_%d" % l, [128, KC, TC], BF16))
                byT_ = S.buf("myT")
                wo = [es.enter_context(nc.sbuf_tensor("wo%d_%d" % (i, l), [128, KC, 256], BF16)) for i in range(2)]
                bwo = [S.buf("wo%d" % i) for i in range(2)]
                osb = es.enter_context(nc.sbuf_tensor("osb_%d" % l, [128, D], F32))
                bosb = S.buf("osb")
                xt_ = es.enter_context(nc.sbuf_tensor("pxt_%d" % l, [128, D], F32))
                bxt_ = S.buf("pxt")
                gbc = es.enter_context(nc.sbuf_tensor("gbc_%d" % l, [128, D], F32))
                pgb = es.enter_context(nc.sbuf_tensor("pgb_%d" % l, [128, 2, D], F32))
                bgbc, bpgb = S.buf("gbc"), S.buf("pgb")
                st2 = es.enter_context(nc.sbuf_tensor("st2_%d" % l, [128, 8, 6], F32))
                mv2 = es.enter_context(nc.sbuf_tensor("mv2_%d" % l, [128, 2], F32))
                bst2 = S.buf("st2")
                LD(pgb[:, 0, :], bass.AP(plg_in.tensor, l * D, [[0, 128], [1, D]]), bpgb, [], [bpgb])
                LD(pgb[:, 1, :], bass.AP(plb_in.tensor, l * D, [[0, 128], [1, D]]), bpgb, [], [bpgb])
                cur_v = -1
                woit = 0
                mit = 0
                gws = [gw_pna[l], gw_pdf[l], gw_pcv[l]]
                for t0 in range(0, TE, TC):
                    v = 0 if t0 < TOK else 1
                    if v != cur_v:
                        cur_v = v
                        for (ap, o, m) in mod_cols(v, l, 2 * D, D):
                            LD(gbc[:, o:o + m], bass.AP(ap.tensor, ap.offset, [[0, 128], [1, m]]), bgbc, [bmod], [bgbc])
                    for b_ in range(3):
                        LD(gch[:, b_ * WK:(b_ + 1) * WK, :], gT[b_ * NAW:(b_ + 1) * NAW, t0:t0 + TC].rearrange("(k p) t -> p k t", p=128),
                           bgch, [bgT], [bgch])
                    for fst in range(D // 256):
                        for b_ in range(3):
                            S.dma_multi(SP, [(lambda hh, b_=b_, r=r, fst=fst: hh.dma_start(out=wps[:, b_, r * gws[b_].kcl:(r + 1) * gws[b_].kcl, :],
                                                                                          in_=gws[b_].st_ap(fst)[:, r])) for r in range(NCORE)],
                                        bwps, reads=[gws[b_].buf], writes=[bwps])
                        for half in range(2):
                            ftile = fst * 2 + half
                            m_ = mit % 2
                            mit += 1
                            for b_ in range(3):
                                LD(mgt[m_][:, b_, :], zT[o_mg[b_] + ftile * 128:o_mg[b_] + (ftile + 1) * 128, t0:t0 + TC], bmgt[m_], [bzT], [bmgt[m_]])
                            AC(sg[:].rearrange("p a t -> p (a t)"), mgt[m_][:].rearrange("p a t -> p (a t)"), AF.Sigmoid, [bmgt[m_]], [bsg])
                            for b_ in range(3):
                                for k in range(WK):
                                    MM(banks[b_][:, 0:TC], wps[:, b_, k, half * 128:(half + 1) * 128], gch[:, b_ * WK + k, :], k == 0, k == WK - 1,
                                       [bwps, bgch], [bbank[b_]])
                            TT(ya[:], banks[0][:, 0:TC], sg[:, 0, :], ALU.mult, [bbank[0], bsg], [bya])
                            TT(yb[:], banks[1][:, 0:TC], sg[:, 1, :], ALU.mult, [bbank[1], bsg], [byb])
                            TT(ya[:], ya[:], yb[:], ALU.add, [bya, byb], [bya])
                            TT(yb[:], banks[2][:, 0:TC], sg[:, 2, :], ALU.mult, [bbank[2], bsg], [byb])
                            TT(yT_[:, ftile, :], ya[:], yb[:], ALU.add, [bya, byb], [byT_])
                    for tt in range(TC // 128):
                        tok0 = t0 + tt * 128
                        LD(xt_[:], xcur[tok0:tok0 + 128, :], bxt_, [bxcur], [bxt_])
                        for fst in range(D // 256):
                            w_ = woit % 2
                            woit += 1
                            bk = 3 + (woit % 4)
                            S.dma_multi(SP, [(lambda hh, w_=w_, r=r, fst=fst: hh.dma_start(out=wo[w_][:, r * gw_out[l].kcl:(r + 1) * gw_out[l].kcl, :],
                                                                                          in_=gw_out[l].st_ap(fst)[:, r])) for r in range(NCORE)],
                                        bwo[w_], reads=[gw_out[l].buf], writes=[bwo[w_]])
                            for k in range(KC):
                                MM(banks[bk][:, 0:256], yT_[:, k, tt * 128:(tt + 1) * 128], wo[w_][:, k, :], k == 0, k == KC - 1, [byT_, bwo[w_]], [bbank[bk]])
                            TT(osb[:, fst * 256:(fst + 1) * 256], banks[bk][:, 0:256], gbc[:, fst * 256:(fst + 1) * 256], ALU.mult, [bbank[bk], bgbc], [bosb])
                        STT(osb[:], xt_[:], alpha, osb[:], ALU.mult, ALU.add, [bxt_, bosb], [bosb])
                        nchs = (D + 511) // 512
                        for c in range(nchs):
                            S.op(DVE, lambda hh, c=c: hh.bn_stats(st2[:, c, :], osb[:, c * 512:(c + 1) * 512]), reads=[bosb], writes=[bst2])
                        S.op(DVE, lambda hh: hh.bn_aggr(mv2[:], st2[:, 0:nchs, :].rearrange("p c s -> p (c s)")), reads=[bst2], writes=[bst2])
                        AC(mv2[:, 1:2], mv2[:, 1:2], AF.Sqrt, [bst2, bconst], [bst2], bias=eps6[:, 0:1])
                        RCP(mv2[:, 1:2], mv2[:, 1:2], [bst2], [bst2])
                        TS(osb[:], osb[:], mv2[:, 0:1], mv2[:, 1:2], ALU.subtract, ALU.mult, [bosb, bst2], [bosb])
                        TT(osb[:], osb[:], pgb[:, 0, :], ALU.mult, [bosb, bpgb], [bosb])
                        TT(xt_[:], osb[:], pgb[:, 1, :], ALU.add, [bosb, bpgb], [bxt_])
                        LD(xcur[tok0:tok0 + 128, :], xt_[:], bxt_, [bxt_], [bxcur], q=POOL)
                S.emit_phase()

        if stop < 99:
            return nc
        for r0 in range(0, TOK, 128):
            S.dma(SP, lambda h, r0=r0: h.dma_start(out=out[r0:r0 + 128, :], in_=xcur[r0:r0 + 128, :]), castbuf, reads=[bxcur])
        S.emit_phase()
    return nc


NEG = -30000.0


def make_maps(cfg, x, c, ctx, c_ctx, w_ada, b_ada, w_in, b_in, na_rpb, diff_lq1, diff_lk1, diff_lq2, diff_lk2, diff_subln_g,
              conv_w, conv_b, conv_ln_g, conv_ln_b, w_proj_na, w_proj_diff, w_proj_conv, w_out, post_ln_g, post_ln_b):
    D, L, TOK, NAW, DFW, CVW = cfg.D, cfg.L, cfg.TOK, cfg.NAW, cfg.DFW, cfg.CVW
    f32 = np.float32
    NA3 = 3 * D // NCORE
    x2 = np.asarray(x, f32).reshape(cfg.SEQ, D)
    ctx2 = np.ascontiguousarray(np.asarray(ctx, f32).reshape(cfg.CTX, D))
    c2 = np.ascontiguousarray(np.stack([np.asarray(c, f32).reshape(D), np.asarray(c_ctx, f32).reshape(D)]))
    ident = np.eye(128, dtype=ml_dtypes.bfloat16)
    perm = np.zeros((128, 128), f32)
    perm[np.arange(128) ^ 1, np.arange(128)] = 1.0
    perm = perm.astype(ml_dtypes.bfloat16)
    ROWS = TOK // 64
    NP = ROWS // 2
    GROWS = cfg.SEQ // 64
    NH = NAW // 128
    b2 = np.arange(2)[:, None, None, None, None]
    kc = np.arange(64)[None, :, None, None, None]
    cc = np.arange(8)[None, None, :, None, None]
    aa = np.arange(2)[None, None, None, :, None]
    qc = np.arange(64)[None, None, None, None, :]
    dr = 2 * cc + b2 - aa - 1
    dc = kc - qc + 15
    cs = np.clip(qc - 8, 0, 48)
    ok = (dr >= 0) & (dr <= 14) & (kc >= cs) & (kc < cs + 16) & (dc >= 0) & (dc <= 30)
    ok = np.broadcast_to(ok, (2, 64, 8, 2, 64))
    dri = np.broadcast_to(np.clip(dr, 0, 14), ok.shape)
    dci = np.broadcast_to(np.clip(dc, 0, 30), ok.shape)
    rpb = np.asarray(na_rpb, f32)
    rpbT = np.where(ok[None, None], rpb[:, :, dri, dci], f32(NEG)).astype(f32).reshape(L, NH, 128, 8, 128)
    lamv = np.ascontiguousarray(np.stack([diff_lq1, diff_lk1, diff_lq2, diff_lk2], 1).astype(f32).reshape(-1))
    inv_freq = (10000.0 ** (-np.arange(32, dtype=f32) / f32(32))).astype(f32)
    maps = []
    for r in range(NCORE):
        t = np.arange(r * TOK, (r + 1) * TOK)
        row = (t // 64).astype(f32)
        col = (t % 64).astype(f32)
        ang = np.concatenate([row[:, None] * inv_freq, col[:, None] * inv_freq], -1).astype(f32)
        cosT = np.ascontiguousarray(np.repeat(np.cos(ang).astype(f32), 2, axis=1).T)
        sgn = np.where(np.arange(128) % 2 == 0, -1.0, 1.0).astype(f32)
        sinT = np.ascontiguousarray((np.repeat(np.sin(ang).astype(f32), 2, axis=1) * sgn).T)
        sel = np.zeros((128, 16), f32)
        if r > 0:
            sel[:, r - 1] = 1.0
        if r < NCORE - 1:
            sel[:, 8 + r + 1] = 1.0
        rowmask = np.zeros((5, 2, 64, 8, 2, 64), f32)
        for si, p in enumerate((0, 1, 2, NP - 2, NP - 1)):
            q_abs = r * ROWS + 2 * p + aa
            k_abs = r * ROWS + 2 * p + 2 * cc + b2 - 8
            r0 = np.clip(q_abs - 4, 0, GROWS - 8)
            valid = (k_abs >= r0) & (k_abs < r0 + 8)
            rowmask[si] = np.where(np.broadcast_to(valid, (2, 64, 8, 2, 64)), 0.0, NEG)
        rowmask = rowmask.reshape(5, 128, 8, 128)
        maps.append(dict(
            x=np.ascontiguousarray(x2[r * TOK:(r + 1) * TOK]), ctx=ctx2, c2=c2,
            w_ada=np.ascontiguousarray(w_ada[:, :, r * NA3:(r + 1) * NA3]),
            b_ada=np.ascontiguousarray(b_ada[:, r * NA3:(r + 1) * NA3]),
            w_in=np.ascontiguousarray(w_in[:, r * D // NCORE:(r + 1) * D // NCORE, :]),
            b_in=np.ascontiguousarray(b_in), ident=ident, perm=perm, ropecos=cosT, ropesin=sinT, sel=sel, rowmask=rowmask,
            rpbT=rpbT, lamv=lamv, subg=np.ascontiguousarray(diff_subln_g, f32), conv_w=np.ascontiguousarray(conv_w, f32),
            conv_b=np.ascontiguousarray(conv_b, f32), conv_g=np.ascontiguousarray(conv_ln_g, f32), conv_bt=np.ascontiguousarray(conv_ln_b, f32),
            post_g=np.ascontiguousarray(post_ln_g, f32).reshape(-1), post_b=np.ascontiguousarray(post_ln_b, f32).reshape(-1),
            w_pna=np.ascontiguousarray(w_proj_na[:, r * NAW // NCORE:(r + 1) * NAW // NCORE, :]),
            w_pdf=np.ascontiguousarray(w_proj_diff[:, r * DFW // NCORE:(r + 1) * DFW // NCORE, :]),
            w_pcv=np.ascontiguousarray(w_proj_conv[:, r * CVW // NCORE:(r + 1) * CVW // NCORE, :]),
            w_out=np.ascontiguousarray(w_out[:, r * D // NCORE:(r + 1) * D // NCORE, :])))
    return maps


def kernel(**inputs):
    cfg = Cfg()
    nc = build(cfg)
    maps = make_maps(cfg, **{k: np.asarray(v) for k, v in inputs.items()})
    res = run_bass_kernel_spmd(nc, maps, core_ids=list(range(NCORE)))
    out = np.concatenate([res.results[r]["out"] for r in range(NCORE)], axis=0)
    return out.reshape(1, cfg.SEQ, cfg.D).astype(np.float32)
```
